# Optimizing a Trainium2 kernel written in Bass

```python
import math
import jax, jax.numpy as jnp
from jax import lax
import numpy as np

D_MODEL = 1024
BATCH = 4
SEQ = 8192
DEPTH = 2
DEC_BATCH = 128
DEC_SEQ = 8
PAST_LEN = 16384
PAGE_SIZE = 128

N_META = 16
EPS = 1e-6
POOL_GROUPS = 4
POOL_GROUP_DIM = D_MODEL // 8
POOL_WIDTH = POOL_GROUPS * POOL_GROUP_DIM
POOL_WINDOWS = (2, 4, 8, 16)
POOL_BUF = max(POOL_WINDOWS) - 1
DN_HEADS = 4
DN_DK = 128
DN_DV = 128
DN_QK = DN_HEADS * DN_DK
DN_VW = DN_HEADS * DN_DV
CONV_W = 4
CONV_CH = 2 * DN_QK + DN_VW
DN_CHUNK = 64
SWA_HEADS = 8
SWA_KV_HEADS = 2
SWA_GROUP = SWA_HEADS // SWA_KV_HEADS
SWA_HD = 64
SWA_WIDTH = SWA_HEADS * SWA_HD
SWA_KV_WIDTH = SWA_KV_HEADS * SWA_HD
SWA_WINDOW = 128
SWA_BLOCK = 128
ROPE_THETA = 10000.0
D_FF = 4 * D_MODEL
IN_SPLITS = (POOL_WIDTH, DN_QK, DN_QK, DN_VW, DN_VW, DN_HEADS, DN_HEADS,
             SWA_WIDTH, SWA_KV_WIDTH, SWA_KV_WIDTH, D_MODEL, D_MODEL, D_MODEL)
D_IN = sum(IN_SPLITS)

kernel_name = 'hybrid_pool_delta_swa_decode_step'


def rmsnorm(x, w):
    xf = x.astype(jnp.float32)
    y = xf * lax.rsqrt(jnp.mean(xf * xf, axis=-1, keepdims=True) + EPS)
    return (y * w.astype(jnp.float32)).astype(x.dtype)


def l2norm(x):
    xf = x.astype(jnp.float32)
    return (xf * lax.rsqrt(jnp.sum(xf * xf, axis=-1, keepdims=True) + EPS)).astype(x.dtype)


def rope(x, pos):
    half = x.shape[-1] // 2
    inv = ROPE_THETA ** (-jnp.arange(half, dtype=jnp.float32) / half)
    ang = pos.astype(jnp.float32)[:, None] * inv[None, :]
    cos = jnp.cos(ang)[None, :, None, :]
    sin = jnp.sin(ang)[None, :, None, :]
    xf = x.astype(jnp.float32)
    x1, x2 = xf[..., :half], xf[..., half:]
    return jnp.concatenate([x1 * cos - x2 * sin, x2 * cos + x1 * sin], axis=-1).astype(x.dtype)


def pool_mix(u, buf, pos0, w_pool, s_pool):
    B, T, _ = u.shape
    ext = jnp.concatenate([buf.astype(u.dtype), u], axis=1).astype(jnp.float32)
    c = jnp.cumsum(jnp.pad(ext, ((0, 0), (1, 0), (0, 0))), axis=1)
    end = c[:, POOL_BUF + 1:]
    pos = pos0 + jnp.arange(T)
    means = []
    for g, w in enumerate(POOL_WINDOWS):
        sl = slice(g * POOL_GROUP_DIM, (g + 1) * POOL_GROUP_DIM)
        start = c[:, POOL_BUF + 1 - w:POOL_BUF + 1 - w + T, sl]
        cnt = jnp.minimum(w, pos + 1).astype(jnp.float32)[None, :, None]
        means.append((end[..., sl] - start) / cnt)
    d = (jnp.concatenate(means, axis=-1) - ext[:, POOL_BUF:]).astype(u.dtype)
    d = d.reshape(B, T, POOL_GROUPS, POOL_GROUP_DIM)
    y = jnp.einsum('btgc,gcd->btgd', d, w_pool).reshape(B, T, POOL_WIDTH)
    return y * s_pool, ext[:, -POOL_BUF:].astype(u.dtype)


def short_conv(u, buf, w_conv):
    T = u.shape[1]
    ext = jnp.concatenate([buf.astype(u.dtype), u], axis=1)
    y = ext[:, 0:T] * w_conv[0]
    for j in range(1, CONV_W):
        y = y + ext[:, j:j + T] * w_conv[j]
    return jax.nn.silu(y), ext[:, T:]


def gated_delta(q, k, v, g, beta, S0, chunk):
    B, T, H, _ = q.shape
    DV = v.shape[-1]
    N = T // chunk
    f32 = jnp.float32

    def blk(x):
        return x.astype(f32).reshape(B, N, chunk, H, -1).transpose(1, 0, 3, 2, 4)

    qc, kc, vc = blk(q), blk(k), blk(v)
    gc = g.astype(f32).reshape(B, N, chunk, H).transpose(1, 0, 3, 2)
    bc = beta.astype(f32).reshape(B, N, chunk, H).transpose(1, 0, 3, 2)
    G = jnp.cumsum(gc, axis=-1)
    tri = jnp.tril(jnp.ones((chunk, chunk), bool))
    strict = jnp.tril(jnp.ones((chunk, chunk), bool), -1)
    decay = jnp.exp(jnp.where(tri, G[..., :, None] - G[..., None, :], -jnp.inf))
    kk = jnp.einsum('nbhid,nbhjd->nbhij', kc, kc)
    M = jnp.where(strict, kk * decay * bc[..., :, None], 0.0)
    eye = jnp.eye(chunk, dtype=f32)
    Tinv = lax.linalg.triangular_solve(eye + M, jnp.broadcast_to(eye, M.shape),
                                       left_side=True, lower=True, unit_diagonal=True)
    u = Tinv @ (vc * bc[..., None])
    w = Tinv @ (kc * (bc * jnp.exp(G))[..., None])
    qk = jnp.einsum('nbhid,nbhjd->nbhij', qc, kc) * decay
    qg = qc * jnp.exp(G)[..., None]
    kg = kc * jnp.exp(G[..., -1:] - G)[..., None]
    gl = jnp.exp(G[..., -1])

    def step(S, xs):
        qk_i, qg_i, kg_i, u_i, w_i, gl_i = xs
        dlt = u_i - w_i @ S
        o = qg_i @ S + qk_i @ dlt
        S = S * gl_i[..., None, None] + jnp.swapaxes(kg_i, -1, -2) @ dlt
        return S, o

    S, o = lax.scan(step, S0.astype(f32), (qk, qg, kg, u, w, gl))
    o = o.transpose(1, 0, 3, 2, 4).reshape(B, T, H, DV)
    return o.astype(v.dtype), S


def deltanet(q, k, v, z, b, a, conv_buf, S0, conv_w, a_log, dt_bias, onorm_w, segments):
    B, T, _ = q.shape
    qkv, conv_new = short_conv(jnp.concatenate([q, k, v], axis=-1), conv_buf, conv_w)
    q, k, v = jnp.split(qkv, [DN_QK, 2 * DN_QK], axis=-1)
    q = l2norm(q.reshape(B, T, DN_HEADS, DN_DK)) * (DN_DK ** -0.5)
    k = l2norm(k.reshape(B, T, DN_HEADS, DN_DK))
    v = v.reshape(B, T, DN_HEADS, DN_DV)
    beta = jax.nn.sigmoid(b.astype(jnp.float32))
    g = -jnp.exp(a_log.astype(jnp.float32)) * jax.nn.softplus(a.astype(jnp.float32) + dt_bias.astype(jnp.float32))
    S = S0
    outs = []
    start = 0
    for length, chunk in segments:
        sl = slice(start, start + length)
        o, S = gated_delta(q[:, sl], k[:, sl], v[:, sl], g[:, sl], beta[:, sl], S, chunk)
        outs.append(o)
        start += length
    o = jnp.concatenate(outs, axis=1)
    o = rmsnorm(o, onorm_w) * jax.nn.silu(z.reshape(B, T, DN_HEADS, DN_DV))
    return o.reshape(B, T, DN_VW), conv_new, S.astype(S0.dtype)


def sink_probs(s, mask, sink):
    s = jnp.where(mask, s, -jnp.inf)
    m = jnp.maximum(jnp.max(s, axis=-1, keepdims=True), sink)
    e = jnp.exp(s - m)
    return e / (jnp.sum(e, axis=-1, keepdims=True) + jnp.exp(sink - m))


def swa_prompt(q, k, v, sinks):
    B, L = q.shape[:2]
    P = (-L) % SWA_BLOCK
    nb = (L + P) // SWA_BLOCK
    f32 = jnp.float32
    qb = jnp.pad(q, ((0, 0), (P, 0), (0, 0), (0, 0))).reshape(B, nb, SWA_BLOCK, SWA_KV_HEADS, SWA_GROUP, SWA_HD)
    kp = jnp.pad(k, ((0, 0), (P + SWA_BLOCK, 0), (0, 0), (0, 0))).reshape(B, nb + 1, SWA_BLOCK, SWA_KV_HEADS, SWA_HD)
    vp = jnp.pad(v, ((0, 0), (P + SWA_BLOCK, 0), (0, 0), (0, 0))).reshape(B, nb + 1, SWA_BLOCK, SWA_KV_HEADS, SWA_HD)
    kw = jnp.concatenate([kp[:, :-1], kp[:, 1:]], axis=2)
    vw = jnp.concatenate([vp[:, :-1], vp[:, 1:]], axis=2)
    s = jnp.einsum('bnqkgd,bnjkd->bnkgqj', qb.astype(f32), kw.astype(f32)) * (SWA_HD ** -0.5)
    r = jnp.arange(SWA_BLOCK)[:, None]
    j = jnp.arange(2 * SWA_BLOCK)[None, :]
    diff = r - j + SWA_BLOCK
    kpos = jnp.arange(nb)[:, None, None] * SWA_BLOCK + j[None] - SWA_BLOCK - P
    mask = (diff >= 0) & (diff < SWA_WINDOW) & (kpos >= 0)
    sink = sinks.astype(f32).reshape(SWA_KV_HEADS, SWA_GROUP)[None, None, :, :, None, None]
    prob = sink_probs(s, mask[None, :, None, None], sink)
    o = jnp.einsum('bnkgqj,bnjkd->bnqkgd', prob, vw.astype(f32)).reshape(B, nb * SWA_BLOCK, SWA_WIDTH)
    return o[:, P:].astype(q.dtype)


def swa_sample(q, k, v, k_buf, v_buf, sinks):
    B, T = q.shape[:2]
    W = k_buf.shape[1]
    f32 = jnp.float32
    ke = jnp.concatenate([k_buf.astype(k.dtype), k], axis=1)
    ve = jnp.concatenate([v_buf.astype(v.dtype), v], axis=1)
    qg = q.reshape(B, T, SWA_KV_HEADS, SWA_GROUP, SWA_HD)
    s = jnp.einsum('btkgd,bjkd->bkgtj', qg.astype(f32), ke.astype(f32)) * (SWA_HD ** -0.5)
    diff = (jnp.arange(T)[:, None] + W) - jnp.arange(W + T)[None, :]
    mask = (diff >= 0) & (diff < SWA_WINDOW)
    sink = sinks.astype(f32).reshape(SWA_KV_HEADS, SWA_GROUP)[None, :, :, None, None]
    prob = sink_probs(s, mask, sink)
    o = jnp.einsum('bkgtj,bjkd->btkgd', prob, ve.astype(f32)).reshape(B, T, SWA_WIDTH)
    return o.astype(q.dtype), ke[:, -W:], ve[:, -W:]


def mixer(xn, p, pos0, pool_buf, conv_buf, S0, k_buf, v_buf, is_prompt):
    B, T, _ = xn.shape
    h = xn @ p['w_in']
    (u_a, q_b, k_b, v_b, z_b, b_b, a_b, q_c, k_c, v_c, g_a, g_b, g_c) = jnp.split(
        h, np.cumsum(IN_SPLITS)[:-1].tolist(), axis=-1)
    o_a, pool_new = pool_mix(u_a, pool_buf, pos0, p['pool_w'], p['pool_scale'])
    segments = ((N_META, N_META), (T - N_META, DN_CHUNK)) if is_prompt else ((T, T),)
    o_b, conv_new, S_new = deltanet(q_b, k_b, v_b, z_b, b_b, a_b, conv_buf, S0, p['dn_conv_w'],
                                    p['dn_a_log'], p['dn_dt_bias'], p['dn_onorm_w'], segments)
    pos = pos0 + jnp.arange(T)
    qh = rope(q_c.reshape(B, T, SWA_HEADS, SWA_HD), pos)
    kh = rope(k_c.reshape(B, T, SWA_KV_HEADS, SWA_HD), pos)
    vh = v_c.reshape(B, T, SWA_KV_HEADS, SWA_HD)
    if is_prompt:
        o_c = swa_prompt(qh, kh, vh, p['swa_sinks'])
        k_new, v_new = kh[:, -SWA_WINDOW:], vh[:, -SWA_WINDOW:]
    else:
        o_c, k_new, v_new = swa_sample(qh, kh, vh, k_buf, v_buf, p['swa_sinks'])
    m = (jax.nn.sigmoid(g_a) * (o_a @ p['proj_a'])
         + jax.nn.sigmoid(g_b) * (o_b @ p['proj_b'])
         + jax.nn.sigmoid(g_c) * (o_c @ p['proj_c']))
    return m @ p['w_out'], (pool_new, conv_new, S_new, k_new, v_new)


def block(x, p, pos0, pool_buf, conv_buf, S0, k_buf, v_buf, is_prompt):
    mix, new = mixer(rmsnorm(x, p['norm1_w']), p, pos0, pool_buf, conv_buf, S0, k_buf, v_buf, is_prompt)
    h = x + mix
    f = jnp.square(jax.nn.relu(rmsnorm(h, p['norm2_w']) @ p['w_up'])) @ p['w_down']
    return h + f, new


def setup_inputs(seed: int = 0) -> dict:
    key = jax.random.key(seed)
    ks = list(jax.random.split(key, 32))

    def nrm(i, shape, scale):
        return jax.random.normal(ks[i], shape, jnp.float32) * scale

    swa_buf = min(SWA_WINDOW, PAST_LEN)
    dt = jnp.exp(jax.random.uniform(ks[10], (DEPTH, DN_HEADS), jnp.float32, math.log(1e-3), math.log(1e-1)))
    return {
        'x_prompt': nrm(0, (BATCH, SEQ, D_MODEL), 1.0),
        'x_sample': nrm(1, (DEC_BATCH, DEC_SEQ, D_MODEL), 1.0),
        'state_pool': nrm(2, (DEPTH, DEC_BATCH, POOL_BUF, POOL_WIDTH), 1.0),
        'state_conv': nrm(3, (DEPTH, DEC_BATCH, CONV_W - 1, CONV_CH), 1.0),
        'state_delta': nrm(4, (DEPTH, DEC_BATCH, DN_HEADS, DN_DK, DN_DV), DN_DK ** -0.5),
        'cache_swa_k': nrm(5, (DEPTH, DEC_BATCH, swa_buf, SWA_KV_HEADS, SWA_HD), 1.0),
        'cache_swa_v': nrm(6, (DEPTH, DEC_BATCH, swa_buf, SWA_KV_HEADS, SWA_HD), 1.0),
        'meta_tokens': nrm(7, (N_META, D_MODEL), 1.0),
        'norm1_w': 1.0 + nrm(8, (DEPTH, D_MODEL), 0.05),
        'w_in': nrm(9, (DEPTH, D_MODEL, D_IN), D_MODEL ** -0.5),
        'pool_w': nrm(11, (DEPTH, POOL_GROUPS, POOL_GROUP_DIM, POOL_GROUP_DIM), POOL_GROUP_DIM ** -0.5),
        'pool_scale': 1.0 + nrm(12, (DEPTH, POOL_WIDTH), 0.1),
        'dn_conv_w': nrm(13, (DEPTH, CONV_W, CONV_CH), CONV_W ** -0.5),
        'dn_a_log': jnp.log(jax.random.uniform(ks[14], (DEPTH, DN_HEADS), jnp.float32, 1.0, 16.0)),
        'dn_dt_bias': dt + jnp.log(-jnp.expm1(-dt)),
        'dn_onorm_w': 1.0 + nrm(15, (DEPTH, DN_DV), 0.05),
        'swa_sinks': nrm(16, (DEPTH, SWA_HEADS), 0.5),
        'proj_a': nrm(17, (DEPTH, POOL_WIDTH, D_MODEL), POOL_WIDTH ** -0.5),
        'proj_b': nrm(18, (DEPTH, DN_VW, D_MODEL), DN_VW ** -0.5),
        'proj_c': nrm(19, (DEPTH, SWA_WIDTH, D_MODEL), SWA_WIDTH ** -0.5),
        'w_out': nrm(20, (DEPTH, D_MODEL, D_MODEL), D_MODEL ** -0.5),
        'norm2_w': 1.0 + nrm(21, (DEPTH, D_MODEL), 0.05),
        'w_up': nrm(22, (DEPTH, D_MODEL, D_FF), D_MODEL ** -0.5),
        'w_down': nrm(23, (DEPTH, D_FF, D_MODEL), D_FF ** -0.5),
        'final_norm_w': 1.0 + nrm(24, (D_MODEL,), 0.05),
    }


def reference(x_prompt, x_sample, state_pool, state_conv, state_delta, cache_swa_k, cache_swa_v,
              meta_tokens, norm1_w, w_in, pool_w, pool_scale, dn_conv_w, dn_a_log, dn_dt_bias,
              dn_onorm_w, swa_sinks, proj_a, proj_b, proj_c, w_out, norm2_w, w_up, w_down, final_norm_w):
    Bp = x_prompt.shape[0]
    xp = jnp.concatenate([jnp.broadcast_to(meta_tokens[None].astype(x_prompt.dtype), (Bp, N_META, D_MODEL)),
                          x_prompt], axis=1)
    xs = x_sample
    new_p = []
    new_s = []
    for l in range(DEPTH):
        p = dict(norm1_w=norm1_w[l], w_in=w_in[l], pool_w=pool_w[l], pool_scale=pool_scale[l],
                 dn_conv_w=dn_conv_w[l], dn_a_log=dn_a_log[l], dn_dt_bias=dn_dt_bias[l],
                 dn_onorm_w=dn_onorm_w[l], swa_sinks=swa_sinks[l], proj_a=proj_a[l], proj_b=proj_b[l],
                 proj_c=proj_c[l], w_out=w_out[l], norm2_w=norm2_w[l], w_up=w_up[l], w_down=w_down[l])
        xp, st_p = block(xp, p, 0,
                         jnp.zeros((Bp, POOL_BUF, POOL_WIDTH), xp.dtype),
                         jnp.zeros((Bp, CONV_W - 1, CONV_CH), xp.dtype),
                         jnp.zeros((Bp, DN_HEADS, DN_DK, DN_DV), xp.dtype),
                         None, None, True)
        xs, st_s = block(xs, p, PAST_LEN, state_pool[l], state_conv[l], state_delta[l],
                         cache_swa_k[l], cache_swa_v[l], False)
        new_p.append(st_p)
        new_s.append(st_s)
    y_prompt = rmsnorm(xp, final_norm_w)[:, N_META:]
    y_sample = rmsnorm(xs, final_norm_w)
    pool_p = jnp.stack([st[0] for st in new_p])
    conv_p = jnp.stack([st[1] for st in new_p])
    delta_p = jnp.stack([st[2] for st in new_p])
    k_p = jnp.stack([st[3] for st in new_p])
    v_p = jnp.stack([st[4] for st in new_p])
    pool_s = jnp.stack([st[0] for st in new_s])
    conv_s = jnp.stack([st[1] for st in new_s])
    delta_s = jnp.stack([st[2] for st in new_s])
    k_s = jnp.stack([st[3] for st in new_s])
    v_s = jnp.stack([st[4] for st in new_s])
    return (y_prompt, y_sample, pool_p, conv_p, delta_p, k_p, v_p, pool_s, conv_s, delta_s, k_s, v_s)
```

```python
import contextlib
import numpy as np
import concourse.bass as bass
import concourse.mybir as mybir
from concourse.bass_utils import run_bass_kernel_spmd

F32 = mybir.dt.float32
BF16 = mybir.dt.bfloat16
AF = mybir.ActivationFunctionType
ALU = mybir.AluOpType
AX = mybir.AxisListType

SAME_ENGINE_SYNC = ("gpsimd", "vector", "scalar")
ENGS = ("tensor", "vector", "scalar", "gpsimd", "sync")
BIG = 30000.0


class Buf:
    __slots__ = ("name", "last_w", "readers", "aliases", "excl")

    def __init__(self, name="", excl=False):
        self.excl = excl
        self.name = name
        self.last_w = None
        self.readers = []
        self.aliases = []


class Op:
    __slots__ = ("eng", "fn", "deps", "is_dma", "signal", "count", "slot", "use", "dbg", "pos", "edeps")

    def __init__(self, eng, fn, is_dma):
        self.eng = eng
        self.fn = fn
        self.deps = []
        self.is_dma = is_dma
        self.signal = False
        self.count = 0
        self.slot = None
        self.use = 0


class _Rec:
    def __init__(self):
        self.call = None

    def __getattr__(self, name):
        def f(*a, **k):
            self.call = (name, a, k)
            return None
        return f


class Prog:
    def __init__(self, nc, n_dma_slots=8):
        self.nc = nc
        self.ops = {e: [] for e in ENGS}
        self.n_dma_slots = n_dma_slots
        self.dma_rr = {e: 0 for e in ENGS}
        self.dma_uses = {}

    def add(self, eng, fn, reads=(), writes=(), dma=False):
        rec = _Rec()
        fn(rec)
        name_, a_, k_ = rec.call
        op = Op(eng, (lambda e: getattr(e, name_)(*a_, **k_)), dma)
        op.dbg = name_
        deps = {}
        wr = []
        for b in writes:
            wr.append(b)
            wr.extend(b.aliases)
        for b in reads:
            if b.last_w is not None:
                deps[id(b.last_w)] = b.last_w
            if b.excl:
                for r in b.readers:
                    if r.eng != eng:
                        deps[id(r)] = r
        for b in wr:
            if b.last_w is not None:
                deps[id(b.last_w)] = b.last_w
            for r in b.readers:
                deps[id(r)] = r
        op.deps = list(deps.values())
        for b in reads:
            b.readers.append(op)
        for b in wr:
            b.last_w = op
            b.readers = []
        if dma:
            s = self.dma_rr[eng]
            self.dma_rr[eng] = (s + 1) % self.n_dma_slots
            key = (eng, s)
            self.dma_uses[key] = self.dma_uses.get(key, 0) + 1
            op.slot = key
            op.use = self.dma_uses[key]
        self.ops[eng].append(op)
        return op

    def pe(self, fn, reads=(), writes=()):
        return self.add("tensor", fn, reads, writes)

    def dve(self, fn, reads=(), writes=()):
        return self.add("vector", fn, reads, writes)

    def act(self, fn, reads=(), writes=()):
        return self.add("scalar", fn, reads, writes)

    def pool(self, fn, reads=(), writes=()):
        return self.add("gpsimd", fn, reads, writes)

    def dma(self, out, in_, reads=(), writes=(), eng="gpsimd", **kw):
        return self.add(eng, lambda e: e.dma_start(out=out, in_=in_, **kw), reads, writes, dma=True)

    def emit(self):
        nc = self.nc
        for e in ENGS:
            for i, op in enumerate(self.ops[e]):
                op.pos = i
        for e in ENGS:
            for op in self.ops[e]:
                last = {}
                for d in op.deps:
                    if d.is_dma:
                        continue
                    if d.eng == op.eng and d.eng not in SAME_ENGINE_SYNC:
                        continue
                    if d.eng not in last or d.pos > last[d.eng].pos:
                        last[d.eng] = d
                op.edeps = list(last.values())
                for d in op.edeps:
                    d.signal = True
        for e in ENGS:
            c = 0
            for op in self.ops[e]:
                if op.signal and not op.is_dma:
                    c += 1
                op.count = c
        with contextlib.ExitStack() as st:
            esem = {e: st.enter_context(nc.semaphore("s_" + e)) for e in ENGS}
            dsem = {}
            for key in self.dma_uses:
                dsem[key] = st.enter_context(nc.semaphore("d_%s_%d" % key))
            block = st.enter_context(nc.Block())

            def make(ename):
                ops = self.ops[ename]

                def body(eng):
                    waited = {}

                    def wait(sem, val):
                        k = id(sem)
                        if waited.get(k, 0) >= val:
                            return
                        waited[k] = val
                        eng.wait_ge(sem, val)

                    for op in ops:
                        if DBG.get("trace"):
                            print("OP", ename, op.dbg, "sig" if op.signal else "", op.count, "slot", op.slot, op.use,
                                  "deps", [(d.eng, d.dbg, (d.slot, d.use) if d.is_dma else d.count) for d in op.deps])
                        for d in op.deps:
                            if d.is_dma:
                                wait(dsem[d.slot], 16 * d.use)
                        for d in op.edeps:
                            wait(esem[d.eng], d.count)
                        if op.is_dma:
                            if op.use > 1:
                                wait(dsem[op.slot], 16 * (op.use - 1))
                            op.fn(eng).then_inc(dsem[op.slot], 16)
                        else:
                            ins = op.fn(eng)
                            if op.signal:
                                ins.then_inc(esem[ename], 1)
                    for key, uses in self.dma_uses.items():
                        if key[0] == ename:
                            wait(dsem[key], 16 * uses)
                return body

            block.tensor(make("tensor"))
            block.vector(make("vector"))
            block.scalar(make("scalar"))
            block.gpsimd(make("gpsimd"))
            block.sync(make("sync"))


D = 1024
KC = 8
DIN = 6408
DFF = 4096
L = 2
NMETA = 16
PAST = 16384
C_UA, C_QB, C_KB, C_VB, C_Z, C_B, C_SQ, C_SK, C_SV, C_GA, C_GB, C_GC = (
    0, 512, 1024, 1536, 2048, 2560, 2568, 3080, 3208, 3336, 4360, 5384)
NS = 16
TS = 8


DBG = {}


def build(NT):
    TP = NT * 512 + 16
    nc = bass.Bass("TRN2", target_bir_lowering=False)
    P = Prog(nc)

    def din(name, shape, dt=F32):
        return nc.dram_tensor(name, list(shape), dt, kind="ExternalInput").ap()

    def dout(name, shape, dt=F32):
        return nc.dram_tensor(name, list(shape), dt, kind="ExternalOutput").ap()

    def dscr(name, shape, dt=BF16):
        return nc.dram_tensor(name, list(shape), dt, kind="Internal").ap()

    xin = din("xin", [TP, D])
    xs_in = din("xs", [NS * TS, D])
    st_pool = din("st_pool", [L, NS * 15, 512])
    st_conv = din("st_conv", [L, NS * 3, 1536])
    st_delta = din("st_delta", [L, NS, 4, 128, 128])
    st_k = din("st_k", [L, NS, 128, 128])
    st_v = din("st_v", [L, NS, 128, 128])
    w_in = din("w_in", [L, D, DIN])
    pool_w = din("pool_w", [L, 4, 128, 128])
    proj = [din("proj_a", [L, 512, D]), din("proj_b", [L, 512, D]), din("proj_c", [L, 512, D])]
    w_out = din("w_out", [L, D, D])
    w_up = din("w_up", [L, D, DFF])
    w_down = din("w_down", [L, DFF, D])
    pvec = din("pvec", [128, 96])
    convw = din("convw", [128, L * 12 * 4])
    rowc = din("rowc", [1, 64])
    rope_p = din("rope_p", [TP, 64])
    rope_s = din("rope_s", [128, 64])
    cm = din("cm", [128, 6 * 128])
    cmask = din("cmask", [128, 4 * 128])
    band = din("band", [128, 3 * 256])
    smask = din("smask", [NS, 128, 256])
    selp = din("selp", [3, 128, NS * 23])
    selc = din("selc", [2, 128, NS * 11])
    invc = din("invc", [128, 4 * 16])
    dmask = din("dmask", [128, 4 * 128], BF16)

    y_p = dout("y_p", [TP, D])
    y_s = dout("y_s", [NS * TS, D])
    o_pool_p = dout("pool_p", [L, 15, 512])
    o_conv_p = dout("conv_p", [L, 3, 1536])
    o_delta_p = dout("delta_p", [L, 4, 128, 128])
    o_k_p = dout("k_p", [L, 128, 128])
    o_v_p = dout("v_p", [L, 128, 128])
    o_pool_s = dout("pool_s", [L, NS, 15, 512])
    o_conv_s = dout("conv_s", [L, NS, 3, 1536])
    o_delta_s = dout("delta_s", [L, NS, 4, 128, 128])
    o_k_s = dout("k_s", [L, NS, 128, 128])
    o_v_s = dout("v_s", [L, NS, 128, 128])

    win_b = [dscr("win_b%d" % l, [D, DIN]) for l in range(L)]
    poolw_b = [dscr("poolw_b%d" % l, [512, 128]) for l in range(L)]
    proj_b = [[dscr("proj_b%d_%d" % (l, i), [512, D]) for i in range(3)] for l in range(L)]
    wout_b = [dscr("wout_b%d" % l, [D, D]) for l in range(L)]
    wup_b = [dscr("wup_b%d" % l, [D, DFF]) for l in range(L)]
    wdown_b = [dscr("wdown_b%d" % l, [DFF, D]) for l in range(L)]
    castB = []

    with contextlib.ExitStack() as st:
        def sb(name, shape, dt):
            return st.enter_context(nc.sbuf_tensor(name, list(shape), dt))

        banks = [st.enter_context(nc.psum_tensor("pb%d" % i, [128, 512], F32)) for i in range(8)]
        bbufs = [Buf("pb%d" % i, excl=True) for i in range(8)]
        rot = {"all": 0, "d": 0, "s": 0}
        BANKSETS = {"all": [0, 1, 2, 3, 4, 5, 6], "d": [0, 1, 2, 3], "s": [4, 5, 6]}

        def bank(pool="all"):
            rot[pool] = (rot[pool] + 1) % len(BANKSETS[pool])
            i_ = BANKSETS[pool][rot[pool]]
            return banks[i_], bbufs[i_]
        LB, LBb = banks[7], bbufs[7]

        evi = [0]

        def evac(out, in_, reads, writes, func=None, scale=1.0, eng=None):
            if func is not None:
                eng = "act"
            if eng is None:
                evi[0] ^= 1
                eng = "act" if evi[0] else "dve"
                if DBG.get("evdve"):
                    eng = "dve"
                if DBG.get("evact"):
                    eng = "act"
            if eng == "act":
                f = func if func is not None else AF.Copy
                P.act(lambda e: e.activation(out=out, in_=in_, func=f, scale=scale), reads, writes)
            else:
                P.dve(lambda e: e.tensor_copy(out=out, in_=in_), reads, writes)

        pv = sb("pv", [128, 96], F32); pvB = Buf()
        cw = sb("cw", [128, L * 48], F32); cwB = Buf()
        rc = sb("rc", [128, 64], F32); rcB = Buf()
        cm32 = sb("cm32", [128, 6 * 128], F32); cm32B = Buf()
        cmk = sb("cmk", [128, 4 * 128], F32); cmkB = Buf()
        bnd = sb("bnd", [128, 3 * 256], F32); bndB = Buf()
        ivc = sb("ivc", [128, 64], F32); ivcB = Buf()
        idb = sb("idb", [128, 128], BF16); idbB = Buf()
        oneb = sb("oneb", [128, 128], BF16); onebB = Buf()
        P.dma(pv[:], pvec, writes=[pvB])
        P.dma(cw[:], convw, writes=[cwB])
        P.dma(rc[:], rowc.partition_broadcast(128), writes=[rcB])
        P.dma(cm32[:], cm, writes=[cm32B])
        P.dma(cmk[:], cmask, writes=[cmkB])
        P.dma(bnd[:], band, writes=[bndB])
        P.dma(ivc[:], invc, writes=[ivcB])

        def CM(i):
            return cm32[:, i * 128:(i + 1) * 128]
        ID32, ONE32, U_P, U_S, BM_P, BM_S = CM(0), CM(1), CM(2), CM(3), CM(4), CM(5)
        P.dve(lambda e: e.tensor_copy(out=idb[:], in_=ID32), [cm32B], [idbB])
        P.dve(lambda e: e.tensor_copy(out=oneb[:], in_=ONE32), [cm32B], [onebB])
        cmb = sb("cmb", [128, 12 * 128], BF16); cmbB = Buf()
        P.dve(lambda e: e.tensor_copy(out=cmb[:, 0:512], in_=cm32[:, 256:768]), [cm32B], [cmbB])
        P.dve(lambda e: e.tensor_copy(out=cmb[:, 512:1024], in_=cmk[:, :]), [cmkB, cmbB], [cmbB])
        P.dma(cmb[:, 1024:1536], dmask, reads=[cmbB], writes=[cmbB])
        PV_N1, PV_N2, PV_FN, PV_PS, PV_ON, PV_RM = 0, 16, 32, 40, 48, 50
        negA = sb("negA", [128, 8], F32); negAB = Buf()
        P.act(lambda e: e.activation(out=negA[:], in_=rc[:, 0:8], func=AF.Exp), [rcB], [negAB])
        P.dve(lambda e: e.tensor_scalar(out=negA[:], in0=negA[:], scalar1=-1.0, scalar2=None, op0=ALU.mult), [negAB], [negAB])

        def _cb():
            castB.append(Buf())
            return castB[-1]

        for l in range(L):
            for r in range(0, D, 128):
                P.dma(win_b[l][r:r + 128, :], w_in[l, r:r + 128, :], writes=[_cb()])
                P.dma(wout_b[l][r:r + 128, :], w_out[l, r:r + 128, :], writes=[_cb()])
                P.dma(wup_b[l][r:r + 128, :], w_up[l, r:r + 128, :], writes=[_cb()])
            for r in range(0, DFF, 128):
                P.dma(wdown_b[l][r:r + 128, :], w_down[l, r:r + 128, :], writes=[_cb()])
            for i in range(3):
                for r in range(0, 512, 128):
                    P.dma(proj_b[l][i][r:r + 128, :], proj[i][l, r:r + 128, :], writes=[_cb()])
            P.dma(poolw_b[l], pool_w[l].rearrange("g c d -> (g c) d"), writes=[_cb()])

        NWB = 3
        wbufs = [sb("wb%d" % i, [128, 8, 512], BF16) for i in range(NWB)]
        wbB = [Buf("wb%d" % i) for i in range(NWB)]
        wsm = [sb("wsm%d" % i, [128, 8, 8], BF16) for i in range(2)]
        wsmB = [Buf() for i in range(2)]

        def layer_specs(l):
            s = []
            W = win_b[l].rearrange("(kc p) m -> p kc m", p=128)
            for c0 in (C_UA, C_QB, C_KB, C_VB, C_Z):
                s.append(("in%d" % c0, W[:, :, c0:c0 + 512], (8, 512)))
            s.append(("ba", W[:, :, C_B:C_B + 8], (8, 8)))
            s.append(("sq", W[:, :, C_SQ:C_SQ + 512], (8, 512)))
            s.append(("skv", W[:, :, C_SK:C_SK + 256], (8, 256)))
            s.append(("poolw", poolw_b[l].rearrange("(g c) d -> c g d", c=128), (4, 128)))
            for i, c0 in enumerate((C_GA, C_GB, C_GC)):
                s.append(("proj%d" % i, proj_b[l][i].rearrange("(kc p) m -> p kc m", p=128), (4, 1024)))
                s.append(("g%d_0" % i, W[:, :, c0:c0 + 512], (8, 512)))
                s.append(("g%d_1" % i, W[:, :, c0 + 512:c0 + 1024], (8, 512)))
            Wo = wout_b[l].rearrange("(kc p) m -> p kc m", p=128)
            s.append(("out0", Wo[:, :, 0:512], (8, 512)))
            s.append(("out1", Wo[:, :, 512:1024], (8, 512)))
            Wu = wup_b[l].rearrange("(kc p) m -> p kc m", p=128)
            Wd = wdown_b[l].rearrange("(kc p) m -> p kc m", p=128)
            for half in range(2):
                for j in range(4):
                    c0 = half * 2048 + j * 512
                    s.append(("up%d" % (half * 4 + j), Wu[:, :, c0:c0 + 512], (8, 512)))
                for ch in range(2):
                    for kg in range(2):
                        k0 = half * 16 + kg * 8
                        s.append(("dn%d_%d_%d" % (half, kg, ch), Wd[:, k0:k0 + 8, ch * 512:(ch + 1) * 512], (8, 512)))
            return s

        n_tl = (NT + 2)
        specs = []
        for t in range(n_tl):
            for l in range(L):
                specs.extend(layer_specs(l))
        HOLD = {"sq": 1, "proj0": 2, "proj1": 2, "proj2": 2}
        wstate = {"issued": 0, "next": 0, "big": 0, "sm": 0, "loc": {}, "occ": [None] * NWB}

        def w_can_issue(j, i):
            if specs[j][0] == "ba":
                return True
            p = wstate["occ"][wstate["big"] % NWB]
            return p is None or p + HOLD.get(specs[p][0], 0) < i

        def w_issue(i):
            name, src, (a, b) = specs[i]
            if name == "ba":
                k = wstate["sm"] % 2
                wstate["sm"] += 1
                dst, bf = wsm[k][:, 0:a, 0:b], wsmB[k]
                view = wsm[k]
            else:
                k = wstate["big"] % NWB
                wstate["big"] += 1
                wstate["occ"][k] = i
                flat = wbufs[k][:].rearrange("p a b -> p (a b)")
                view = flat[:, 0:a * b].rearrange("p (a b) -> p a b", a=a)
                dst, bf = view, wbB[k]
            P.dma(dst, src, reads=castB[-8:], writes=[bf], eng="sync")
            wstate["loc"][i] = (view, bf)

        def wget(name):
            i = wstate["next"]
            assert specs[i][0] == name, (specs[i][0], name)
            while wstate["issued"] < min(len(specs), i + NWB) and w_can_issue(wstate["issued"], i):
                w_issue(wstate["issued"])
                wstate["issued"] += 1
            assert wstate["issued"] > i, ("weight buffer deadlock", name, i)
            wstate["next"] += 1
            return wstate["loc"].pop(i)

        xT = sb("xT", [128, 8, 512], F32); xTB = [Buf("xT%d" % c) for c in range(8)]
        xn = sb("xn", [128, 8, 512], BF16); xnB = [Buf("xn%d" % c) for c in range(8)]
        xtm0 = sb("xtm0", [128, 1024], F32); xtm = [xtm0, xtm0]; _xb = Buf(); xtmB = [_xb, _xb]
        sq = [sb("sq%d" % i, [128, 512], BF16) for i in range(2)]; sqB = [Buf(), Buf()]
        rstd = sb("rstd", [128, 512], F32); rstdB = Buf()
        R1 = sb("R1", [128, 8320], F32)
        exta = R1[:, 0:4 * 528].rearrange("p (g m) -> p g m", g=4)
        qkvp = R1[:, 2112:2112 + 12 * 516].rearrange("p (g m) -> p g m", g=12)
        macc = R1[:, 0:4096].rearrange("p (g m) -> p g m", g=8)
        upb = R1[:, 4096:8192].bitcast(BF16).rearrange("p (g m) -> p g m", g=16)
        extaB = [Buf("exta%d" % g) for g in range(4)]
        qkvpB = [Buf("qkvp%d" % g) for g in range(12)]
        maccB = [Buf("macc%d" % g) for g in range(8)]
        upB = [Buf("up%d" % g) for g in range(16)]
        for a in extaB + qkvpB:
            for b in maccB + upB:
                a.aliases.append(b)
                b.aliases.append(a)
        oT = R1[:, 0:2048].rearrange("p (g m) -> p g m", g=4); oTB = [Buf() for g in range(4)]
        for a in oTB:
            for b in extaB + maccB[0:4]:
                a.aliases.append(b)
                b.aliases.append(a)
        pscr = [sb("pscr%d" % i, [128, 528], F32) for i in range(2)]; pscrB = [Buf(), Buf()]
        dpool = sb("dpool", [128, 4, 512], BF16); dpoolB = [Buf() for g in range(4)]
        oa = sb("oa", [128, 4, 512], BF16); oaB = [Buf() for g in range(4)]
        ob = sb("ob", [128, 4, 512], BF16); obB = [Buf() for g in range(4)]
        oc = sb("oc", [128, 4, 512], BF16); ocB = [Buf() for g in range(4)]
        zs = sb("zs", [128, 4, 512], BF16); zsB = [Buf() for g in range(4)]
        cvo = sb("cvo", [128, 512], F32); cvoB = Buf()
        R3 = sb("R3", [128, 8, 512], BF16)
        qn = R3[:, 0:4, :]; qnB = [Buf() for g in range(4)]
        kn = R3[:, 4:8, :]; knB = [Buf() for g in range(4)]
        vf = sb("vf", [128, 4, 512], BF16); vfB = [Buf() for g in range(4)]
        sil = sb("sil", [128, 512], F32); silB = Buf()
        mbf = R3; mbfB = [Buf() for g in range(8)]
        for g in range(4):
            for a, b in ((qnB[g], mbfB[g]), (knB[g], mbfB[4 + g])):
                a.aliases.append(b)
                b.aliases.append(a)
        qtm = sb("qtm", [128, 512], F32); qtmB = Buf()
        kvtm = sb("kvtm", [128, 256], F32); kvtmB = Buf()
        rp = sb("rp", [128, 64], F32); rpB = Buf()
        rt = [sb("rt%d" % i, [128, 256], F32) for i in range(2)]; rtB = [Buf(), Buf()]
        qr = sb("qr", [128, 512], BF16); qrB = Buf()
        kr32 = sb("kr32", [128, 128], F32); kr32B = Buf()
        krb = sb("krb", [128, 128], BF16); krbB = Buf()
        qTt = sb("qTt", [64, 8, 128], BF16); qTtB = Buf()
        kTc = sb("kTc", [64, 2, 256], BF16); kTcB = Buf()
        vtm = sb("vtm", [128, 2, 128], BF16); vtmB = [Buf(), Buf()]
        R2 = sb("R2", [128, 2048], F32)
        smx = R2[:, :].rearrange("p (h m) -> p h m", h=8); smxB = Buf()
        sg = [R2[:, 0:512], R2[:, 512:1024]]; sgB = [Buf(), Buf()]
        rl = [R2[:, 1024:1536], R2[:, 1536:2048]]; rlB = [Buf(), Buf()]
        for b in sgB + rlB:
            b.aliases.append(smxB)
            smxB.aliases.append(b)
        R4 = sb("R4", [128, 4096], BF16)
        pexp = R4[:, 0:2048].rearrange("p (h m) -> p h m", h=8); pexpB = Buf()
        pT = R4[:, 2048:4096].rearrange("p (a m) -> p a m", a=16); pTB = Buf()
        T32 = R1[:, 2304:2816].rearrange("p (h m) -> p h m", h=4); T32B = Buf()
        Tm = R1[:, 2816:3072].bitcast(BF16).rearrange("p (h m) -> p h m", h=4); TmB = Buf()
        P1a = R1[:, 3072:3328].bitcast(BF16).rearrange("p (h m) -> p h m", h=4); P1aB = Buf()
        P1b = R1[:, 3328:3584].bitcast(BF16).rearrange("p (h m) -> p h m", h=4); P1bB = Buf()
        Nf = R1[:, 3584:3840].bitcast(BF16).rearrange("p (h m) -> p h m", h=4); NfB = Buf()
        NTf = R1[:, 3840:4096].bitcast(BF16).rearrange("p (h m) -> p h m", h=4); NTfB = Buf()
        for a in (T32B, TmB, P1aB, P1bB, NfB, NTfB):
            for b in qkvpB + maccB[4:8]:
                a.aliases.append(b)
                b.aliases.append(a)
        st8 = sb("st8", [128, 64], F32); st8B = Buf()
        otm = sb("otm", [128, 512], BF16); otmB = Buf()
        poolh = sb("poolh", [128, L, 4, 16], F32); poolhB = [Buf() for l in range(L)]
        convh = sb("convh", [128, L, 12, 4], F32); convhB = [Buf() for l in range(L)]
        S32 = sb("S32", [128, L, 4, 128], F32); S32B = [Buf() for l in range(L)]
        Sbf = sb("Sbf", [128, 4, 128], BF16); SbfB = Buf()
        kTh = sb("kTh", [64, L, 2, 128], BF16); kThB = [Buf() for l in range(L)]
        vh = sb("vh", [128, L, 128], BF16); vhB = [Buf() for l in range(L)]
        bg = sb("bg", [128, 16], F32); bgB = Buf()
        gst = sb("gst", [128, 32], F32); gstB = Buf()
        Ug = sb("Ug", [128, 12, 128], BF16); UgB = Buf()
        g3 = sb("g3", [128, 12], BF16); g3B = Buf()
        dlo = sb("dlo", [128, 4, 128], F32); dloB = Buf()
        dup = sb("dup", [128, 4, 128], F32); dupB = Buf()
        egb = sb("egb", [128, 4, 128], F32); egbB = Buf()
        Qm = [sb("Qm%d" % i, [128, 4, 128], BF16) for i in range(2)]; QmB = [Buf(), Buf()]
        Rm = [sb("Rm%d" % i, [128, 4, 128], BF16) for i in range(2)]; RmB = [Buf(), Buf()]
        Ym = sb("Ym", [128, 4, 128], BF16); YmB = Buf()
        Y32 = sb("Y32", [128, 4, 128], F32); Y32B = Buf()
        qkT = sb("qkT", [128, 4, 128], BF16); qkTB = Buf()
        qgT = sb("qgT", [128, 4, 128], BF16); qgTB = Buf()
        kbg = sb("kbg", [128, 4, 128], BF16); kbgB = Buf()
        kgm = sb("kgm", [128, 4, 128], BF16); kgmB = Buf()
        kgs = sb("kgs", [128, 4, 128], BF16); kgsB = Buf()
        vb = sb("vb", [128, 4, 128], BF16); vbB = Buf()
        u32 = sb("u32", [128, 4, 128], F32); u32B = Buf()
        wT = sb("wT", [128, 4, 128], BF16); wTB = Buf()
        dlt = sb("dlt", [128, 4, 128], BF16); dltB = Buf()
        Sld = [S32[:, 0], S32[:, 1]]; SldB = S32B
        utm = xtm0[:, 0:512]; utmB = _xb
        hb = [sb("hb%d" % i, [128, 512], F32) for i in range(2)]; hbB = [Buf(), Buf()]
        msk = [hb[0][:, 0:256]] * 2; mskB = [hbB[0]] * 2
        ctm = [hb[1][:, 0:256]] * 2; ctmB = [hbB[1]] * 2
        ctb = sb("ctb", [128, 256], BF16); ctbB = Buf()
        ytm = xtm0; ytmB = _xb
        pfix = sb("pfix", [128, 16], F32); pfixB = Buf()
        cvo2 = xtm0[:, 0:512]; cvo2B = Buf()
        sil2 = xtm0[:, 512:1024]; sil2B = Buf()
        rstd2 = hb[0][:, 0:512]; rstd2B = Buf()
        for a, b in ((cvo2B, _xb), (sil2B, _xb), (rstd2B, hbB[0])):
            a.aliases.append(b)
            b.aliases.append(a)
        nul = Buf("outs")
        if DBG.get("mem"):
            print("SBUF remaining", nc.sbuf_bytes_remaining)

        dumps = {}

        def dump(name, ap, bufs, dt):
            if not DBG.get("dump") or name in dumps:
                return
            shp = list(ap.shape)
            d = nc.dram_tensor("dbg_" + name, shp, dt, kind="ExternalOutput").ap()
            dumps[name] = d
            P.dma(d, ap, reads=bufs, writes=[nul])

        def rms_stats(src_fn, src_bufs, nch, n, scale, eps_bias, pool="all", sqk=None, rs=None, rsB=None):
            pb, pbB = bank(pool)
            rs_, rsB_ = (rstd, rstdB) if rs is None else (rs, rsB)
            for c in range(nch):
                k = c % 2 if sqk is None else sqk
                P.act(lambda e, c=c, k=k: e.activation(out=sq[k][:, 0:n], in_=src_fn(c), func=AF.Square),
                      [src_bufs[c]], [sqB[k]])
                P.pe(lambda e, c=c, k=k: e.matmul(pb[:, 0:n], lhsT=oneb[:], rhs=sq[k][:, 0:n],
                                                 start=(c == 0), stop=(c == nch - 1)), [sqB[k], onebB], [pbB])
            P.act(lambda e: e.activation(out=rs_[:, 0:n], in_=pb[:, 0:n], func=AF.Ln, bias=pv[:, 95:96], scale=scale),
                  [pbB, pvB], [rsB_])
            P.act(lambda e: e.activation(out=rs_[:, 0:n], in_=rs_[:, 0:n], func=AF.Exp, scale=-0.5, bias=pv[:, eps_bias:eps_bias + 1]),
                  [rsB_, pvB], [rsB_])

        def rmsnorm_fm(n, wcol0):
            rms_stats(lambda c: xT[:, c, 0:n], xTB, 8, n, 1.0 / D, 92)
            for c in range(8):
                P.dve(lambda e, c=c: e.scalar_tensor_tensor(out=xn[:, c, 0:n], in0=xT[:, c, 0:n],
                                                            scalar=pv[:, wcol0 + c:wcol0 + c + 1], in1=rstd[:, 0:n],
                                                            op0=ALU.mult, op1=ALU.mult),
                      [xTB[c], rstdB, pvB], [xnB[c]])

        def fm_group(wv, wB, mc_list, n, kcs, rhs_fn, rhs_bufs, consume):
            for mi, m0 in enumerate(mc_list):
                pb, pbB = bank()
                for j, kc in enumerate(kcs):
                    P.pe(lambda e, kc=kc, j=j, m0=m0: e.matmul(pb[:, 0:n], lhsT=wv[:, j, m0:m0 + 128], rhs=rhs_fn(kc),
                                                              start=(j == 0), stop=(j == len(kcs) - 1)),
                         [wB, rhs_bufs[kc]], [pbB])
                consume(mi, pb, pbB)

        def tm_group(wv, wB, ncols, r0, nr, consume):
            pb, pbB = bank()
            for kc in range(8):
                P.pe(lambda e, kc=kc: e.matmul(pb[0:nr, 0:ncols], lhsT=xn[:, kc, r0:r0 + nr], rhs=wv[:, kc, 0:ncols],
                                               start=(kc == 0), stop=(kc == 7)), [wB, xnB[kc]], [pbB])
            consume(pb, pbB)

        def tile_layer(l, mode, n, pos0, first, last):
            S, T = (1, n) if mode == "p" else (NS, TS)
            HP, HC = (16, 4) if mode == "p" else (15, 3)
            blocks = [(o, min(128, n - o)) for o in range(0, n, 128)]
            rmsnorm_fm(n, PV_N1 + l * 8)

            def ext_view(base, g, H):
                return base[:, g, 0:S * (H + T)].rearrange("p (s m) -> p s m", s=S)

            need_tm = (mode == "s") or last
            for gi, c0 in enumerate((C_UA, C_QB, C_KB, C_VB)):
                wv, wB = wget("in%d" % c0)
                if mode == "p":
                    def cons(mi, pb, pbB, gi=gi):
                        if gi == 0:
                            evac(exta[:, mi, HP:HP + n], pb[:, 0:n], [pbB], [extaB[mi]])
                        else:
                            g = (gi - 1) * 4 + mi
                            evac(qkvp[:, g, HC:HC + n], pb[:, 0:n], [pbB], [qkvpB[g]])
                    fm_group(wv, wB, [0, 128, 256, 384], n, list(range(8)), lambda kc: xn[:, kc, 0:n], xnB, cons)
                if need_tm:
                    def cons2(pb, pbB, gi=gi):
                        evac(utm[0:n, :], pb[0:n, :], [pbB], [utmB])
                    tm_group(wv, wB, 512, 0, n, cons2)
                    if mode == "s":
                        for s_ in range(NS):
                            if gi == 0:
                                P.dma(o_pool_s[l, s_, 7:15, :], utm[s_ * TS:(s_ + 1) * TS, :], reads=[utmB], writes=[nul])
                            else:
                                P.dma(o_conv_s[l, s_, :, (gi - 1) * 512:gi * 512], utm[s_ * TS + 5:(s_ + 1) * TS, :], reads=[utmB], writes=[nul])
                        if gi == 0:
                            P.dma(o_pool_s[l, :, 0:7, :], st_pool[l].rearrange("(s t) c -> s t c", t=15)[:, 8:15, :], writes=[nul])
                            for i in range(2):
                                P.dma(hb[i][0:120, :], st_pool[l, i * 120:(i + 1) * 120, :], writes=[hbB[i]])
                            for g in range(4):
                                pb, pbB = bank()
                                P.pe(lambda e, g=g, pb=pb: e.transpose(pb[:, 0:128], utm[:, g * 128:(g + 1) * 128], ID32), [utmB, cm32B], [pbB])
                                for i in range(2):
                                    P.pe(lambda e, g=g, pb=pb, i=i: e.transpose(pb[:, 128 + i * 128:256 + i * 128], hb[i][:, g * 128:(g + 1) * 128], ID32), [hbB[i], cm32B], [pbB])
                                ev = exta[:, g, 0:368].rearrange("p (s m) -> p s m", s=NS)
                                evac(ev[:, :, 15:23], pb[:, 0:128].rearrange("p (s t) -> p s t", t=TS), [pbB], [extaB[g]])
                                for i in range(2):
                                    evac(ev[:, i * 8:(i + 1) * 8, 0:15], pb[:, 128 + i * 128:128 + i * 128 + 120].rearrange("p (s t) -> p s t", t=15), [pbB], [extaB[g]])
                        else:
                            hk = gi % 2
                            P.dma(hb[hk][0:48, :], st_conv[l, :, (gi - 1) * 512:gi * 512], writes=[hbB[hk]])
                            for g4 in range(4):
                                g = (gi - 1) * 4 + g4
                                pb, pbB = bank()
                                P.pe(lambda e, g4=g4, pb=pb: e.transpose(pb[:, 0:128], utm[:, g4 * 128:(g4 + 1) * 128], ID32), [utmB, cm32B], [pbB])
                                P.pe(lambda e, g4=g4, pb=pb, hk=hk: e.transpose(pb[:, 128:192], hb[hk][0:64, g4 * 128:(g4 + 1) * 128], ID32[0:64, 0:64]), [hbB[hk], cm32B], [pbB])
                                ev = qkvp[:, g, 0:176].rearrange("p (s m) -> p s m", s=NS)
                                evac(ev[:, :, 3:11], pb[:, 0:128].rearrange("p (s t) -> p s t", t=TS), [pbB], [qkvpB[g]])
                                evac(ev[:, :, 0:3], pb[:, 128:176].rearrange("p (s t) -> p s t", t=3), [pbB], [qkvpB[g]])
                    elif last:
                        if gi == 0:
                            P.dma(o_pool_p[l], utm[1:16, :], reads=[utmB], writes=[nul])
                        else:
                            P.dma(o_conv_p[l, :, (gi - 1) * 512:gi * 512], utm[13:16, :], reads=[utmB], writes=[nul])
            if mode == "p":
                for g in range(4):
                    P.pool(lambda e, g=g: e.tensor_copy(out=exta[:, g, 0:16], in_=poolh[:, l, g, :]), [poolhB[l]], [extaB[g]])
                for g in range(12):
                    P.pool(lambda e, g=g: e.tensor_copy(out=qkvp[:, g, 0:4], in_=convh[:, l, g, :]), [convhB[l]], [qkvpB[g]])

            wv, wB = wget("in%d" % C_Z)

            def consz(mi, pb, pbB):
                evac(zs[:, mi, 0:n], pb[:, 0:n], [pbB], [zsB[mi]], func=AF.Silu)
            fm_group(wv, wB, [0, 128, 256, 384], n, list(range(8)), lambda kc: xn[:, kc, 0:n], xnB, consz)
            wba, wbaB = wget("ba")
            wsq, wsqB = wget("sq")
            wkv, wkvB = wget("skv")


            def pool_chain():
                wpv, wpB = None, None
                for g in range(4):
                    e_ = ext_view(exta, g, HP)
                    W = HP + T
                    cur, curB = e_, extaB[g]
                    for k in range(g + 1):
                        sh = 1 << k
                        o_ = pscr[k % 2][:, 0:S * W].rearrange("p (s m) -> p s m", s=S)
                        P.pool(lambda e, cur=cur, o_=o_, sh=sh, W=W: e.tensor_tensor(out=o_[:, :, sh:W], in0=cur[:, :, sh:W],
                                                                                  in1=cur[:, :, 0:W - sh], op=ALU.add),
                               [curB], [pscrB[k % 2]])
                        yield
                        cur, curB = o_, pscrB[k % 2]
                    w_ = 2 << g
                    dv_ = dpool[:, g, 0:n].rearrange("p (s m) -> p s m", s=S)
                    P.dve(lambda e, cur=cur, e_=e_, dv_=dv_, w_=w_: e.scalar_tensor_tensor(
                        out=dv_, in0=cur[:, :, HP:HP + T], scalar=1.0 / w_, in1=e_[:, :, HP:HP + T],
                        op0=ALU.mult, op1=ALU.subtract), [curB, extaB[g]], [dpoolB[g]])
                    yield
                    if mode == "p" and first:
                        P.dve(lambda e, cur=cur, g=g: e.tensor_tensor(out=pfix[:, 0:16], in0=cur[:, 0, HP:HP + 16],
                                                                      in1=ivc[:, g * 16:(g + 1) * 16], op=ALU.mult),
                              [curB, ivcB], [pfixB])
                        yield
                        P.dve(lambda e, g=g: e.tensor_tensor(out=dpool[:, g, 0:16], in0=pfix[:, 0:16], in1=exta[:, g, HP:HP + 16],
                                                             op=ALU.subtract), [pfixB, extaB[g]], [dpoolB[g]])
                        yield
                    if mode == "p" and not last:
                        P.pool(lambda e, g=g: e.tensor_copy(out=poolh[:, l, g, :], in_=exta[:, g, n:n + 16]), [extaB[g]], [poolhB[l]])
                        yield

            def conv_prologue(gs, cvo, cvoB, sil, silB, rstd, rstdB, sqk):
                for g in gs:
                    e_ = ext_view(qkvp, g, HC)
                    cv = cvo[:, 0:n].rearrange("p (s m) -> p s m", s=S)
                    wc0 = l * 48 + g * 4
                    off = 4 - HC if mode == "p" else 0
                    j0 = 1 if mode == "p" else 0
                    P.act(lambda e, e_=e_, cv=cv, wc0=wc0, j0=j0: e.activation(out=cv, in_=e_[:, :, j0:j0 + T], func=AF.Copy, scale=cw[:, wc0:wc0 + 1]),
                          [qkvpB[g], cwB], [cvoB])
                    yield
                    for j in range(1, 4):
                        P.dve(lambda e, e_=e_, cv=cv, wc0=wc0, j=j, j0=j0: e.scalar_tensor_tensor(
                            out=cv, in0=e_[:, :, j0 + j:j0 + j + T], scalar=cw[:, wc0 + j:wc0 + j + 1], in1=cv,
                            op0=ALU.mult, op1=ALU.add), [qkvpB[g], cwB, cvoB], [cvoB])
                        yield
                    if mode == "p" and not last:
                        P.pool(lambda e, g=g: e.tensor_copy(out=convh[:, l, g, :], in_=qkvp[:, g, n:n + 4]), [qkvpB[g]], [convhB[l]])
                        yield
                    h = g % 4
                    if g < 8:
                        dst, dstB = (qn, qnB) if g < 4 else (kn, knB)
                        P.act(lambda e: e.activation(out=sil[:, 0:n], in_=cvo[:, 0:n], func=AF.Silu), [cvoB], [silB])
                        yield
                        rms_stats(lambda c: sil[:, 0:n], [silB], 1, n, 1.0, 93 if g < 4 else 92, pool="d", sqk=sqk, rs=rstd, rsB=rstdB)
                        yield
                        P.dve(lambda e, dst=dst, h=h: e.tensor_tensor(out=dst[:, h, 0:n], in0=sil[:, 0:n], in1=rstd[:, 0:n], op=ALU.mult),
                              [silB, rstdB], [dstB[h]])
                        yield
                    else:
                        P.act(lambda e, h=h: e.activation(out=vf[:, h, 0:n], in_=cvo[:, 0:n], func=AF.Silu), [cvoB], [vfB[h]])
                        yield

            if mode == "p":
                P.act(lambda e: e.copy(out=Sbf[:], in_=S32[:, l]), [S32B[l]], [SbfB])
                P.pool(lambda e: e.tensor_copy(out=kTc[:, :, 0:128], in_=kTh[:, l]), [kThB[l]], [kTcB])
                P.pool(lambda e: e.tensor_copy(out=vtm[:, 0, :], in_=vh[:, l]), [vhB[l]], [vtmB[0]])
            def delta_chain():
                if mode == "p" and DBG.get("conv2"):
                    subs = [conv_prologue(range(0, 12, 2), cvo, cvoB, sil, silB, rstd, rstdB, 0),
                            conv_prologue(range(1, 12, 2), cvo2, cvo2B, sil2, sil2B, rstd2, rstd2B, 1)]
                else:
                    subs = [conv_prologue(range(12), cvo, cvoB, sil, silB, rstd, rstdB, 0)]
                while subs:
                    for g_ in list(subs):
                        try:
                            next(g_)
                            yield
                        except StopIteration:
                            subs.remove(g_)
                for bi, (b0, bn) in enumerate(blocks):
                    smp = mode == "s"
                    U_, BM_ = (U_S, BM_S) if smp else (U_P, BM_P)
                    A_lo = cmk[:, (2 if smp else 0) * 128:(3 if smp else 1) * 128]
                    A_up = cmk[:, (3 if smp else 1) * 128:(4 if smp else 2) * 128]
                    nlev = 3 if smp else (4 if bn == 16 else 6)
                    def consba(pb, pbB):
                        P.act(lambda e: e.activation(out=bg[0:bn, 0:4], in_=pb[0:bn, 0:4], func=AF.Sigmoid), [pbB], [bgB])
                        P.dve(lambda e: e.tensor_tensor(out=bg[0:bn, 12:16], in0=pb[0:bn, 4:8], in1=rc[0:bn, 8 + l * 4:12 + l * 4], op=ALU.add),
                              [pbB, rcB], [bgB])
                    pbx, pbxB = bank("d")
                    for kc in range(8):
                        P.pe(lambda e, kc=kc: e.matmul(pbx[0:bn, 0:8], lhsT=xn[:, kc, b0:b0 + bn], rhs=wba[:, kc, 0:8],
                                                       start=(kc == 0), stop=(kc == 7)), [wbaB, xnB[kc]], [pbxB])
                        yield
                    consba(pbx, pbxB)
                    yield
                    P.act(lambda e: e.activation(out=bg[0:bn, 12:16], in_=bg[0:bn, 12:16], func=AF.Exp), [bgB], [bgB])
                    yield
                    P.act(lambda e: e.activation(out=bg[0:bn, 12:16], in_=bg[0:bn, 12:16], func=AF.Ln, bias=pv[0:bn, 94:95]), [bgB, pvB], [bgB])
                    yield
                    P.dve(lambda e: e.tensor_tensor(out=bg[0:bn, 4:8], in0=bg[0:bn, 12:16], in1=negA[0:bn, l * 4:l * 4 + 4], op=ALU.mult),
                          [bgB, negAB], [bgB])
                    yield
                    P.dve(lambda e: e.tensor_scalar(out=bg[0:bn, 8:12], in0=bg[0:bn, 0:4], scalar1=-1.0, scalar2=None, op0=ALU.mult), [bgB], [bgB])
                    yield
                    P.dve(lambda e: e.tensor_copy(out=g3[0:bn, 0:4], in_=bg[0:bn, 4:8]), [bgB], [g3B])
                    yield
                    P.dve(lambda e: e.tensor_tensor(out=bg[0:bn, 12:16], in0=bg[0:bn, 4:8], in1=g3[0:bn, 0:4], op=ALU.subtract), [bgB, g3B], [bgB])
                    yield
                    P.dve(lambda e: e.tensor_copy(out=g3[0:bn, 4:8], in_=bg[0:bn, 12:16]), [bgB], [g3B])
                    yield
                    P.dve(lambda e: e.tensor_tensor(out=bg[0:bn, 12:16], in0=bg[0:bn, 12:16], in1=g3[0:bn, 4:8], op=ALU.subtract), [bgB, g3B], [bgB])
                    yield
                    P.dve(lambda e: e.tensor_copy(out=g3[0:bn, 8:12], in_=bg[0:bn, 12:16]), [bgB], [g3B])
                    yield
                    Ub, BMb = cmb[:, (1 if smp else 0) * 128:(2 if smp else 1) * 128], cmb[:, (3 if smp else 2) * 128:(4 if smp else 3) * 128]
                    Alo_b, Aup_b = cmb[:, (6 if smp else 4) * 128:(7 if smp else 5) * 128], cmb[:, (7 if smp else 5) * 128:(8 if smp else 6) * 128]
                    P.dve(lambda e: e.tensor_tensor(out=Ug[0:bn, :, 0:bn], in0=Ub[0:bn, 0:bn].unsqueeze(1).to_broadcast([bn, 12, bn]),
                                                    in1=g3[0:bn, 0:12].unsqueeze(2).to_broadcast([bn, 12, bn]), op=ALU.mult),
                          [cmbB, g3B], [UgB])
                    yield
                    plo, ploB = bank("d")
                    pup, pupB = bank("d")
                    ppl, pplB = bank("d")
                    pg, pgB = bank("d")
                    for (pb_, pbB_, A_) in ((plo, ploB, Alo_b), (pup, pupB, Aup_b), (ppl, pplB, None)):
                        for h in range(4):
                            mo = 128 if A_ is None else bn
                            for q3 in range(3):
                                P.pe(lambda e, pb_=pb_, h=h, A_=A_, mo=mo, q3=q3: e.matmul(pb_[0:mo, h * 128:h * 128 + bn], lhsT=oneb[0:bn, 0:mo], rhs=Ug[0:bn, q3 * 4 + h, 0:bn],
                                                                                       start=(q3 == 0), stop=(A_ is None and q3 == 2)), [onebB, UgB], [pbB_])
                                yield
                            if A_ is not None:
                                P.pe(lambda e, pb_=pb_, h=h, A_=A_: e.matmul(pb_[0:bn, h * 128:h * 128 + bn], lhsT=idb[0:bn, 0:bn], rhs=A_[0:bn, 0:bn],
                                                                           start=False, stop=True), [idbB, cmbB], [pbB_])
                                yield
                    for q3 in range(3):
                        P.pe(lambda e, q3=q3: e.matmul(pg[0:bn, 0:4], lhsT=Ub[0:bn, 0:bn], rhs=g3[0:bn, q3 * 4:q3 * 4 + 4], start=(q3 == 0), stop=(q3 == 2)), [cmbB, g3B], [pgB])
                        yield
                    for q3 in range(3):
                        P.pe(lambda e, q3=q3: e.matmul(pg[0:bn, 4:8], lhsT=BMb[0:bn, 0:bn], rhs=g3[0:bn, q3 * 4:q3 * 4 + 4], start=(q3 == 0), stop=(q3 == 2)), [cmbB, g3B], [pgB])
                        yield
                    P.dve(lambda e: e.tensor_copy(out=gst[0:bn, 0:8], in_=pg[0:bn, 0:8]), [pgB], [gstB])
                    yield
                    P.dve(lambda e: e.tensor_scalar(out=gst[0:bn, 16:20], in0=gst[0:bn, 0:4], scalar1=-1.0, scalar2=None, op0=ALU.mult), [gstB], [gstB])
                    yield
                    P.dve(lambda e: e.tensor_tensor(out=gst[0:bn, 12:16], in0=gst[0:bn, 4:8], in1=gst[0:bn, 0:4], op=ALU.subtract), [gstB], [gstB])
                    yield
                    P.act(lambda e: e.activation(out=gst[0:bn, 8:12], in_=gst[0:bn, 0:4], func=AF.Exp), [gstB], [gstB])
                    yield
                    P.act(lambda e: e.activation(out=gst[0:bn, 12:16], in_=gst[0:bn, 12:16], func=AF.Exp), [gstB], [gstB])
                    yield
                    P.dve(lambda e: e.tensor_tensor(out=gst[0:bn, 8:12], in0=gst[0:bn, 8:12], in1=bg[0:bn, 0:4], op=ALU.mult), [gstB, bgB], [gstB])
                    yield
                    for h in range(4):
                        P.act(lambda e, h=h: e.activation(out=dlo[0:bn, h, 0:bn], in_=plo[0:bn, h * 128:h * 128 + bn], func=AF.Exp,
                                                          bias=gst[0:bn, h:h + 1], scale=-1.0), [ploB, gstB], [dloB])
                        yield
                        P.act(lambda e, h=h: e.activation(out=dup[0:bn, h, 0:bn], in_=pup[0:bn, h * 128:h * 128 + bn], func=AF.Exp,
                                                          bias=gst[0:bn, 16 + h:17 + h], scale=1.0), [pupB, gstB], [dupB])
                        yield
                        P.act(lambda e, h=h: e.activation(out=egb[:, h, 0:bn], in_=ppl[:, h * 128:h * 128 + bn], func=AF.Exp),
                              [pplB], [egbB])
                        yield
                    ptk, ptkB = bank("d")
                    ptkb = ptk[:].bitcast(BF16)
                    for h in range(4):
                        P.pe(lambda e, h=h: e.transpose(ptkb[0:bn, h * 128:(h + 1) * 128], kn[:, h, b0:b0 + bn], idb[:]), [knB[h], idbB], [ptkB])
                        yield
                        P.pe(lambda e, h=h: e.transpose(ptkb[0:bn, 512 + h * 128:512 + (h + 1) * 128], vf[:, h, b0:b0 + bn], idb[:]), [vfB[h], idbB], [ptkB])
                        yield
                    kview = ptkb[0:bn, 0:512].rearrange("p (h m) -> p h m", h=4)
                    vview = ptkb[0:bn, 512:1024].rearrange("p (h m) -> p h m", h=4)
                    P.dve(lambda e: e.tensor_tensor(out=kbg[0:bn], in0=kview, in1=gst[0:bn, 8:12].unsqueeze(2).to_broadcast([bn, 4, 128]), op=ALU.mult),
                          [ptkB, gstB], [kbgB])
                    yield
                    P.dve(lambda e: e.tensor_tensor(out=kgm[0:bn], in0=kview, in1=gst[0:bn, 12:16].unsqueeze(2).to_broadcast([bn, 4, 128]), op=ALU.mult),
                          [ptkB, gstB], [kgmB])
                    yield
                    P.dve(lambda e: e.tensor_tensor(out=vb[0:bn], in0=vview, in1=bg[0:bn, 0:4].unsqueeze(2).to_broadcast([bn, 4, 128]), op=ALU.mult),
                          [ptkB, bgB], [vbB])
                    yield
                    pkk, pkkB = bank("d")
                    pkq, pkqB = bank("d")
                    for h in range(4):
                        P.pe(lambda e, h=h: e.matmul(pkk[0:bn, h * 128:h * 128 + bn], lhsT=kn[:, h, b0:b0 + bn], rhs=kn[:, h, b0:b0 + bn], start=True, stop=True),
                             [knB[h]], [pkkB])
                        yield
                        P.pe(lambda e, h=h: e.matmul(pkq[0:bn, h * 128:h * 128 + bn], lhsT=kn[:, h, b0:b0 + bn], rhs=qn[:, h, b0:b0 + bn], start=True, stop=True),
                             [knB[h], qnB[h]], [pkqB])
                        yield
                    for h in range(4):
                        P.dve(lambda e, h=h: e.scalar_tensor_tensor(out=Nf[0:bn, h, 0:bn], in0=pkk[0:bn, h * 128:h * 128 + bn], scalar=bg[0:bn, 8 + h:9 + h],
                                                                    in1=dlo[0:bn, h, 0:bn], op0=ALU.mult, op1=ALU.mult), [pkkB, bgB, dloB], [NfB])
                        yield
                        P.dve(lambda e, h=h: e.tensor_tensor(out=qkT[0:bn, h, 0:bn], in0=pkq[0:bn, h * 128:h * 128 + bn], in1=dup[0:bn, h, 0:bn], op=ALU.mult),
                              [pkqB, dupB], [qkTB])
                        yield
                        P.dve(lambda e, h=h: e.tensor_tensor(out=qgT[:, h, 0:bn], in0=qn[:, h, b0:b0 + bn], in1=egb[:, h, 0:bn], op=ALU.mult),
                              [qnB[h], egbB], [qgTB])
                        yield
                    ptr, ptrB = bank("d")
                    ptrb = ptr[:].bitcast(BF16)
                    for h in range(4):
                        P.pe(lambda e, h=h: e.transpose(ptrb[0:bn, h * 128:h * 128 + bn], Nf[0:bn, h, 0:bn], idb[0:bn, 0:bn]), [NfB, idbB], [ptrB])
                        yield
                    P.act(lambda e: e.copy(out=NTf[0:bn, :, 0:bn], in_=ptrb[0:bn, 0:512].rearrange("p (h m) -> p h m", h=4)[:, :, 0:bn]), [ptrB], [NTfB])
                    yield

                    def mk(i_):
                        return cmb[0:bn, (8 + i_) * 128:(8 + i_) * 128 + bn].unsqueeze(1).to_broadcast([bn, 4, bn])
                    P.dve(lambda e: e.tensor_tensor(out=Qm[0][0:bn, :, 0:bn], in0=Nf[0:bn, :, 0:bn], in1=mk(0), op=ALU.mult), [NfB, cmbB], [QmB[0]])
                    yield
                    P.pool(lambda e: e.tensor_tensor(out=Rm[0][0:bn, :, 0:bn], in0=NTf[0:bn, :, 0:bn], in1=mk(0), op=ALU.mult), [NTfB, cmbB], [RmB[0]])
                    yield
                    idbc = ID32[0:bn, 0:bn].unsqueeze(1).to_broadcast([bn, 4, bn])
                    P.dve(lambda e: e.tensor_tensor(out=Ym[0:bn, :, 0:bn], in0=Rm[0][0:bn, :, 0:bn], in1=idbc, op=ALU.add), [RmB[0], cm32B], [YmB])
                    yield
                    P.dve(lambda e: e.tensor_tensor(out=Tm[0:bn, :, 0:bn], in0=Qm[0][0:bn, :, 0:bn], in1=idbc, op=ALU.add), [QmB[0], cm32B], [TmB])
                    yield

                    def v4(pb_):
                        return pb_[0:bn, :].rearrange("p (h m) -> p h m", h=4)[:, :, 0:bn]

                    def mm4(pb_, pbB_, lhs, lhsB, rhs, rhsB):
                        for h in range(4):
                            P.pe(lambda e, h=h: e.matmul(pb_[0:bn, h * 128:h * 128 + bn], lhsT=lhs[0:bn, h, 0:bn], rhs=rhs[0:bn, h, 0:bn], start=True, stop=True),
                                 [lhsB, rhsB], [pbB_])

                    def upd(M32, M32B, Mb, MbB, pb_, pbB_):
                        P.dve(lambda e: e.tensor_tensor(out=Mb[0:bn, :, 0:bn], in0=Mb[0:bn, :, 0:bn], in1=v4(pb_), op=ALU.add), [MbB, pbB_], [MbB])

                    cq = 0
                    for lev in range(3):
                        nq_ = 1 - cq
                        pq, pqB = bank("d")
                        pr, prB = bank("d")
                        mm4(pq, pqB, Rm[cq], RmB[cq], Qm[cq], QmB[cq])
                        yield
                        mm4(pr, prB, Qm[cq], QmB[cq], Rm[cq], RmB[cq])
                        yield
                        P.dve(lambda e, nq_=nq_: e.tensor_copy(out=Qm[nq_][0:bn, :, 0:bn], in_=v4(pq)), [pqB], [QmB[nq_]])
                        yield
                        P.act(lambda e, nq_=nq_: e.copy(out=Rm[nq_][0:bn, :, 0:bn], in_=v4(pr)), [prB], [RmB[nq_]])
                        yield
                        py, pyB = bank("d")
                        pt_, pt_B = bank("d")
                        mm4(py, pyB, Qm[nq_], QmB[nq_], Ym, YmB)
                        yield
                        mm4(pt_, pt_B, Rm[nq_], RmB[nq_], Tm, TmB)
                        yield
                        upd(Y32, Y32B, Ym, YmB, py, pyB)
                        yield
                        upd(T32, T32B, Tm, TmB, pt_, pt_B)
                        yield
                        cq = nq_
                    if bn == 128 and not smp:
                        for mi_ in (1, 2, 3):
                            P.dve(lambda e, mi_=mi_: e.tensor_tensor(out=Qm[0][:, :, :], in0=Nf[:, :, :], in1=mk(mi_), op=ALU.mult), [NfB, cmbB], [QmB[0]])
                            yield
                            P.pool(lambda e, mi_=mi_: e.tensor_tensor(out=Rm[0][:, :, :], in0=NTf[:, :, :], in1=mk(mi_), op=ALU.mult), [NTfB, cmbB], [RmB[0]])
                            yield
                            p1, p1B = bank("d")
                            p1p, p1pB = bank("d")
                            mm4(p1, p1B, Rm[0], RmB[0], Tm, TmB)
                            yield
                            mm4(p1p, p1pB, Qm[0], QmB[0], Ym, YmB)
                            yield
                            P.dve(lambda e: e.tensor_copy(out=P1a[:, :, :], in_=v4(p1)), [p1B], [P1aB])
                            yield
                            P.act(lambda e: e.copy(out=P1b[:, :, :], in_=v4(p1p)), [p1pB], [P1bB])
                            yield
                            p2, p2B = bank("d")
                            p2p, p2pB = bank("d")
                            mm4(p2, p2B, Ym, YmB, P1a, P1aB)
                            yield
                            mm4(p2p, p2pB, Tm, TmB, P1b, P1bB)
                            yield
                            upd(T32, T32B, Tm, TmB, p2, p2B)
                            yield
                            upd(Y32, Y32B, Ym, YmB, p2p, p2pB)
                            yield
                    pu, puB = bank("d")
                    pw, pwB = bank("d")
                    for h in range(4):
                        P.pe(lambda e, h=h: e.matmul(pu[0:bn, h * 128:(h + 1) * 128], lhsT=Ym[0:bn, h, 0:bn], rhs=vb[0:bn, h, :], start=True, stop=True), [YmB, vbB], [puB])
                        yield
                        P.pe(lambda e, h=h: e.matmul(pw[:, h * 128:h * 128 + bn], lhsT=kbg[0:bn, h, :], rhs=Ym[0:bn, h, 0:bn], start=True, stop=True), [YmB, kbgB], [pwB])
                        yield
                    P.dve(lambda e: e.tensor_copy(out=u32[0:bn], in_=pu[0:bn, :].rearrange("p (h m) -> p h m", h=4)), [puB], [u32B])
                    yield
                    P.act(lambda e: e.copy(out=wT[:, :, 0:bn], in_=pw[:, :].rearrange("p (h m) -> p h m", h=4)[:, :, 0:bn]), [pwB], [wTB])
                    yield

                    if l == 0 and mode == "p" and first and bi == 0:
                        dump("bg", bg[:], [bgB], F32)
                        dump("gst", gst[:], [gstB], F32)
                        dump("dlo", dlo[:], [dloB], F32)
                        dump("dup", dup[:], [dupB], F32)
                        dump("egb", egb[:], [egbB], F32)
                        dump("u32", u32[:], [u32B], F32)
                        dump("wT", wT[:], [wTB], BF16)
                        dump("qkT", qkT[:], [qkTB], BF16)
                        dump("kgm", kgm[:], [kgmB], BF16)
                        dump("kbg", kbg[:], [kbgB], BF16)
                        dump("qn", qn[:, :, 0:128], qnB, BF16)
                        dump("kn", kn[:, :, 0:128], knB, BF16)
                    def state_step(Ssrc, SsrcB, Sb, SbB, c0, cn, po, poB, kg_, kg_B, glcol, Sdst, SdstB):
                        pws, pwsB = bank("d")
                        for h in range(4):
                            P.pe(lambda e, h=h: e.matmul(pws[0:bn, h * 128:(h + 1) * 128], lhsT=wT[:, h, 0:bn], rhs=Sb[:, h, :], start=True, stop=True), [wTB, SbB], [pwsB])
                            yield
                        P.dve(lambda e: e.tensor_tensor(out=dlt[0:bn], in0=u32[0:bn], in1=pws[0:bn, :].rearrange("p (h m) -> p h m", h=4), op=ALU.subtract),
                              [u32B, pwsB], [dltB])
                        yield
                        for h in range(4):
                            P.pe(lambda e, h=h: e.matmul(po[:, h * 128 + c0:h * 128 + c0 + cn], lhsT=Sb[:, h, :], rhs=qgT[:, h, c0:c0 + cn], start=True, stop=False), [SbB, qgTB], [poB])
                            yield
                            P.pe(lambda e, h=h: e.matmul(po[:, h * 128 + c0:h * 128 + c0 + cn], lhsT=dlt[0:bn, h, :], rhs=qkT[0:bn, h, c0:c0 + cn], start=False, stop=True), [dltB, qkTB], [poB])
                            yield
                        pS, pSB = bank("d")
                        for h in range(4):
                            P.pe(lambda e, h=h: e.matmul(pS[:, h * 128:(h + 1) * 128], lhsT=kg_[0:bn, h, :], rhs=dlt[0:bn, h, :], start=True, stop=True), [kg_B, dltB], [pSB])
                            yield
                        for h in range(4):
                            P.dve(lambda e, h=h: e.scalar_tensor_tensor(out=Sdst[:, h, :], in0=Ssrc[:, h, :], scalar=egb[:, h, glcol:glcol + 1], in1=pS[:, h * 128:(h + 1) * 128],
                                                                        op0=ALU.mult, op1=ALU.add), [SsrcB, egbB, pSB], [SdstB])
                            yield

                    if not smp:
                        po, poB = bank("d")
                        yield from state_step(S32[:, l], S32B[l], Sbf, SbfB, 0, bn, po, poB, kgm, kgmB, bn - 1, S32[:, l], S32B[l])
                        P.act(lambda e: e.copy(out=Sbf[:], in_=S32[:, l]), [S32B[l]], [SbfB])
                        yield
                        evac(oT[:, :, b0:b0 + bn], po[:, :].rearrange("p (h m) -> p h m", h=4)[:, :, 0:bn], [poB], oTB)
                        yield
                    else:
                        for s in range(NS):
                            k = s % 2
                            P.dma(Sld[k][:], st_delta[l, s].rearrange("h k v -> k h v"), writes=[SldB[k]])
                            yield
                            P.act(lambda e, k=k: e.copy(out=Sbf[:], in_=Sld[k][:]), [SldB[k]], [SbfB])
                            yield
                            P.dve(lambda e, s=s: e.tensor_scalar(out=kgs[:], in0=kgm[:], scalar1=pv[:, PV_RM + s:PV_RM + s + 1], scalar2=None, op0=ALU.mult),
                                  [kgmB, pvB], [kgsB])
                            yield
                            yield from state_step(Sld[k], SldB[k], Sbf, SbfB, s * TS, TS, LB, LBb, kgs, kgsB, s * TS + TS - 1, Sld[k], SldB[k])
                            P.dma(o_delta_s[l, s].rearrange("h k v -> k h v"), Sld[k][:], reads=[SldB[k]], writes=[nul])
                            yield
                        evac(oT[:, :, 0:128], LB[:, :].rearrange("p (h m) -> p h m", h=4), [LBb], oTB)
                        yield


            def swa_chain():
                for bi, (b0, bn) in enumerate(blocks):
                    smp = mode == "s"
                    pq_, pq_B = bank("s")
                    for kc in range(8):
                        P.pe(lambda e, kc=kc: e.matmul(pq_[0:bn, :], lhsT=xn[:, kc, b0:b0 + bn], rhs=wsq[:, kc, :], start=(kc == 0), stop=(kc == 7)), [wsqB, xnB[kc]], [pq_B])
                        yield
                    pk_, pk_B = bank("s")
                    for kc in range(8):
                        P.pe(lambda e, kc=kc: e.matmul(pk_[0:bn, 0:256], lhsT=xn[:, kc, b0:b0 + bn], rhs=wkv[:, kc, :], start=(kc == 0), stop=(kc == 7)), [wkvB, xnB[kc]], [pk_B])
                        yield
                    P.act(lambda e: e.copy(out=qtm[0:bn], in_=pq_[0:bn, :]), [pq_B], [qtmB])
                    yield
                    P.dve(lambda e: e.tensor_copy(out=kvtm[0:bn], in_=pk_[0:bn, 0:256]), [pk_B], [kvtmB])
                    yield
                    if smp:
                        P.dma(rp[:], rope_s, writes=[rpB])
                        yield
                    else:
                        P.dma(rp[0:bn], rope_p[pos0 + b0:pos0 + b0 + bn, :], writes=[rpB])
                        yield

                    def rope(src, srcB, nh, dst32, dst32B, dstb, dstbB):
                        sv = src.rearrange("p (h t d) -> p h t d", h=nh, t=2)
                        x1, x2 = sv[:, :, 0, :], sv[:, :, 1, :]
                        cosb = rp[0:bn, 0:32].unsqueeze(1).to_broadcast([bn, nh, 32])
                        sinb = rp[0:bn, 32:64].unsqueeze(1).to_broadcast([bn, nh, 32])
                        t0 = rt[0][0:bn, 0:nh * 32].rearrange("p (h d) -> p h d", h=nh)
                        t1 = rt[1][0:bn, 0:nh * 32].rearrange("p (h d) -> p h d", h=nh)
                        dv = dst32.rearrange("p (h t d) -> p h t d", h=nh, t=2)
                        P.dve(lambda e: e.tensor_tensor(out=t0, in0=x1, in1=cosb, op=ALU.mult), [srcB, rpB], [rtB[0]])
                        P.dve(lambda e: e.tensor_tensor(out=t1, in0=x2, in1=sinb, op=ALU.mult), [srcB, rpB], [rtB[1]])
                        P.dve(lambda e: e.tensor_tensor(out=dv[:, :, 0, :], in0=t0, in1=t1, op=ALU.subtract), [rtB[0], rtB[1]], [dst32B])
                        P.dve(lambda e: e.tensor_tensor(out=t0, in0=x2, in1=cosb, op=ALU.mult), [srcB, rpB], [rtB[0]])
                        P.dve(lambda e: e.tensor_tensor(out=t1, in0=x1, in1=sinb, op=ALU.mult), [srcB, rpB], [rtB[1]])
                        P.dve(lambda e: e.tensor_tensor(out=dv[:, :, 1, :], in0=t0, in1=t1, op=ALU.add), [rtB[0], rtB[1]], [dst32B])
                        if dstb is not None:
                            P.act(lambda e: e.copy(out=dstb, in_=dst32), [dst32B], [dstbB])

                    rope(qtm[0:bn, :], qtmB, 8, smx[0:bn, 0:2, :].rearrange("p a b -> p (a b)"), smxB, qr[0:bn, :], qrB)
                    yield
                    rope(kvtm[0:bn, 0:128], kvtmB, 2, kr32[0:bn, :], kr32B, krb[0:bn, :], krbB)
                    yield
                    P.act(lambda e: e.copy(out=vtm[0:bn, 1, :], in_=kvtm[0:bn, 128:256]), [kvtmB], [vtmB[1]])
                    yield
                    if mode == "p":
                        apos = pos0 + b0
                        lo = max(apos, TP - 128)
                        hi = apos + bn
                        if hi > lo:
                            P.dma(o_k_p[l, lo - (TP - 128):hi - (TP - 128), :], kr32[lo - apos:hi - apos, :], reads=[kr32B], writes=[nul])
                            yield
                            P.dma(o_v_p[l, lo - (TP - 128):hi - (TP - 128), :], kvtm[lo - apos:hi - apos, 128:256], reads=[kvtmB], writes=[nul])
                            yield
                    else:
                        for s_ in range(NS):
                            P.dma(o_k_s[l, s_, 120:128, :], kr32[s_ * TS:(s_ + 1) * TS, :], reads=[kr32B], writes=[nul])
                            yield
                            P.dma(o_v_s[l, s_, 120:128, :], kvtm[s_ * TS:(s_ + 1) * TS, 128:256], reads=[kvtmB], writes=[nul])
                            yield
                        P.dma(o_k_s[l, :, 0:120, :], st_k[l, :, 8:128, :], writes=[nul])
                        yield
                        P.dma(o_v_s[l, :, 0:120, :], st_v[l, :, 8:128, :], writes=[nul])
                        yield
                    ptq, ptqB = bank("s")
                    ptqb = ptq[:].bitcast(BF16)
                    for h in range(8):
                        P.pe(lambda e, h=h: e.transpose(ptqb[0:64, h * 128:h * 128 + bn], qr[0:bn, h * 64:(h + 1) * 64], idb[0:bn, 0:bn]), [qrB, idbB], [ptqB])
                        yield
                    P.dve(lambda e: e.tensor_copy(out=qTt[:, :, 0:bn], in_=ptqb[0:64, :].rearrange("p (h m) -> p h m", h=8)[:, :, 0:bn]), [ptqB], [qTtB])
                    yield
                    ptk2, ptk2B = bank("s")
                    ptk2b = ptk2[:].bitcast(BF16)
                    for g in range(2):
                        P.pe(lambda e, g=g: e.transpose(ptk2b[0:64, g * 128:g * 128 + bn], krb[0:bn, g * 64:(g + 1) * 64], idb[0:bn, 0:bn]), [krbB, idbB], [ptk2B])
                        yield
                    P.act(lambda e: e.copy(out=kTc[:, :, 128:128 + bn], in_=ptk2b[0:64, 0:256].rearrange("p (g m) -> p g m", g=2)[:, :, 0:bn]), [ptk2B], [kTcB])
                    yield

                    def attend(maskap, maskB, first_acc, last_acc, pov, povB):
                        nk = 128 + bn
                        for hp_ in range(4):
                            psc, pscB = bank("s")
                            for j in range(2):
                                h = hp_ * 2 + j
                                P.pe(lambda e, h=h, j=j: e.matmul(psc[0:bn, j * 256:j * 256 + nk], lhsT=qTt[:, h, 0:bn], rhs=kTc[:, h // 4, 0:nk], start=True, stop=True),
                                     [qTtB, kTcB], [pscB])
                                yield
                            P.dve(lambda e, hp_=hp_: e.scalar_tensor_tensor(out=smx[0:bn, hp_ * 2:hp_ * 2 + 2, 0:nk], in0=psc[0:bn, :].rearrange("p (j m) -> p j m", j=2)[:, :, 0:nk],
                                                                        scalar=0.125, in1=maskap[0:bn, 0:nk].unsqueeze(1).to_broadcast([bn, 2, nk]),
                                                                        op0=ALU.mult, op1=ALU.add), [pscB, maskB], [smxB])
                            yield
                        P.dve(lambda e: e.tensor_reduce(out=st8[0:bn, 0:8], in_=smx[0:bn, :, 0:nk], axis=AX.X, op=ALU.max), [smxB], [st8B])
                        yield
                        P.dve(lambda e: e.tensor_tensor(out=st8[0:bn, 0:8], in0=st8[0:bn, 0:8], in1=rc[0:bn, 16 + l * 8:24 + l * 8], op=ALU.max), [st8B, rcB], [st8B])
                        yield
                        P.dve(lambda e: e.tensor_scalar(out=st8[0:bn, 8:16], in0=st8[0:bn, 0:8], scalar1=-1.0, scalar2=None, op0=ALU.mult), [st8B], [st8B])
                        yield
                        P.dve(lambda e: e.tensor_tensor(out=st8[0:bn, 16:24], in0=rc[0:bn, 16 + l * 8:24 + l * 8], in1=st8[0:bn, 0:8], op=ALU.subtract), [st8B, rcB], [st8B])
                        yield
                        P.act(lambda e: e.activation(out=st8[0:bn, 16:24], in_=st8[0:bn, 16:24], func=AF.Exp), [st8B], [st8B])
                        yield
                        P.dve(lambda e: e.memset(st8[0:bn, 24:32], 0.0), [st8B], [st8B])
                        yield
                        for h in range(8):
                            P.act(lambda e, h=h: e.activation(out=smx[0:bn, h, 0:nk], in_=smx[0:bn, h, 0:nk], func=AF.Exp, bias=st8[0:bn, 8 + h:9 + h],
                                                              accum_out=st8[0:bn, 24 + h:25 + h]), [smxB, st8B], [smxB, st8B])
                            yield
                        P.dve(lambda e: e.tensor_tensor(out=st8[0:bn, 32:40], in0=st8[0:bn, 24:32], in1=st8[0:bn, 16:24], op=ALU.add), [st8B], [st8B])
                        yield
                        P.dve(lambda e: e.reciprocal(out=st8[0:bn, 40:48], in_=st8[0:bn, 32:40]), [st8B], [st8B])
                        yield
                        P.dve(lambda e: e.tensor_tensor(out=pexp[0:bn, :, 0:nk], in0=smx[0:bn, :, 0:nk], in1=st8[0:bn, 40:48].unsqueeze(2).to_broadcast([bn, 8, nk]), op=ALU.mult),
                              [smxB, st8B], [pexpB])
                        yield
                        for half in range(2):
                            ptp, ptpB = bank("s")
                            ptpb = ptp[:].bitcast(BF16)
                            for hh in range(4):
                                h = half * 4 + hh
                                P.pe(lambda e, h=h, hh=hh: e.transpose(ptpb[:, (hh * 2) * 128:(hh * 2) * 128 + bn], pexp[0:bn, h, 0:128], idb[0:bn, 0:bn]), [pexpB, idbB], [ptpB])
                                yield
                                P.pe(lambda e, h=h, hh=hh: e.transpose(ptpb[0:bn, (hh * 2 + 1) * 128:(hh * 2 + 1) * 128 + bn], pexp[0:bn, h, 128:128 + bn], idb[0:bn, 0:bn]), [pexpB, idbB], [ptpB])
                                yield
                            if bn == 128:
                                evac(pT[:, half * 8:half * 8 + 8, :], ptpb[:, :].rearrange("p (a m) -> p a m", a=8), [ptpB], [pTB])
                                yield
                            else:
                                for hh in range(4):
                                    evac(pT[:, half * 8 + hh * 2, 0:bn], ptpb[:, (hh * 2) * 128:(hh * 2) * 128 + bn], [ptpB], [pTB])
                                    yield
                                    evac(pT[0:bn, half * 8 + hh * 2 + 1, 0:bn], ptpb[0:bn, (hh * 2 + 1) * 128:(hh * 2 + 1) * 128 + bn], [ptpB], [pTB])
                                    yield
                        for h in range(8):
                            g = h // 4
                            P.pe(lambda e, h=h, g=g: e.matmul(pov[0:bn, h * 64:(h + 1) * 64], lhsT=pT[:, h * 2, 0:bn], rhs=vtm[:, 0, g * 64:(g + 1) * 64],
                                                              start=first_acc, stop=False), [pTB, vtmB[0]], [povB])
                            yield
                            P.pe(lambda e, h=h, g=g: e.matmul(pov[0:bn, h * 64:(h + 1) * 64], lhsT=pT[0:bn, h * 2 + 1, 0:bn], rhs=vtm[0:bn, 1, g * 64:(g + 1) * 64],
                                                              start=False, stop=last_acc), [pTB, vtmB[1]], [povB])
                            yield

                    if not smp:
                        mi_ = 1 if (first and bi == 0) else 0
                        pov, povB = bank("s")
                        yield from attend(bnd[:, mi_ * 256:(mi_ + 1) * 256], bndB, True, True, pov, povB)
                        P.act(lambda e: e.copy(out=otm[0:bn, :], in_=pov[0:bn, :]), [povB], [otmB])
                        yield
                        if l == 0 and first and bi == 0:
                            dump("qr", qr[:], [qrB], BF16)
                            dump("qtm", qtm[:], [qtmB], F32)
                            dump("rp", rp[:], [rpB], F32)
                            dump("kr32", kr32[:], [kr32B], F32)
                            dump("qTt", qTt[:], [qTtB], BF16)
                            dump("kTc", kTc[:], [kTcB], BF16)
                            dump("st8", st8[:], [st8B], F32)
                            dump("pexp", pexp[:], [pexpB], BF16)
                            dump("pT", pT[:], [pTB], BF16)
                            dump("otm", otm[:], [otmB], BF16)
                            dump("vtm", vtm[:], vtmB, BF16)
                        if not last:
                            P.pool(lambda e: e.tensor_copy(out=kTc[:, :, 0:128], in_=kTc[:, :, 128:256]), [kTcB], [kTcB])
                            yield
                            P.pool(lambda e: e.tensor_copy(out=vtm[:, 0, :], in_=vtm[:, 1, :]), [vtmB[1]], [vtmB[0]])
                            yield
                    else:
                        for s in range(NS):
                            k = s % 2
                            P.dma(msk[k], smask[s], writes=[mskB[k]])
                            yield
                            P.dma(ctm[k][:, 0:128], st_k[l, s], writes=[ctmB[k]])
                            yield
                            P.dma(ctm[k][:, 128:256], st_v[l, s], writes=[ctmB[k]])
                            yield
                            P.act(lambda e, k=k: e.copy(out=ctb[:], in_=ctm[k]), [ctmB[k]], [ctbB])
                            yield
                            P.pool(lambda e: e.tensor_copy(out=vtm[:, 0, :], in_=ctb[:, 128:256]), [ctbB], [vtmB[0]])
                            yield
                            pck, pckB = bank("s")
                            pckb = pck[:].bitcast(BF16)
                            for g in range(2):
                                P.pe(lambda e, g=g: e.transpose(pckb[0:64, g * 128:(g + 1) * 128], ctb[:, g * 64:(g + 1) * 64], idb[:]), [ctbB, idbB], [pckB])
                                yield
                            P.dve(lambda e: e.tensor_copy(out=kTc[:, :, 0:128], in_=pckb[0:64, 0:256].rearrange("p (g m) -> p g m", g=2)), [pckB], [kTcB])
                            yield
                            pov, povB = bank("s")
                            yield from attend(msk[k], mskB[k], True, True, pov, povB)
                            if s == 0:
                                P.dve(lambda e: e.tensor_copy(out=qtm[:, :], in_=pov[:, :]), [povB], [qtmB])
                                yield
                            elif s < NS - 1:
                                P.dve(lambda e: e.tensor_tensor(out=qtm[:, :], in0=qtm[:, :], in1=pov[:, :], op=ALU.add), [qtmB, povB], [qtmB])
                                yield
                            else:
                                P.dve(lambda e: e.tensor_tensor(out=otm[:, :], in0=qtm[:, :], in1=pov[:, :], op=ALU.add), [qtmB, povB], [otmB])
                                yield
                    pto, ptoB = bank("s")
                    ptob = pto[:].bitcast(BF16)
                    for c in range(4):
                        P.pe(lambda e, c=c: e.transpose(ptob[:, c * 128:c * 128 + bn], otm[0:bn, c * 128:(c + 1) * 128], idb[0:bn, 0:bn]), [otmB, idbB], [ptoB])
                        yield
                    evac(oc[:, :, b0:b0 + bn], ptob[:, 0:512].rearrange("p (c m) -> p c m", c=4)[:, :, 0:bn], [ptoB], ocB)
                    yield

            chains = [pool_chain(), delta_chain(), swa_chain()]
            while chains:
                for g_ in list(chains):
                    try:
                        next(g_)
                    except StopIteration:
                        chains.remove(g_)
            if mode == "p":
                if not last:
                    P.pool(lambda e: e.tensor_copy(out=kTh[:, l], in_=kTc[:, :, 0:128]), [kTcB], [kThB[l]])
                    P.pool(lambda e: e.tensor_copy(out=vh[:, l], in_=vtm[:, 0, :]), [vtmB[0]], [vhB[l]])
                else:
                    P.dma(o_delta_p[l].rearrange("h k v -> k h v"), S32[:, l], reads=[S32B[l]], writes=[nul])

            for h in range(4):
                rms_stats(lambda c, h=h: oT[:, h, 0:n], [oTB[h]], 1, n, 1.0 / 128.0, 92)
                P.dve(lambda e, h=h: e.scalar_tensor_tensor(out=sil[:, 0:n], in0=oT[:, h, 0:n], scalar=pv[:, PV_ON + l:PV_ON + l + 1], in1=rstd[:, 0:n],
                                                            op0=ALU.mult, op1=ALU.mult), [oTB[h], pvB, rstdB], [silB])
                P.dve(lambda e, h=h: e.tensor_tensor(out=ob[:, h, 0:n], in0=sil[:, 0:n], in1=zs[:, h, 0:n], op=ALU.mult), [silB, zsB[h]], [obB[h]])

            wpv, wpB = wget("poolw")
            for g in range(4):
                pb, pbB = bank()
                P.pe(lambda e, g=g: e.matmul(pb[:, 0:n], lhsT=wpv[:, g, :], rhs=dpool[:, g, 0:n], start=True, stop=True), [wpB, dpoolB[g]], [pbB])
                P.dve(lambda e, g=g: e.tensor_scalar(out=oa[:, g, 0:n], in0=pb[:, 0:n], scalar1=pv[:, PV_PS + l * 4 + g:PV_PS + l * 4 + g + 1], scalar2=None, op0=ALU.mult),
                      [pbB, pvB], [oaB[g]])

            if l == 0 and mode == "p" and first:
                dump("oa", oa[:], oaB, BF16)
                dump("ob", ob[:], obB, BF16)
                dump("oc", oc[:], ocB, BF16)
                dump("oT", oT[:], oTB, F32)
                dump("zs", zs[:], zsB, BF16)
                dump("S", S32[:, 0], [S32B[0]], F32)
            if l == 0 and mode == "s":
                dump("s_oa", oa[:, :, 0:128], oaB, BF16)
                dump("s_ob", ob[:, :, 0:128], obB, BF16)
                dump("s_oc", oc[:, :, 0:128], ocB, BF16)
                dump("s_oT", oT[:, :, 0:128], oTB, F32)
            for i, (osrc, osrcB) in enumerate(((oa, oaB), (ob, obB), (oc, ocB))):
                wpj, wpjB = wget("proj%d" % i)
                for half in range(2):
                    wg, wgB = wget("g%d_%d" % (i, half))
                    for m4 in range(4):
                        mc = half * 4 + m4
                        pg_, pg_B = bank()
                        for kc in range(8):
                            P.pe(lambda e, kc=kc, m4=m4: e.matmul(pg_[:, 0:n], lhsT=wg[:, kc, m4 * 128:(m4 + 1) * 128], rhs=xn[:, kc, 0:n], start=(kc == 0), stop=(kc == 7)),
                                 [wgB, xnB[kc]], [pg_B])
                        k = mc % 2
                        P.act(lambda e, k=k: e.activation(out=sg[k][:, 0:n], in_=pg_[:, 0:n], func=AF.Sigmoid), [pg_B], [sgB[k]])
                        pp_, pp_B = bank()
                        for kc in range(4):
                            P.pe(lambda e, kc=kc, mc=mc: e.matmul(pp_[:, 0:n], lhsT=wpj[:, kc, mc * 128:(mc + 1) * 128], rhs=osrc[:, kc, 0:n], start=(kc == 0), stop=(kc == 3)),
                                 [wpjB, osrcB[kc]], [pp_B])
                        if i == 0:
                            P.dve(lambda e, k=k, mc=mc: e.tensor_tensor(out=macc[:, mc, 0:n], in0=sg[k][:, 0:n], in1=pp_[:, 0:n], op=ALU.mult), [sgB[k], pp_B], [maccB[mc]])
                        else:
                            P.dve(lambda e, k=k: e.tensor_tensor(out=sg[k][:, 0:n], in0=sg[k][:, 0:n], in1=pp_[:, 0:n], op=ALU.mult), [sgB[k], pp_B], [sgB[k]])
                            if i == 1:
                                P.pool(lambda e, k=k, mc=mc: e.tensor_tensor(out=macc[:, mc, 0:n], in0=macc[:, mc, 0:n], in1=sg[k][:, 0:n], op=ALU.add), [sgB[k], maccB[mc]], [maccB[mc]])
                            else:
                                P.pool(lambda e, k=k, mc=mc: e.tensor_tensor(out=mbf[:, mc, 0:n], in0=macc[:, mc, 0:n], in1=sg[k][:, 0:n], op=ALU.add), [sgB[k], maccB[mc]], [mbfB[mc]])

            for half in range(2):
                wo, woB = wget("out%d" % half)

                def conso(mi, pb, pbB, half=half):
                    c = half * 4 + mi
                    P.dve(lambda e: e.tensor_tensor(out=xT[:, c, 0:n], in0=xT[:, c, 0:n], in1=pb[:, 0:n], op=ALU.add), [xTB[c], pbB], [xTB[c]])
                fm_group(wo, woB, [0, 128, 256, 384], n, list(range(8)), lambda kc: mbf[:, kc, 0:n], mbfB, conso)

            if l == 0 and mode == "p" and first:
                dump("mbf", mbf[:], mbfB, BF16)
                dump("h", xT[:], xTB, F32)
            rmsnorm_fm(n, PV_N2 + l * 8)
            for half in range(2):
                for j in range(4):
                    wu, wuB = wget("up%d" % (half * 4 + j))

                    def consu(mi, pb, pbB, j=j):
                        k = mi % 2
                        P.act(lambda e: e.activation(out=rl[k][:, 0:n], in_=pb[:, 0:n], func=AF.Relu), [pbB], [rlB[k]])
                        P.pool(lambda e: e.tensor_tensor(out=upb[:, j * 4 + mi, 0:n], in0=rl[k][:, 0:n], in1=rl[k][:, 0:n], op=ALU.mult), [rlB[k]], [upB[j * 4 + mi]])
                    fm_group(wu, wuB, [0, 128, 256, 384], n, list(range(8)), lambda kc: xn[:, kc, 0:n], xnB, consu)
                acc = {}
                for ch in range(2):
                    for kg in range(2):
                        wd, wdB = wget("dn%d_%d_%d" % (half, kg, ch))
                        for mi in range(4):
                            c = ch * 4 + mi
                            if kg == 0:
                                acc[c] = bank()
                            pb, pbB = acc[c]
                            for kc in range(8):
                                P.pe(lambda e, kc=kc, mi=mi, pb=pb, kg=kg: e.matmul(pb[:, 0:n], lhsT=wd[:, kc, mi * 128:(mi + 1) * 128], rhs=upb[:, kg * 8 + kc, 0:n],
                                                                                start=(kg == 0 and kc == 0), stop=(kg == 1 and kc == 7)), [wdB, upB[kg * 8 + kc]], [pbB])
                            if kg == 1:
                                P.dve(lambda e, c=c, pb=pb: e.tensor_tensor(out=xT[:, c, 0:n], in0=xT[:, c, 0:n], in1=pb[:, 0:n], op=ALU.add), [xTB[c], pbB], [xTB[c]])

        def after_layer(l, mode, first):
            if l == 0 and mode == "p" and first:
                dump("x1", xT[:], xTB, F32)

        def load_x(src, r0, nr, c0):
            k = (r0 // 128) % 2
            P.dma(xtm[k][0:nr, :], src[r0:r0 + nr, :], writes=[xtmB[k]])
            kp = max(32, nr)
            if DBG.get("xdmaonly"):
                return
            for hf in range(2):
                pb, pbB = bank()
                for j in range(4):
                    c = hf * 4 + j
                    P.pe(lambda e, c=c, j=j: e.transpose(pb[:, j * 128:j * 128 + kp], xtm[k][0:kp, c * 128:(c + 1) * 128], ID32[0:kp, 0:kp]), [xtmB[k], cm32B], [pbB])
                if DBG.get("xnoevac"):
                    continue
                for j in range(4):
                    c = hf * 4 + j
                    evac(xT[:, c, c0:c0 + nr], pb[:, j * 128:j * 128 + nr], [pbB], [xTB[c]], eng=None if DBG.get("alt") else ("act" if hf else "dve"))

        def store_y(dst, r0, nr, c0):
            for hf in range(2):
                pb, pbB = bank()
                for j in range(4):
                    c = hf * 4 + j
                    P.pe(lambda e, c=c, j=j: e.transpose(pb[0:nr, j * 128:(j + 1) * 128], macc[:, c, c0:c0 + nr], ID32[:, :]), [maccB[c], cm32B], [pbB])
                evac(ytm[0:nr, hf * 512:(hf + 1) * 512], pb[0:nr, :], [pbB], [ytmB])
            P.dma(dst[r0:r0 + nr, :], ytm[0:nr, :], reads=[ytmB], writes=[nul])

        def final_norm(n):
            rms_stats(lambda c: xT[:, c, 0:n], xTB, 8, n, 1.0 / D, 92)
            for c in range(8):
                P.dve(lambda e, c=c: e.scalar_tensor_tensor(out=macc[:, c, 0:n], in0=xT[:, c, 0:n], scalar=pv[:, PV_FN + c:PV_FN + c + 1], in1=rstd[:, 0:n],
                                                            op0=ALU.mult, op1=ALU.mult), [xTB[c], rstdB, pvB], [maccB[c]])

        P.pool(lambda e: e.memset(xtm0[:], 0.0), [], [_xb])
        for i in range(2):
            P.pool(lambda e, i=i: e.memset(hb[i][:], 0.0), [], [hbB[i]])
        for i in range(2):
            P.pool(lambda e, i=i: e.memset(pscr[i][:], 0.0), [], [pscrB[i]])
        for l in range(L):
            P.pool(lambda e, l=l: e.memset(poolh[:, l], 0.0), [], [poolhB[l]])
            P.pool(lambda e, l=l: e.memset(convh[:, l], 0.0), [], [convhB[l]])
            P.pool(lambda e, l=l: e.memset(S32[:, l], 0.0), [], [S32B[l]])
            P.pool(lambda e, l=l: e.memset(kTh[:, l], 0.0), [], [kThB[l]])
            P.pool(lambda e, l=l: e.memset(vh[:, l], 0.0), [], [vhB[l]])
        P.pool(lambda e: e.memset(vtm[:], 0.0), [], vtmB)
        P.pool(lambda e: e.memset(pT[:], 0.0), [], [pTB])

        tiles = [(t * 512, 512) for t in range(NT)] + [(NT * 512, 16)]
        if DBG.get("notiles"):
            tiles = []
        if DBG.get("notail"):
            tiles = tiles[:-1]
        if DBG.get("tailonly"):
            tiles = tiles[-1:]
        for ti, (p0, n) in enumerate(tiles):
            for r in range(0, n, 128):
                load_x(xin, p0 + r, min(128, n - r), r)
            if not DBG.get("nolayers"):
                for l in range(L):
                    tile_layer(l, "p", n, p0, ti == 0, ti == len(tiles) - 1)
                    after_layer(l, "p", ti == 0)
            if not DBG.get("nofinal"):
                final_norm(n)
            if not DBG.get("nostore"):
                for r in range(0, n, 128):
                    store_y(y_p, p0 + r, min(128, n - r), r)
        if not DBG.get("nosample"):
            load_x(xs_in, 0, 128, 0)
            if not DBG.get("nolayers"):
                for l in range(L):
                    tile_layer(l, "s", 128, PAST, False, False)
            final_norm(128)
            store_y(y_s, 0, 128, 0)
        if not DBG:
            assert wstate["next"] == len(specs), (wstate["next"], len(specs))
        P.emit()
    return nc


def _consts(TP):
    f = np.float32
    cm = np.zeros((6, 128, 128), f)
    cm[0] = np.eye(128)
    cm[1] = 1.0
    i = np.arange(128)
    cm[2] = (i[:, None] <= i[None, :])
    same = (i[:, None] // TS) == (i[None, :] // TS)
    cm[3] = cm[2] * same
    cm[4] = 1.0
    cm[5] = same
    cm = np.ascontiguousarray(cm.transpose(1, 0, 2).reshape(128, 6 * 128))
    a_lo = np.where(i[None, :] < i[:, None], 0.0, BIG).astype(f)
    a_up = np.where(i[None, :] >= i[:, None], 0.0, -BIG).astype(f)
    a_lo_s = np.where((i[None, :] < i[:, None]) & same, 0.0, BIG).astype(f)
    a_up_s = np.where((i[None, :] >= i[:, None]) & same, 0.0, -BIG).astype(f)
    cmask = np.concatenate([a_lo, a_up, a_lo_s, a_up_s], axis=1)
    r = np.arange(128)[:, None]
    j = np.arange(256)[None, :]
    ok = (j > r) & (j <= r + 128)
    band0 = np.where(ok, 0.0, -BIG).astype(f)
    band1 = np.where(ok & (j >= 128), 0.0, -BIG).astype(f)
    band = np.concatenate([band0, band1, band0], axis=1)
    sm = np.full((NS, 128, 256), -BIG, f)
    rs, rt_ = np.arange(128) // TS, np.arange(128) % TS
    for s in range(NS):
        rows = rs == s
        cache_ok = (np.arange(128)[None, :] > rt_[:, None]) & rows[:, None]
        new_ok = (rs[None, :] == s) & (rt_[None, :] <= rt_[:, None]) & rows[:, None]
        sm[s][:, 0:128][cache_ok] = 0.0
        sm[s][:, 128:256][new_ok] = 0.0
    selp = np.zeros((3, 128, NS * 23), f)
    for tok in range(128):
        selp[0, tok, (tok // TS) * 23 + 15 + tok % TS] = 1.0
    for rr in range(240):
        selp[1 + rr // 120, rr % 120, (rr // 15) * 23 + rr % 15] = 1.0
    selc = np.zeros((2, 128, NS * 11), f)
    for tok in range(128):
        selc[0, tok, (tok // TS) * 11 + 3 + tok % TS] = 1.0
    for rr in range(48):
        selc[1, rr, (rr // 3) * 11 + rr % 3] = 1.0
    invc = np.zeros((128, 4 * 16), f)
    for g, w in enumerate((2, 4, 8, 16)):
        invc[:, g * 16:(g + 1) * 16] = 1.0 / np.minimum(w, np.arange(16) + 1)
    half = 32
    inv = (10000.0 ** (-np.arange(half, dtype=np.float32) / half)).astype(f)

    def rope_tab(pos):
        ang = pos.astype(f)[:, None] * inv[None, :]
        return np.concatenate([np.cos(ang), np.sin(ang)], axis=1).astype(f)
    rope_p = rope_tab(np.arange(TP))
    rope_s = rope_tab(PAST + (np.arange(128) % TS))
    import ml_dtypes
    Dm = lambda s_: ((i[:, None] // s_) == (i[None, :] // s_)).astype(f)
    dmask = np.concatenate([Dm(16), Dm(32) - Dm(16), Dm(64) - Dm(32), 1.0 - Dm(64)], axis=1).astype(ml_dtypes.bfloat16)
    return dict(dmask=dmask, cm=cm, cmask=cmask, band=band, smask=sm, selp=selp, selc=selc, invc=invc, rope_p=rope_p, rope_s=rope_s)


_CACHE = {}


def kernel(x_prompt, x_sample, state_pool, state_conv, state_delta, cache_swa_k, cache_swa_v,
           meta_tokens, norm1_w, w_in, pool_w, pool_scale, dn_conv_w, dn_a_log, dn_dt_bias,
           dn_onorm_w, swa_sinks, proj_a, proj_b, proj_c, w_out, norm2_w, w_up, w_down, final_norm_w):
    f = np.float32
    A = lambda a: np.ascontiguousarray(np.asarray(a, dtype=f))
    x_prompt, x_sample = A(x_prompt), A(x_sample)
    B, SEQ, _ = x_prompt.shape
    NT = SEQ // 512
    TP = NT * 512 + 16
    assert SEQ == NT * 512
    nb = x_sample.shape[0]
    assert nb == 8 * NS and x_sample.shape[1] == TS
    if NT not in _CACHE:
        _CACHE[NT] = (build(NT), _consts(TP))
    nc, cst = _CACHE[NT]
    pvec = np.zeros((128, 96), f)
    n1, n2, fn = A(norm1_w), A(norm2_w), A(final_norm_w)
    for l in range(L):
        pvec[:, 0 + l * 8:8 + l * 8] = n1[l].reshape(8, 128).T
        pvec[:, 16 + l * 8:24 + l * 8] = n2[l].reshape(8, 128).T
        pvec[:, 40 + l * 4:44 + l * 4] = A(pool_scale)[l].reshape(4, 128).T
        pvec[:, 48 + l] = A(dn_onorm_w)[l]
    pvec[:, 32:40] = fn.reshape(8, 128).T
    pvec[:, 50:66] = (np.arange(128)[:, None] // TS == np.arange(NS)[None, :])
    pvec[:, 94] = 1.0
    pvec[:, 93] = -0.5 * np.log(128.0)
    pvec[:, 92] = 0.0
    pvec[:, 95] = 1e-6
    convw = A(dn_conv_w).reshape(L, 4, 12, 128).transpose(3, 0, 2, 1).reshape(128, L * 48)
    rowc = np.zeros((1, 64), f)
    rowc[0, 0:8] = A(dn_a_log).reshape(-1)
    rowc[0, 8:16] = A(dn_dt_bias).reshape(-1)
    rowc[0, 16:32] = A(swa_sinks).reshape(-1)
    shared = dict(w_in=A(w_in), pool_w=A(pool_w), proj_a=A(proj_a), proj_b=A(proj_b), proj_c=A(proj_c),
                  w_out=A(w_out), w_up=A(w_up), w_down=A(w_down), pvec=pvec, convw=np.ascontiguousarray(convw), rowc=rowc, **cst)
    meta = A(meta_tokens)
    sp, sc, sd = A(state_pool), A(state_conv), A(state_delta)
    ck, cv = A(cache_swa_k), A(cache_swa_v)
    in_maps = []
    for c in range(8):
        b = c % B
        s0 = c * NS
        m = dict(shared)
        m["xin"] = np.ascontiguousarray(np.concatenate([meta, x_prompt[b]], axis=0))
        m["xs"] = np.ascontiguousarray(x_sample[s0:s0 + NS].reshape(NS * TS, D))
        m["st_pool"] = np.ascontiguousarray(sp[:, s0:s0 + NS].reshape(L, NS * 15, 512))
        m["st_conv"] = np.ascontiguousarray(sc[:, s0:s0 + NS].reshape(L, NS * 3, 1536))
        m["st_delta"] = np.ascontiguousarray(sd[:, s0:s0 + NS])
        m["st_k"] = np.ascontiguousarray(ck[:, s0:s0 + NS].reshape(L, NS, 128, 128))
        m["st_v"] = np.ascontiguousarray(cv[:, s0:s0 + NS].reshape(L, NS, 128, 128))
        in_maps.append(m)
    res = run_bass_kernel_spmd(nc, in_maps, core_ids=list(range(8)))
    R = res.results
    y_prompt = np.stack([R[b]["y_p"][NMETA:] for b in range(B)])
    y_sample = np.concatenate([R[c]["y_s"].reshape(NS, TS, D) for c in range(8)])
    st = lambda k, shp: np.stack([R[b][k].reshape(shp) for b in range(B)], axis=1)
    pool_p = st("pool_p", (L, 15, 512))
    conv_p = st("conv_p", (L, 3, 1536))
    delta_p = st("delta_p", (L, 4, 128, 128))
    k_p = st("k_p", (L, 128, 2, 64))
    v_p = st("v_p", (L, 128, 2, 64))
    cat = lambda k, shp: np.concatenate([R[c][k].reshape(shp) for c in range(8)], axis=1)
    pool_s = cat("pool_s", (L, NS, 15, 512))
    conv_s = cat("conv_s", (L, NS, 3, 1536))
    delta_s = cat("delta_s", (L, NS, 4, 128, 128))
    k_s = cat("k_s", (L, NS, 128, 2, 64))
    v_s = cat("v_s", (L, NS, 128, 2, 64))
    return (y_prompt.astype(f), y_sample.astype(f), pool_p, conv_p, delta_p, k_p, v_p, pool_s, conv_s, delta_s, k_s, v_s)
```

```python
import contextlib
import numpy as np
import concourse.bass as bass
import concourse.mybir as mybir
from concourse.bass_utils import run_bass_kernel_spmd

F32 = mybir.dt.float32
BF16 = mybir.dt.bfloat16
AF = mybir.ActivationFunctionType
ALU = mybir.AluOpType
AX = mybir.AxisListType

SAME_ENGINE_SYNC = ("gpsimd", "vector", "scalar")
ENGS = ("tensor", "vector", "scalar", "gpsimd", "sync")
BIG = 30000.0


class Buf:
    __slots__ = ("name", "last_w", "readers", "aliases", "excl")

    def __init__(self, name="", excl=False):
        self.excl = excl
        self.name = name
        self.last_w = None
        self.readers = []
        self.aliases = []


class Op:
    __slots__ = ("eng", "fn", "deps", "is_dma", "signal", "count", "slot", "use", "dbg", "pos", "edeps")

    def __init__(self, eng, fn, is_dma):
        self.eng = eng
        self.fn = fn
        self.deps = []
        self.is_dma = is_dma
        self.signal = False
        self.count = 0
        self.slot = None
        self.use = 0


class _Rec:
    def __init__(self):
        self.call = None

    def __getattr__(self, name):
        def f(*a, **k):
            self.call = (name, a, k)
            return None
        return f


class Prog:
    def __init__(self, nc, n_dma_slots=8):
        self.nc = nc
        self.ops = {e: [] for e in ENGS}
        self.n_dma_slots = n_dma_slots
        self.dma_rr = {e: 0 for e in ENGS}
        self.dma_uses = {}

    def add(self, eng, fn, reads=(), writes=(), dma=False):
        rec = _Rec()
        fn(rec)
        name_, a_, k_ = rec.call
        op = Op(eng, (lambda e: getattr(e, name_)(*a_, **k_)), dma)
        op.dbg = name_
        deps = {}
        wr = []
        for b in writes:
            wr.append(b)
            wr.extend(b.aliases)
        for b in reads:
            if b.last_w is not None:
                deps[id(b.last_w)] = b.last_w
            if b.excl:
                for r in b.readers:
                    if r.eng != eng:
                        deps[id(r)] = r
        for b in wr:
            if b.last_w is not None:
                deps[id(b.last_w)] = b.last_w
            for r in b.readers:
                deps[id(r)] = r
        op.deps = list(deps.values())
        for b in reads:
            b.readers.append(op)
        for b in wr:
            b.last_w = op
            b.readers = []
        if dma:
            s = self.dma_rr[eng]
            self.dma_rr[eng] = (s + 1) % self.n_dma_slots
            key = (eng, s)
            self.dma_uses[key] = self.dma_uses.get(key, 0) + 1
            op.slot = key
            op.use = self.dma_uses[key]
        self.ops[eng].append(op)
        return op

    def pe(self, fn, reads=(), writes=()):
        return self.add("tensor", fn, reads, writes)

    def dve(self, fn, reads=(), writes=()):
        return self.add("vector", fn, reads, writes)

    def act(self, fn, reads=(), writes=()):
        return self.add("scalar", fn, reads, writes)

    def pool(self, fn, reads=(), writes=()):
        return self.add("gpsimd", fn, reads, writes)

    def dma(self, out, in_, reads=(), writes=(), eng="gpsimd", **kw):
        return self.add(eng, lambda e: e.dma_start(out=out, in_=in_, **kw), reads, writes, dma=True)

    def emit(self):
        nc = self.nc
        for e in ENGS:
            for i, op in enumerate(self.ops[e]):
                op.pos = i
        for e in ENGS:
            for op in self.ops[e]:
                last = {}
                for d in op.deps:
                    if d.is_dma:
                        continue
                    if d.eng == op.eng and d.eng not in SAME_ENGINE_SYNC:
                        continue
                    if d.eng not in last or d.pos > last[d.eng].pos:
                        last[d.eng] = d
                op.edeps = list(last.values())
                for d in op.edeps:
                    d.signal = True
        for e in ENGS:
            c = 0
            for op in self.ops[e]:
                if op.signal and not op.is_dma:
                    c += 1
                op.count = c
        with contextlib.ExitStack() as st:
            esem = {e: st.enter_context(nc.semaphore("s_" + e)) for e in ENGS}
            dsem = {}
            for key in self.dma_uses:
                dsem[key] = st.enter_context(nc.semaphore("d_%s_%d" % key))
            block = st.enter_context(nc.Block())

            def make(ename):
                ops = self.ops[ename]

                def body(eng):
                    waited = {}

                    def wait(sem, val):
                        k = id(sem)
                        if waited.get(k, 0) >= val:
                            return
                        waited[k] = val
                        eng.wait_ge(sem, val)

                    for op in ops:
                        if DBG.get("trace"):
                            print("OP", ename, op.dbg, "sig" if op.signal else "", op.count, "slot", op.slot, op.use,
                                  "deps", [(d.eng, d.dbg, (d.slot, d.use) if d.is_dma else d.count) for d in op.deps])
                        for d in op.deps:
                            if d.is_dma:
                                wait(dsem[d.slot], 16 * d.use)
                        for d in op.edeps:
                            wait(esem[d.eng], d.count)
                        if op.is_dma:
                            if op.use > 1:
                                wait(dsem[op.slot], 16 * (op.use - 1))
                            op.fn(eng).then_inc(dsem[op.slot], 16)
                        else:
                            ins = op.fn(eng)
                            if op.signal:
                                ins.then_inc(esem[ename], 1)
                    for key, uses in self.dma_uses.items():
                        if key[0] == ename:
                            wait(dsem[key], 16 * uses)
                return body

            block.tensor(make("tensor"))
            block.vector(make("vector"))
            block.scalar(make("scalar"))
            block.gpsimd(make("gpsimd"))
            block.sync(make("sync"))


D = 1024
KC = 8
DIN = 6408
DFF = 4096
L = 2
NMETA = 16
PAST = 16384
C_UA, C_QB, C_KB, C_VB, C_Z, C_B, C_SQ, C_SK, C_SV, C_GA, C_GB, C_GC = (
    0, 512, 1024, 1536, 2048, 2560, 2568, 3080, 3208, 3336, 4360, 5384)
NS = 16
TS = 8


DBG = {}


def build(NT):
    TP = NT * 512 + 16
    nc = bass.Bass("TRN2", target_bir_lowering=False)
    P = Prog(nc)

    def din(name, shape, dt=F32):
        return nc.dram_tensor(name, list(shape), dt, kind="ExternalInput").ap()

    def dout(name, shape, dt=F32):
        return nc.dram_tensor(name, list(shape), dt, kind="ExternalOutput").ap()

    def dscr(name, shape, dt=BF16):
        return nc.dram_tensor(name, list(shape), dt, kind="Internal").ap()

    xin = din("xin", [TP, D])
    xs_in = din("xs", [NS * TS, D])
    st_pool = din("st_pool", [L, NS * 15, 512])
    st_conv = din("st_conv", [L, NS * 3, 1536])
    st_delta = din("st_delta", [L, NS, 4, 128, 128])
    st_k = din("st_k", [L, NS, 128, 128])
    st_v = din("st_v", [L, NS, 128, 128])
    w_in = din("w_in", [L, D, DIN])
    pool_w = din("pool_w", [L, 4, 128, 128])
    proj = [din("proj_a", [L, 512, D]), din("proj_b", [L, 512, D]), din("proj_c", [L, 512, D])]
    w_out = din("w_out", [L, D, D])
    w_up = din("w_up", [L, D, DFF])
    w_down = din("w_down", [L, DFF, D])
    pvec = din("pvec", [128, 96])
    convw = din("convw", [128, L * 12 * 4])
    rowc = din("rowc", [1, 64])
    rope_p = din("rope_p", [TP, 64])
    rope_s = din("rope_s", [128, 64])
    cm = din("cm", [128, 6 * 128])
    cmask = din("cmask", [128, 4 * 128])
    band = din("band", [128, 3 * 256])
    smask = din("smask", [NS, 128, 256])
    selp = din("selp", [3, 128, NS * 23])
    selc = din("selc", [2, 128, NS * 11])
    invc = din("invc", [128, 4 * 16])
    dmask = din("dmask", [128, 4 * 128], BF16)

    y_p = dout("y_p", [TP, D])
    y_s = dout("y_s", [NS * TS, D])
    o_pool_p = dout("pool_p", [L, 15, 512])
    o_conv_p = dout("conv_p", [L, 3, 1536])
    o_delta_p = dout("delta_p", [L, 4, 128, 128])
    o_k_p = dout("k_p", [L, 128, 128])
    o_v_p = dout("v_p", [L, 128, 128])
    o_pool_s = dout("pool_s", [L, NS, 15, 512])
    o_conv_s = dout("conv_s", [L, NS, 3, 1536])
    o_delta_s = dout("delta_s", [L, NS, 4, 128, 128])
    o_k_s = dout("k_s", [L, NS, 128, 128])
    o_v_s = dout("v_s", [L, NS, 128, 128])

    win_b = [dscr("win_b%d" % l, [D, DIN]) for l in range(L)]
    poolw_b = [dscr("poolw_b%d" % l, [512, 128]) for l in range(L)]
    proj_b = [[dscr("proj_b%d_%d" % (l, i), [512, D]) for i in range(3)] for l in range(L)]
    wout_b = [dscr("wout_b%d" % l, [D, D]) for l in range(L)]
    wup_b = [dscr("wup_b%d" % l, [D, DFF]) for l in range(L)]
    wdown_b = [dscr("wdown_b%d" % l, [DFF, D]) for l in range(L)]
    castB = []

    with contextlib.ExitStack() as st:
        def sb(name, shape, dt):
            return st.enter_context(nc.sbuf_tensor(name, list(shape), dt))

        banks = [st.enter_context(nc.psum_tensor("pb%d" % i, [128, 512], F32)) for i in range(8)]
        bbufs = [Buf("pb%d" % i, excl=True) for i in range(8)]
        rot = {"all": 0, "d": 0, "s": 0}
        BANKSETS = {"all": [0, 1, 2, 3, 4, 5, 6], "d": [0, 1, 2, 3], "s": [4, 5, 6]}

        def bank(pool="all"):
            rot[pool] = (rot[pool] + 1) % len(BANKSETS[pool])
            i_ = BANKSETS[pool][rot[pool]]
            return banks[i_], bbufs[i_]
        LB, LBb = banks[7], bbufs[7]

        evi = [0]

        def evac(out, in_, reads, writes, func=None, scale=1.0, eng=None):
            if func is not None:
                eng = "act"
            if eng is None:
                evi[0] ^= 1
                eng = "act" if evi[0] else "dve"
                if DBG.get("evdve"):
                    eng = "dve"
                if DBG.get("evact"):
                    eng = "act"
            if eng == "act":
                f = func if func is not None else AF.Copy
                P.act(lambda e: e.activation(out=out, in_=in_, func=f, scale=scale), reads, writes)
            else:
                P.dve(lambda e: e.tensor_copy(out=out, in_=in_), reads, writes)

        pv = sb("pv", [128, 96], F32); pvB = Buf()
        cw = sb("cw", [128, L * 48], F32); cwB = Buf()
        rc = sb("rc", [128, 64], F32); rcB = Buf()
        cm32 = sb("cm32", [128, 6 * 128], F32); cm32B = Buf()
        cmk = sb("cmk", [128, 4 * 128], F32); cmkB = Buf()
        bnd = sb("bnd", [128, 3 * 256], F32); bndB = Buf()
        ivc = sb("ivc", [128, 64], F32); ivcB = Buf()
        idb = sb("idb", [128, 128], BF16); idbB = Buf()
        oneb = sb("oneb", [128, 128], BF16); onebB = Buf()
        P.dma(pv[:], pvec, writes=[pvB])
        P.dma(cw[:], convw, writes=[cwB])
        P.dma(rc[:], rowc.partition_broadcast(128), writes=[rcB])
        P.dma(cm32[:], cm, writes=[cm32B])
        P.dma(cmk[:], cmask, writes=[cmkB])
        P.dma(bnd[:], band, writes=[bndB])
        P.dma(ivc[:], invc, writes=[ivcB])

        def CM(i):
            return cm32[:, i * 128:(i + 1) * 128]
        ID32, ONE32, U_P, U_S, BM_P, BM_S = CM(0), CM(1), CM(2), CM(3), CM(4), CM(5)
        P.dve(lambda e: e.tensor_copy(out=idb[:], in_=ID32), [cm32B], [idbB])
        P.dve(lambda e: e.tensor_copy(out=oneb[:], in_=ONE32), [cm32B], [onebB])
        cmb = sb("cmb", [128, 12 * 128], BF16); cmbB = Buf()
        P.dve(lambda e: e.tensor_copy(out=cmb[:, 0:512], in_=cm32[:, 256:768]), [cm32B], [cmbB])
        P.dve(lambda e: e.tensor_copy(out=cmb[:, 512:1024], in_=cmk[:, :]), [cmkB, cmbB], [cmbB])
        P.dma(cmb[:, 1024:1536], dmask, reads=[cmbB], writes=[cmbB])
        PV_N1, PV_N2, PV_FN, PV_PS, PV_ON, PV_RM = 0, 16, 32, 40, 48, 50
        negA = sb("negA", [128, 8], F32); negAB = Buf()
        P.act(lambda e: e.activation(out=negA[:], in_=rc[:, 0:8], func=AF.Exp), [rcB], [negAB])
        P.dve(lambda e: e.tensor_scalar(out=negA[:], in0=negA[:], scalar1=-1.0, scalar2=None, op0=ALU.mult), [negAB], [negAB])

        def _cb():
            castB.append(Buf())
            return castB[-1]

        cast_done = {}
        for l in range(L):
            for r in range(0, D, 128):
                P.dma(win_b[l][r:r + 128, :], w_in[l, r:r + 128, :], writes=[_cb()])
            cast_done[("win", l)] = castB[-8:]
            for i in range(3):
                for r in range(0, 512, 128):
                    P.dma(proj_b[l][i][r:r + 128, :], proj[i][l, r:r + 128, :], writes=[_cb()])
            P.dma(poolw_b[l], pool_w[l].rearrange("g c d -> (g c) d"), writes=[_cb()])
            cast_done[("proj", l)] = castB[-8:]
            for r in range(0, D, 128):
                P.dma(wout_b[l][r:r + 128, :], w_out[l, r:r + 128, :], writes=[_cb()])
            cast_done[("wout", l)] = castB[-8:]
            for r in range(0, D, 128):
                P.dma(wup_b[l][r:r + 128, :], w_up[l, r:r + 128, :], writes=[_cb()])
            cast_done[("wup", l)] = castB[-8:]
            for r in range(0, DFF, 128):
                P.dma(wdown_b[l][r:r + 128, :], w_down[l, r:r + 128, :], writes=[_cb()])
            cast_done[("wdown", l)] = castB[-8:]

        def cast_dep(name, l):
            if name.startswith("proj") or name == "poolw":
                return cast_done[("proj", l)]
            if name.startswith("out"):
                return cast_done[("wout", l)]
            if name.startswith("up"):
                return cast_done[("wup", l)]
            if name.startswith("dn"):
                return cast_done[("wdown", l)]
            return cast_done[("win", l)]

        NWB = 3
        wbufs = [sb("wb%d" % i, [128, 8, 512], BF16) for i in range(NWB)]
        wbB = [Buf("wb%d" % i) for i in range(NWB)]
        wsm = [sb("wsm%d" % i, [128, 8, 8], BF16) for i in range(2)]
        wsmB = [Buf() for i in range(2)]

        def layer_specs(l):
            s = []
            W = win_b[l].rearrange("(kc p) m -> p kc m", p=128)
            for c0 in (C_UA, C_QB, C_KB, C_VB, C_Z):
                s.append(("in%d" % c0, W[:, :, c0:c0 + 512], (8, 512)))
            s.append(("ba", W[:, :, C_B:C_B + 8], (8, 8)))
            s.append(("sq", W[:, :, C_SQ:C_SQ + 512], (8, 512)))
            s.append(("skv", W[:, :, C_SK:C_SK + 256], (8, 256)))
            s.append(("poolw", poolw_b[l].rearrange("(g c) d -> c g d", c=128), (4, 128)))
            for i, c0 in enumerate((C_GA, C_GB, C_GC)):
                s.append(("proj%d" % i, proj_b[l][i].rearrange("(kc p) m -> p kc m", p=128), (4, 1024)))
                s.append(("g%d_0" % i, W[:, :, c0:c0 + 512], (8, 512)))
                s.append(("g%d_1" % i, W[:, :, c0 + 512:c0 + 1024], (8, 512)))
            Wo = wout_b[l].rearrange("(kc p) m -> p kc m", p=128)
            s.append(("out0", Wo[:, :, 0:512], (8, 512)))
            s.append(("out1", Wo[:, :, 512:1024], (8, 512)))
            Wu = wup_b[l].rearrange("(kc p) m -> p kc m", p=128)
            Wd = wdown_b[l].rearrange("(kc p) m -> p kc m", p=128)
            for half in range(2):
                for j in range(4):
                    c0 = half * 2048 + j * 512
                    s.append(("up%d" % (half * 4 + j), Wu[:, :, c0:c0 + 512], (8, 512)))
                for ch in range(2):
                    for kg in range(2):
                        k0 = half * 16 + kg * 8
                        s.append(("dn%d_%d_%d" % (half, kg, ch), Wd[:, k0:k0 + 8, ch * 512:(ch + 1) * 512], (8, 512)))
            return s

        n_tl = (NT + 2)
        specs = []
        spec_layer = []
        for t in range(n_tl):
            for l in range(L):
                ls_ = layer_specs(l)
                specs.extend(ls_)
                spec_layer.extend([l] * len(ls_))
        HOLD = {"sq": 1, "proj0": 2, "proj1": 2, "proj2": 2}
        wstate = {"issued": 0, "next": 0, "big": 0, "sm": 0, "loc": {}, "occ": [None] * NWB}

        def w_can_issue(j, i):
            if specs[j][0] == "ba":
                return True
            p = wstate["occ"][wstate["big"] % NWB]
            return p is None or p + HOLD.get(specs[p][0], 0) < i

        def w_issue(i):
            name, src, (a, b) = specs[i]
            if name == "ba":
                k = wstate["sm"] % 2
                wstate["sm"] += 1
                dst, bf = wsm[k][:, 0:a, 0:b], wsmB[k]
                view = wsm[k]
            else:
                k = wstate["big"] % NWB
                wstate["big"] += 1
                wstate["occ"][k] = i
                flat = wbufs[k][:].rearrange("p a b -> p (a b)")
                view = flat[:, 0:a * b].rearrange("p (a b) -> p a b", a=a)
                dst, bf = view, wbB[k]
            P.dma(dst, src, reads=cast_dep(name, spec_layer[i]), writes=[bf], eng="sync")
            wstate["loc"][i] = (view, bf)

        def wget(name):
            i = wstate["next"]
            assert specs[i][0] == name, (specs[i][0], name)
            while wstate["issued"] < min(len(specs), i + NWB) and w_can_issue(wstate["issued"], i):
                w_issue(wstate["issued"])
                wstate["issued"] += 1
            assert wstate["issued"] > i, ("weight buffer deadlock", name, i)
            wstate["next"] += 1
            return wstate["loc"].pop(i)

        xT = sb("xT", [128, 8, 512], F32); xTB = [Buf("xT%d" % c) for c in range(8)]
        xn = sb("xn", [128, 8, 512], BF16); xnB = [Buf("xn%d" % c) for c in range(8)]
        xtm0 = sb("xtm0", [128, 1024], F32); xtm = [xtm0, xtm0]; _xb = Buf(); xtmB = [_xb, _xb]
        sq = [sb("sq%d" % i, [128, 512], BF16) for i in range(2)]; sqB = [Buf(), Buf()]
        rstd = sb("rstd", [128, 512], F32); rstdB = Buf()
        R1 = sb("R1", [128, 8320], F32)
        exta = R1[:, 0:4 * 528].rearrange("p (g m) -> p g m", g=4)
        qkvp = R1[:, 2112:2112 + 12 * 516].rearrange("p (g m) -> p g m", g=12)
        macc = R1[:, 0:4096].rearrange("p (g m) -> p g m", g=8)
        upb = R1[:, 4096:8192].bitcast(BF16).rearrange("p (g m) -> p g m", g=16)
        extaB = [Buf("exta%d" % g) for g in range(4)]
        qkvpB = [Buf("qkvp%d" % g) for g in range(12)]
        maccB = [Buf("macc%d" % g) for g in range(8)]
        upB = [Buf("up%d" % g) for g in range(16)]
        for a in extaB + qkvpB:
            for b in maccB + upB:
                a.aliases.append(b)
                b.aliases.append(a)
        oT = R1[:, 0:2048].rearrange("p (g m) -> p g m", g=4); oTB = [Buf() for g in range(4)]
        for a in oTB:
            for b in extaB + maccB[0:4]:
                a.aliases.append(b)
                b.aliases.append(a)
        pscr = [sb("pscr%d" % i, [128, 528], F32) for i in range(2)]; pscrB = [Buf(), Buf()]
        dpool = sb("dpool", [128, 4, 512], BF16); dpoolB = [Buf() for g in range(4)]
        oa = sb("oa", [128, 4, 512], BF16); oaB = [Buf() for g in range(4)]
        ob = sb("ob", [128, 4, 512], BF16); obB = [Buf() for g in range(4)]
        oc = sb("oc", [128, 4, 512], BF16); ocB = [Buf() for g in range(4)]
        zs = sb("zs", [128, 4, 512], BF16); zsB = [Buf() for g in range(4)]
        cvo = sb("cvo", [128, 512], F32); cvoB = Buf()
        R3 = sb("R3", [128, 8, 512], BF16)
        qn = R3[:, 0:4, :]; qnB = [Buf() for g in range(4)]
        kn = R3[:, 4:8, :]; knB = [Buf() for g in range(4)]
        vf = sb("vf", [128, 4, 512], BF16); vfB = [Buf() for g in range(4)]
        sil = sb("sil", [128, 512], F32); silB = Buf()
        mbf = R3; mbfB = [Buf() for g in range(8)]
        for g in range(4):
            for a, b in ((qnB[g], mbfB[g]), (knB[g], mbfB[4 + g])):
                a.aliases.append(b)
                b.aliases.append(a)
        qtm = sb("qtm", [128, 512], F32); qtmB = Buf()
        kvtm = sb("kvtm", [128, 256], F32); kvtmB = Buf()
        rp = sb("rp", [128, 64], F32); rpB = Buf()
        rt = [sb("rt%d" % i, [128, 256], F32) for i in range(2)]; rtB = [Buf(), Buf()]
        qr = sb("qr", [128, 512], BF16); qrB = Buf()
        kr32 = sb("kr32", [128, 128], F32); kr32B = Buf()
        krb = sb("krb", [128, 128], BF16); krbB = Buf()
        qTt = sb("qTt", [64, 8, 128], BF16); qTtB = Buf()
        kTc = sb("kTc", [64, 2, 256], BF16); kTcB = Buf()
        vtm = sb("vtm", [128, 2, 128], BF16); vtmB = [Buf(), Buf()]
        R2 = sb("R2", [128, 2048], F32)
        smx = R2[:, :].rearrange("p (h m) -> p h m", h=8); smxB = Buf()
        sg = [R2[:, 0:512], R2[:, 512:1024]]; sgB = [Buf(), Buf()]
        rl = [R2[:, 1024:1536], R2[:, 1536:2048]]; rlB = [Buf(), Buf()]
        for b in sgB + rlB:
            b.aliases.append(smxB)
            smxB.aliases.append(b)
        R4 = sb("R4", [128, 4096], BF16)
        pexp = R4[:, 0:2048].rearrange("p (h m) -> p h m", h=8); pexpB = Buf()
        pT = R4[:, 2048:4096].rearrange("p (a m) -> p a m", a=16); pTB = Buf()
        T32 = R1[:, 2304:2816].rearrange("p (h m) -> p h m", h=4); T32B = Buf()
        Tm = R1[:, 2816:3072].bitcast(BF16).rearrange("p (h m) -> p h m", h=4); TmB = Buf()
        P1a = R1[:, 3072:3328].bitcast(BF16).rearrange("p (h m) -> p h m", h=4); P1aB = Buf()
        P1b = R1[:, 3328:3584].bitcast(BF16).rearrange("p (h m) -> p h m", h=4); P1bB = Buf()
        Nf = R1[:, 3584:3840].bitcast(BF16).rearrange("p (h m) -> p h m", h=4); NfB = Buf()
        NTf = R1[:, 3840:4096].bitcast(BF16).rearrange("p (h m) -> p h m", h=4); NTfB = Buf()
        for a in (T32B, TmB, P1aB, P1bB, NfB, NTfB):
            for b in qkvpB + maccB[4:8]:
                a.aliases.append(b)
                b.aliases.append(a)
        st8 = sb("st8", [128, 64], F32); st8B = Buf()
        otm = sb("otm", [128, 512], BF16); otmB = Buf()
        poolh = sb("poolh", [128, L, 4, 16], F32); poolhB = [Buf() for l in range(L)]
        convh = sb("convh", [128, L, 12, 4], F32); convhB = [Buf() for l in range(L)]
        S32 = sb("S32", [128, L, 4, 128], F32); S32B = [Buf() for l in range(L)]
        Sbf = sb("Sbf", [128, 4, 128], BF16); SbfB = Buf()
        kTh = sb("kTh", [64, L, 2, 128], BF16); kThB = [Buf() for l in range(L)]
        vh = sb("vh", [128, L, 128], BF16); vhB = [Buf() for l in range(L)]
        bg = sb("bg", [128, 16], F32); bgB = Buf()
        gst = sb("gst", [128, 32], F32); gstB = Buf()
        Ug = sb("Ug", [128, 12, 128], BF16); UgB = Buf()
        g3 = sb("g3", [128, 12], BF16); g3B = Buf()
        dlo = sb("dlo", [128, 4, 128], F32); dloB = Buf()
        dup = sb("dup", [128, 4, 128], F32); dupB = Buf()
        egb = sb("egb", [128, 4, 128], F32); egbB = Buf()
        Qm = [sb("Qm%d" % i, [128, 4, 128], BF16) for i in range(2)]; QmB = [Buf(), Buf()]
        Rm = [sb("Rm%d" % i, [128, 4, 128], BF16) for i in range(2)]; RmB = [Buf(), Buf()]
        Ym = sb("Ym", [128, 4, 128], BF16); YmB = Buf()
        Y32 = sb("Y32", [128, 4, 128], F32); Y32B = Buf()
        qkT = sb("qkT", [128, 4, 128], BF16); qkTB = Buf()
        qgT = sb("qgT", [128, 4, 128], BF16); qgTB = Buf()
        kbg = sb("kbg", [128, 4, 128], BF16); kbgB = Buf()
        kgm = sb("kgm", [128, 4, 128], BF16); kgmB = Buf()
        kgs = sb("kgs", [128, 4, 128], BF16); kgsB = Buf()
        vb = sb("vb", [128, 4, 128], BF16); vbB = Buf()
        u32 = sb("u32", [128, 4, 128], F32); u32B = Buf()
        wT = sb("wT", [128, 4, 128], BF16); wTB = Buf()
        dlt = sb("dlt", [128, 4, 128], BF16); dltB = Buf()
        Sld = [S32[:, 0], S32[:, 1]]; SldB = S32B
        utm = xtm0[:, 0:512]; utmB = _xb
        hb = [sb("hb%d" % i, [128, 512], F32) for i in range(2)]; hbB = [Buf(), Buf()]
        msk = [hb[0][:, 0:256]] * 2; mskB = [hbB[0]] * 2
        ctm = [hb[1][:, 0:256]] * 2; ctmB = [hbB[1]] * 2
        ctb = sb("ctb", [128, 256], BF16); ctbB = Buf()
        ytm = xtm0; ytmB = _xb
        pfix = sb("pfix", [128, 16], F32); pfixB = Buf()
        cvo2 = xtm0[:, 0:512]; cvo2B = Buf()
        sil2 = xtm0[:, 512:1024]; sil2B = Buf()
        rstd2 = hb[0][:, 0:512]; rstd2B = Buf()
        for a, b in ((cvo2B, _xb), (sil2B, _xb), (rstd2B, hbB[0])):
            a.aliases.append(b)
            b.aliases.append(a)
        nul = Buf("outs")
        if DBG.get("mem"):
            print("SBUF remaining", nc.sbuf_bytes_remaining)

        dumps = {}

        def dump(name, ap, bufs, dt):
            if not DBG.get("dump") or name in dumps:
                return
            shp = list(ap.shape)
            d = nc.dram_tensor("dbg_" + name, shp, dt, kind="ExternalOutput").ap()
            dumps[name] = d
            P.dma(d, ap, reads=bufs, writes=[nul])

        def rms_stats(src_fn, src_bufs, nch, n, scale, eps_bias, pool="all", sqk=None, rs=None, rsB=None):
            pb, pbB = bank(pool)
            rs_, rsB_ = (rstd, rstdB) if rs is None else (rs, rsB)
            for c in range(nch):
                k = c % 2 if sqk is None else sqk
                P.act(lambda e, c=c, k=k: e.activation(out=sq[k][:, 0:n], in_=src_fn(c), func=AF.Square),
                      [src_bufs[c]], [sqB[k]])
                P.pe(lambda e, c=c, k=k: e.matmul(pb[:, 0:n], lhsT=oneb[:], rhs=sq[k][:, 0:n],
                                                 start=(c == 0), stop=(c == nch - 1)), [sqB[k], onebB], [pbB])
            P.act(lambda e: e.activation(out=rs_[:, 0:n], in_=pb[:, 0:n], func=AF.Ln, bias=pv[:, 95:96], scale=scale),
                  [pbB, pvB], [rsB_])
            P.act(lambda e: e.activation(out=rs_[:, 0:n], in_=rs_[:, 0:n], func=AF.Exp, scale=-0.5, bias=pv[:, eps_bias:eps_bias + 1]),
                  [rsB_, pvB], [rsB_])

        def rmsnorm_fm(n, wcol0):
            rms_stats(lambda c: xT[:, c, 0:n], xTB, 8, n, 1.0 / D, 92)
            for c in range(8):
                P.dve(lambda e, c=c: e.scalar_tensor_tensor(out=xn[:, c, 0:n], in0=xT[:, c, 0:n],
                                                            scalar=pv[:, wcol0 + c:wcol0 + c + 1], in1=rstd[:, 0:n],
                                                            op0=ALU.mult, op1=ALU.mult),
                      [xTB[c], rstdB, pvB], [xnB[c]])

        def fm_group(wv, wB, mc_list, n, kcs, rhs_fn, rhs_bufs, consume):
            for mi, m0 in enumerate(mc_list):
                pb, pbB = bank()
                for j, kc in enumerate(kcs):
                    P.pe(lambda e, kc=kc, j=j, m0=m0: e.matmul(pb[:, 0:n], lhsT=wv[:, j, m0:m0 + 128], rhs=rhs_fn(kc),
                                                              start=(j == 0), stop=(j == len(kcs) - 1)),
                         [wB, rhs_bufs[kc]], [pbB])
                consume(mi, pb, pbB)

        def tm_group(wv, wB, ncols, r0, nr, consume):
            pb, pbB = bank()
            for kc in range(8):
                P.pe(lambda e, kc=kc: e.matmul(pb[0:nr, 0:ncols], lhsT=xn[:, kc, r0:r0 + nr], rhs=wv[:, kc, 0:ncols],
                                               start=(kc == 0), stop=(kc == 7)), [wB, xnB[kc]], [pbB])
            consume(pb, pbB)

        def tile_layer(l, mode, n, pos0, first, last):
            S, T = (1, n) if mode == "p" else (NS, TS)
            HP, HC = (16, 4) if mode == "p" else (15, 3)
            blocks = [(o, min(128, n - o)) for o in range(0, n, 128)]
            rmsnorm_fm(n, PV_N1 + l * 8)

            def ext_view(base, g, H):
                return base[:, g, 0:S * (H + T)].rearrange("p (s m) -> p s m", s=S)

            need_tm = (mode == "s") or last
            for gi, c0 in enumerate((C_UA, C_QB, C_KB, C_VB)):
                wv, wB = wget("in%d" % c0)
                if mode == "p":
                    def cons(mi, pb, pbB, gi=gi):
                        if gi == 0:
                            evac(exta[:, mi, HP:HP + n], pb[:, 0:n], [pbB], [extaB[mi]])
                        else:
                            g = (gi - 1) * 4 + mi
                            evac(qkvp[:, g, HC:HC + n], pb[:, 0:n], [pbB], [qkvpB[g]])
                    fm_group(wv, wB, [0, 128, 256, 384], n, list(range(8)), lambda kc: xn[:, kc, 0:n], xnB, cons)
                if need_tm:
                    def cons2(pb, pbB, gi=gi):
                        evac(utm[0:n, :], pb[0:n, :], [pbB], [utmB])
                    tm_group(wv, wB, 512, 0, n, cons2)
                    if mode == "s":
                        for s_ in range(NS):
                            if gi == 0:
                                P.dma(o_pool_s[l, s_, 7:15, :], utm[s_ * TS:(s_ + 1) * TS, :], reads=[utmB], writes=[nul])
                            else:
                                P.dma(o_conv_s[l, s_, :, (gi - 1) * 512:gi * 512], utm[s_ * TS + 5:(s_ + 1) * TS, :], reads=[utmB], writes=[nul])
                        if gi == 0:
                            P.dma(o_pool_s[l, :, 0:7, :], st_pool[l].rearrange("(s t) c -> s t c", t=15)[:, 8:15, :], writes=[nul])
                            for i in range(2):
                                P.dma(hb[i][0:120, :], st_pool[l, i * 120:(i + 1) * 120, :], writes=[hbB[i]])
                            for g in range(4):
                                pb, pbB = bank()
                                P.pe(lambda e, g=g, pb=pb: e.transpose(pb[:, 0:128], utm[:, g * 128:(g + 1) * 128], ID32), [utmB, cm32B], [pbB])
                                for i in range(2):
                                    P.pe(lambda e, g=g, pb=pb, i=i: e.transpose(pb[:, 128 + i * 128:256 + i * 128], hb[i][:, g * 128:(g + 1) * 128], ID32), [hbB[i], cm32B], [pbB])
                                ev = exta[:, g, 0:368].rearrange("p (s m) -> p s m", s=NS)
                                evac(ev[:, :, 15:23], pb[:, 0:128].rearrange("p (s t) -> p s t", t=TS), [pbB], [extaB[g]])
                                for i in range(2):
                                    evac(ev[:, i * 8:(i + 1) * 8, 0:15], pb[:, 128 + i * 128:128 + i * 128 + 120].rearrange("p (s t) -> p s t", t=15), [pbB], [extaB[g]])
                        else:
                            hk = gi % 2
                            P.dma(hb[hk][0:48, :], st_conv[l, :, (gi - 1) * 512:gi * 512], writes=[hbB[hk]])
                            for g4 in range(4):
                                g = (gi - 1) * 4 + g4
                                pb, pbB = bank()
                                P.pe(lambda e, g4=g4, pb=pb: e.transpose(pb[:, 0:128], utm[:, g4 * 128:(g4 + 1) * 128], ID32), [utmB, cm32B], [pbB])
                                P.pe(lambda e, g4=g4, pb=pb, hk=hk: e.transpose(pb[:, 128:192], hb[hk][0:64, g4 * 128:(g4 + 1) * 128], ID32[0:64, 0:64]), [hbB[hk], cm32B], [pbB])
                                ev = qkvp[:, g, 0:176].rearrange("p (s m) -> p s m", s=NS)
                                evac(ev[:, :, 3:11], pb[:, 0:128].rearrange("p (s t) -> p s t", t=TS), [pbB], [qkvpB[g]])
                                evac(ev[:, :, 0:3], pb[:, 128:176].rearrange("p (s t) -> p s t", t=3), [pbB], [qkvpB[g]])
                    elif last:
                        if gi == 0:
                            P.dma(o_pool_p[l], utm[1:16, :], reads=[utmB], writes=[nul])
                        else:
                            P.dma(o_conv_p[l, :, (gi - 1) * 512:gi * 512], utm[13:16, :], reads=[utmB], writes=[nul])
            if mode == "p":
                for g in range(4):
                    P.pool(lambda e, g=g: e.tensor_copy(out=exta[:, g, 0:16], in_=poolh[:, l, g, :]), [poolhB[l]], [extaB[g]])
                for g in range(12):
                    P.pool(lambda e, g=g: e.tensor_copy(out=qkvp[:, g, 0:4], in_=convh[:, l, g, :]), [convhB[l]], [qkvpB[g]])

            wv, wB = wget("in%d" % C_Z)

            def consz(mi, pb, pbB):
                evac(zs[:, mi, 0:n], pb[:, 0:n], [pbB], [zsB[mi]], func=AF.Silu)
            fm_group(wv, wB, [0, 128, 256, 384], n, list(range(8)), lambda kc: xn[:, kc, 0:n], xnB, consz)
            wba, wbaB = wget("ba")
            wsq, wsqB = wget("sq")
            wkv, wkvB = wget("skv")


            def pool_chain():
                wpv, wpB = None, None
                for g in range(4):
                    e_ = ext_view(exta, g, HP)
                    W = HP + T
                    cur, curB = e_, extaB[g]
                    for k in range(g + 1):
                        sh = 1 << k
                        o_ = pscr[k % 2][:, 0:S * W].rearrange("p (s m) -> p s m", s=S)
                        P.pool(lambda e, cur=cur, o_=o_, sh=sh, W=W: e.tensor_tensor(out=o_[:, :, sh:W], in0=cur[:, :, sh:W],
                                                                                  in1=cur[:, :, 0:W - sh], op=ALU.add),
                               [curB], [pscrB[k % 2]])
                        yield
                        cur, curB = o_, pscrB[k % 2]
                    w_ = 2 << g
                    dv_ = dpool[:, g, 0:n].rearrange("p (s m) -> p s m", s=S)
                    P.dve(lambda e, cur=cur, e_=e_, dv_=dv_, w_=w_: e.scalar_tensor_tensor(
                        out=dv_, in0=cur[:, :, HP:HP + T], scalar=1.0 / w_, in1=e_[:, :, HP:HP + T],
                        op0=ALU.mult, op1=ALU.subtract), [curB, extaB[g]], [dpoolB[g]])
                    yield
                    if mode == "p" and first:
                        P.dve(lambda e, cur=cur, g=g: e.tensor_tensor(out=pfix[:, 0:16], in0=cur[:, 0, HP:HP + 16],
                                                                      in1=ivc[:, g * 16:(g + 1) * 16], op=ALU.mult),
                              [curB, ivcB], [pfixB])
                        yield
                        P.dve(lambda e, g=g: e.tensor_tensor(out=dpool[:, g, 0:16], in0=pfix[:, 0:16], in1=exta[:, g, HP:HP + 16],
                                                             op=ALU.subtract), [pfixB, extaB[g]], [dpoolB[g]])
                        yield
                    if mode == "p" and not last:
                        P.pool(lambda e, g=g: e.tensor_copy(out=poolh[:, l, g, :], in_=exta[:, g, n:n + 16]), [extaB[g]], [poolhB[l]])
                        yield

            def conv_prologue(gs, cvo, cvoB, sil, silB, rstd, rstdB, sqk):
                for g in gs:
                    e_ = ext_view(qkvp, g, HC)
                    cv = cvo[:, 0:n].rearrange("p (s m) -> p s m", s=S)
                    wc0 = l * 48 + g * 4
                    off = 4 - HC if mode == "p" else 0
                    j0 = 1 if mode == "p" else 0
                    P.act(lambda e, e_=e_, cv=cv, wc0=wc0, j0=j0: e.activation(out=cv, in_=e_[:, :, j0:j0 + T], func=AF.Copy, scale=cw[:, wc0:wc0 + 1]),
                          [qkvpB[g], cwB], [cvoB])
                    yield
                    for j in range(1, 4):
                        P.dve(lambda e, e_=e_, cv=cv, wc0=wc0, j=j, j0=j0: e.scalar_tensor_tensor(
                            out=cv, in0=e_[:, :, j0 + j:j0 + j + T], scalar=cw[:, wc0 + j:wc0 + j + 1], in1=cv,
                            op0=ALU.mult, op1=ALU.add), [qkvpB[g], cwB, cvoB], [cvoB])
                        yield
                    if mode == "p" and not last:
                        P.pool(lambda e, g=g: e.tensor_copy(out=convh[:, l, g, :], in_=qkvp[:, g, n:n + 4]), [qkvpB[g]], [convhB[l]])
                        yield
                    h = g % 4
                    if g < 8:
                        dst, dstB = (qn, qnB) if g < 4 else (kn, knB)
                        P.act(lambda e: e.activation(out=sil[:, 0:n], in_=cvo[:, 0:n], func=AF.Silu), [cvoB], [silB])
                        yield
                        rms_stats(lambda c: sil[:, 0:n], [silB], 1, n, 1.0, 93 if g < 4 else 92, pool="d", sqk=sqk, rs=rstd, rsB=rstdB)
                        yield
                        P.dve(lambda e, dst=dst, h=h: e.tensor_tensor(out=dst[:, h, 0:n], in0=sil[:, 0:n], in1=rstd[:, 0:n], op=ALU.mult),
                              [silB, rstdB], [dstB[h]])
                        yield
                    else:
                        P.act(lambda e, h=h: e.activation(out=vf[:, h, 0:n], in_=cvo[:, 0:n], func=AF.Silu), [cvoB], [vfB[h]])
                        yield

            if mode == "p":
                P.act(lambda e: e.copy(out=Sbf[:], in_=S32[:, l]), [S32B[l]], [SbfB])
                P.pool(lambda e: e.tensor_copy(out=kTc[:, :, 0:128], in_=kTh[:, l]), [kThB[l]], [kTcB])
                P.pool(lambda e: e.tensor_copy(out=vtm[:, 0, :], in_=vh[:, l]), [vhB[l]], [vtmB[0]])
            def delta_chain():
                if mode == "p" and DBG.get("conv2"):
                    subs = [conv_prologue(range(0, 12, 2), cvo, cvoB, sil, silB, rstd, rstdB, 0),
                            conv_prologue(range(1, 12, 2), cvo2, cvo2B, sil2, sil2B, rstd2, rstd2B, 1)]
                else:
                    subs = [conv_prologue(range(12), cvo, cvoB, sil, silB, rstd, rstdB, 0)]
                while subs:
                    for g_ in list(subs):
                        try:
                            next(g_)
                            yield
                        except StopIteration:
                            subs.remove(g_)
                for bi, (b0, bn) in enumerate(blocks):
                    smp = mode == "s"
                    U_, BM_ = (U_S, BM_S) if smp else (U_P, BM_P)
                    A_lo = cmk[:, (2 if smp else 0) * 128:(3 if smp else 1) * 128]
                    A_up = cmk[:, (3 if smp else 1) * 128:(4 if smp else 2) * 128]
                    nlev = 3 if smp else (4 if bn == 16 else 6)
                    def consba(pb, pbB):
                        P.act(lambda e: e.activation(out=bg[0:bn, 0:4], in_=pb[0:bn, 0:4], func=AF.Sigmoid), [pbB], [bgB])
                        P.dve(lambda e: e.tensor_tensor(out=bg[0:bn, 12:16], in0=pb[0:bn, 4:8], in1=rc[0:bn, 8 + l * 4:12 + l * 4], op=ALU.add),
                              [pbB, rcB], [bgB])
                    pbx, pbxB = bank("d")
                    for kc in range(8):
                        P.pe(lambda e, kc=kc: e.matmul(pbx[0:bn, 0:8], lhsT=xn[:, kc, b0:b0 + bn], rhs=wba[:, kc, 0:8],
                                                       start=(kc == 0), stop=(kc == 7)), [wbaB, xnB[kc]], [pbxB])
                        yield
                    consba(pbx, pbxB)
                    yield
                    P.act(lambda e: e.activation(out=bg[0:bn, 12:16], in_=bg[0:bn, 12:16], func=AF.Exp), [bgB], [bgB])
                    yield
                    P.act(lambda e: e.activation(out=bg[0:bn, 12:16], in_=bg[0:bn, 12:16], func=AF.Ln, bias=pv[0:bn, 94:95]), [bgB, pvB], [bgB])
                    yield
                    P.dve(lambda e: e.tensor_tensor(out=bg[0:bn, 4:8], in0=bg[0:bn, 12:16], in1=negA[0:bn, l * 4:l * 4 + 4], op=ALU.mult),
                          [bgB, negAB], [bgB])
                    yield
                    P.dve(lambda e: e.tensor_scalar(out=bg[0:bn, 8:12], in0=bg[0:bn, 0:4], scalar1=-1.0, scalar2=None, op0=ALU.mult), [bgB], [bgB])
                    yield
                    P.dve(lambda e: e.tensor_copy(out=g3[0:bn, 0:4], in_=bg[0:bn, 4:8]), [bgB], [g3B])
                    yield
                    P.dve(lambda e: e.tensor_tensor(out=bg[0:bn, 12:16], in0=bg[0:bn, 4:8], in1=g3[0:bn, 0:4], op=ALU.subtract), [bgB, g3B], [bgB])
                    yield
                    P.dve(lambda e: e.tensor_copy(out=g3[0:bn, 4:8], in_=bg[0:bn, 12:16]), [bgB], [g3B])
                    yield
                    P.dve(lambda e: e.tensor_tensor(out=bg[0:bn, 12:16], in0=bg[0:bn, 12:16], in1=g3[0:bn, 4:8], op=ALU.subtract), [bgB, g3B], [bgB])
                    yield
                    P.dve(lambda e: e.tensor_copy(out=g3[0:bn, 8:12], in_=bg[0:bn, 12:16]), [bgB], [g3B])
                    yield
                    Ub, BMb = cmb[:, (1 if smp else 0) * 128:(2 if smp else 1) * 128], cmb[:, (3 if smp else 2) * 128:(4 if smp else 3) * 128]
                    Alo_b, Aup_b = cmb[:, (6 if smp else 4) * 128:(7 if smp else 5) * 128], cmb[:, (7 if smp else 5) * 128:(8 if smp else 6) * 128]
                    P.dve(lambda e: e.tensor_tensor(out=Ug[0:bn, :, 0:bn], in0=Ub[0:bn, 0:bn].unsqueeze(1).to_broadcast([bn, 12, bn]),
                                                    in1=g3[0:bn, 0:12].unsqueeze(2).to_broadcast([bn, 12, bn]), op=ALU.mult),
                          [cmbB, g3B], [UgB])
                    yield
                    plo, ploB = bank("d")
                    pup, pupB = bank("d")
                    ppl, pplB = bank("d")
                    pg, pgB = bank("d")
                    for (pb_, pbB_, A_) in ((plo, ploB, Alo_b), (pup, pupB, Aup_b), (ppl, pplB, None)):
                        for h in range(4):
                            mo = 128 if A_ is None else bn
                            for q3 in range(3):
                                P.pe(lambda e, pb_=pb_, h=h, A_=A_, mo=mo, q3=q3: e.matmul(pb_[0:mo, h * 128:h * 128 + bn], lhsT=oneb[0:bn, 0:mo], rhs=Ug[0:bn, q3 * 4 + h, 0:bn],
                                                                                       start=(q3 == 0), stop=(A_ is None and q3 == 2)), [onebB, UgB], [pbB_])
                                yield
                            if A_ is not None:
                                P.pe(lambda e, pb_=pb_, h=h, A_=A_: e.matmul(pb_[0:bn, h * 128:h * 128 + bn], lhsT=idb[0:bn, 0:bn], rhs=A_[0:bn, 0:bn],
                                                                           start=False, stop=True), [idbB, cmbB], [pbB_])
                                yield
                    for q3 in range(3):
                        P.pe(lambda e, q3=q3: e.matmul(pg[0:bn, 0:4], lhsT=Ub[0:bn, 0:bn], rhs=g3[0:bn, q3 * 4:q3 * 4 + 4], start=(q3 == 0), stop=(q3 == 2)), [cmbB, g3B], [pgB])
                        yield
                    for q3 in range(3):
                        P.pe(lambda e, q3=q3: e.matmul(pg[0:bn, 4:8], lhsT=BMb[0:bn, 0:bn], rhs=g3[0:bn, q3 * 4:q3 * 4 + 4], start=(q3 == 0), stop=(q3 == 2)), [cmbB, g3B], [pgB])
                        yield
                    P.dve(lambda e: e.tensor_copy(out=gst[0:bn, 0:8], in_=pg[0:bn, 0:8]), [pgB], [gstB])
                    yield
                    P.dve(lambda e: e.tensor_scalar(out=gst[0:bn, 16:20], in0=gst[0:bn, 0:4], scalar1=-1.0, scalar2=None, op0=ALU.mult), [gstB], [gstB])
                    yield
                    P.dve(lambda e: e.tensor_tensor(out=gst[0:bn, 12:16], in0=gst[0:bn, 4:8], in1=gst[0:bn, 0:4], op=ALU.subtract), [gstB], [gstB])
                    yield
                    P.act(lambda e: e.activation(out=gst[0:bn, 8:12], in_=gst[0:bn, 0:4], func=AF.Exp), [gstB], [gstB])
                    yield
                    P.act(lambda e: e.activation(out=gst[0:bn, 12:16], in_=gst[0:bn, 12:16], func=AF.Exp), [gstB], [gstB])
                    yield
                    P.dve(lambda e: e.tensor_tensor(out=gst[0:bn, 8:12], in0=gst[0:bn, 8:12], in1=bg[0:bn, 0:4], op=ALU.mult), [gstB, bgB], [gstB])
                    yield
                    for h in range(4):
                        P.act(lambda e, h=h: e.activation(out=dlo[0:bn, h, 0:bn], in_=plo[0:bn, h * 128:h * 128 + bn], func=AF.Exp,
                                                          bias=gst[0:bn, h:h + 1], scale=-1.0), [ploB, gstB], [dloB])
                        yield
                        P.act(lambda e, h=h: e.activation(out=dup[0:bn, h, 0:bn], in_=pup[0:bn, h * 128:h * 128 + bn], func=AF.Exp,
                                                          bias=gst[0:bn, 16 + h:17 + h], scale=1.0), [pupB, gstB], [dupB])
                        yield
                        P.act(lambda e, h=h: e.activation(out=egb[:, h, 0:bn], in_=ppl[:, h * 128:h * 128 + bn], func=AF.Exp),
                              [pplB], [egbB])
                        yield
                    ptk, ptkB = bank("d")
                    ptkb = ptk[:].bitcast(BF16)
                    for h in range(4):
                        P.pe(lambda e, h=h: e.transpose(ptkb[0:bn, h * 128:(h + 1) * 128], kn[:, h, b0:b0 + bn], idb[:]), [knB[h], idbB], [ptkB])
                        yield
                        P.pe(lambda e, h=h: e.transpose(ptkb[0:bn, 512 + h * 128:512 + (h + 1) * 128], vf[:, h, b0:b0 + bn], idb[:]), [vfB[h], idbB], [ptkB])
                        yield
                    kview = ptkb[0:bn, 0:512].rearrange("p (h m) -> p h m", h=4)
                    vview = ptkb[0:bn, 512:1024].rearrange("p (h m) -> p h m", h=4)
                    P.dve(lambda e: e.tensor_tensor(out=kbg[0:bn], in0=kview, in1=gst[0:bn, 8:12].unsqueeze(2).to_broadcast([bn, 4, 128]), op=ALU.mult),
                          [ptkB, gstB], [kbgB])
                    yield
                    P.dve(lambda e: e.tensor_tensor(out=kgm[0:bn], in0=kview, in1=gst[0:bn, 12:16].unsqueeze(2).to_broadcast([bn, 4, 128]), op=ALU.mult),
                          [ptkB, gstB], [kgmB])
                    yield
                    P.dve(lambda e: e.tensor_tensor(out=vb[0:bn], in0=vview, in1=bg[0:bn, 0:4].unsqueeze(2).to_broadcast([bn, 4, 128]), op=ALU.mult),
                          [ptkB, bgB], [vbB])
                    yield
                    pkk, pkkB = bank("d")
                    pkq, pkqB = bank("d")
                    for h in range(4):
                        P.pe(lambda e, h=h: e.matmul(pkk[0:bn, h * 128:h * 128 + bn], lhsT=kn[:, h, b0:b0 + bn], rhs=kn[:, h, b0:b0 + bn], start=True, stop=True),
                             [knB[h]], [pkkB])
                        yield
                        P.pe(lambda e, h=h: e.matmul(pkq[0:bn, h * 128:h * 128 + bn], lhsT=kn[:, h, b0:b0 + bn], rhs=qn[:, h, b0:b0 + bn], start=True, stop=True),
                             [knB[h], qnB[h]], [pkqB])
                        yield
                    for h in range(4):
                        P.dve(lambda e, h=h: e.scalar_tensor_tensor(out=Nf[0:bn, h, 0:bn], in0=pkk[0:bn, h * 128:h * 128 + bn], scalar=bg[0:bn, 8 + h:9 + h],
                                                                    in1=dlo[0:bn, h, 0:bn], op0=ALU.mult, op1=ALU.mult), [pkkB, bgB, dloB], [NfB])
                        yield
                        P.dve(lambda e, h=h: e.tensor_tensor(out=qkT[0:bn, h, 0:bn], in0=pkq[0:bn, h * 128:h * 128 + bn], in1=dup[0:bn, h, 0:bn], op=ALU.mult),
                              [pkqB, dupB], [qkTB])
                        yield
                        P.dve(lambda e, h=h: e.tensor_tensor(out=qgT[:, h, 0:bn], in0=qn[:, h, b0:b0 + bn], in1=egb[:, h, 0:bn], op=ALU.mult),
                              [qnB[h], egbB], [qgTB])
                        yield
                    ptr, ptrB = bank("d")
                    ptrb = ptr[:].bitcast(BF16)
                    for h in range(4):
                        P.pe(lambda e, h=h: e.transpose(ptrb[0:bn, h * 128:h * 128 + bn], Nf[0:bn, h, 0:bn], idb[0:bn, 0:bn]), [NfB, idbB], [ptrB])
                        yield
                    P.act(lambda e: e.copy(out=NTf[0:bn, :, 0:bn], in_=ptrb[0:bn, 0:512].rearrange("p (h m) -> p h m", h=4)[:, :, 0:bn]), [ptrB], [NTfB])
                    yield

                    def mk(i_):
                        return cmb[0:bn, (8 + i_) * 128:(8 + i_) * 128 + bn].unsqueeze(1).to_broadcast([bn, 4, bn])
                    P.dve(lambda e: e.tensor_tensor(out=Qm[0][0:bn, :, 0:bn], in0=Nf[0:bn, :, 0:bn], in1=mk(0), op=ALU.mult), [NfB, cmbB], [QmB[0]])
                    yield
                    P.pool(lambda e: e.tensor_tensor(out=Rm[0][0:bn, :, 0:bn], in0=NTf[0:bn, :, 0:bn], in1=mk(0), op=ALU.mult), [NTfB, cmbB], [RmB[0]])
                    yield
                    idbc = ID32[0:bn, 0:bn].unsqueeze(1).to_broadcast([bn, 4, bn])
                    P.dve(lambda e: e.tensor_tensor(out=Ym[0:bn, :, 0:bn], in0=Rm[0][0:bn, :, 0:bn], in1=idbc, op=ALU.add), [RmB[0], cm32B], [YmB])
                    yield
                    P.dve(lambda e: e.tensor_tensor(out=Tm[0:bn, :, 0:bn], in0=Qm[0][0:bn, :, 0:bn], in1=idbc, op=ALU.add), [QmB[0], cm32B], [TmB])
                    yield

                    def v4(pb_):
                        return pb_[0:bn, :].rearrange("p (h m) -> p h m", h=4)[:, :, 0:bn]

                    def mm4(pb_, pbB_, lhs, lhsB, rhs, rhsB):
                        for h in range(4):
                            P.pe(lambda e, h=h: e.matmul(pb_[0:bn, h * 128:h * 128 + bn], lhsT=lhs[0:bn, h, 0:bn], rhs=rhs[0:bn, h, 0:bn], start=True, stop=True),
                                 [lhsB, rhsB], [pbB_])

                    def upd(M32, M32B, Mb, MbB, pb_, pbB_):
                        P.dve(lambda e: e.tensor_tensor(out=Mb[0:bn, :, 0:bn], in0=Mb[0:bn, :, 0:bn], in1=v4(pb_), op=ALU.add), [MbB, pbB_], [MbB])

                    cq = 0
                    for lev in range(3):
                        nq_ = 1 - cq
                        pq, pqB = bank("d")
                        pr, prB = bank("d")
                        mm4(pq, pqB, Rm[cq], RmB[cq], Qm[cq], QmB[cq])
                        yield
                        mm4(pr, prB, Qm[cq], QmB[cq], Rm[cq], RmB[cq])
                        yield
                        P.dve(lambda e, nq_=nq_: e.tensor_copy(out=Qm[nq_][0:bn, :, 0:bn], in_=v4(pq)), [pqB], [QmB[nq_]])
                        yield
                        P.act(lambda e, nq_=nq_: e.copy(out=Rm[nq_][0:bn, :, 0:bn], in_=v4(pr)), [prB], [RmB[nq_]])
                        yield
                        py, pyB = bank("d")
                        pt_, pt_B = bank("d")
                        mm4(py, pyB, Qm[nq_], QmB[nq_], Ym, YmB)
                        yield
                        mm4(pt_, pt_B, Rm[nq_], RmB[nq_], Tm, TmB)
                        yield
                        upd(Y32, Y32B, Ym, YmB, py, pyB)
                        yield
                        upd(T32, T32B, Tm, TmB, pt_, pt_B)
                        yield
                        cq = nq_
                    if bn == 128 and not smp:
                        for mi_ in (1, 2, 3):
                            P.dve(lambda e, mi_=mi_: e.tensor_tensor(out=Qm[0][:, :, :], in0=Nf[:, :, :], in1=mk(mi_), op=ALU.mult), [NfB, cmbB], [QmB[0]])
                            yield
                            P.pool(lambda e, mi_=mi_: e.tensor_tensor(out=Rm[0][:, :, :], in0=NTf[:, :, :], in1=mk(mi_), op=ALU.mult), [NTfB, cmbB], [RmB[0]])
                            yield
                            p1, p1B = bank("d")
                            p1p, p1pB = bank("d")
                            mm4(p1, p1B, Rm[0], RmB[0], Tm, TmB)
                            yield
                            mm4(p1p, p1pB, Qm[0], QmB[0], Ym, YmB)
                            yield
                            P.dve(lambda e: e.tensor_copy(out=P1a[:, :, :], in_=v4(p1)), [p1B], [P1aB])
                            yield
                            P.act(lambda e: e.copy(out=P1b[:, :, :], in_=v4(p1p)), [p1pB], [P1bB])
                            yield
                            p2, p2B = bank("d")
                            p2p, p2pB = bank("d")
                            mm4(p2, p2B, Ym, YmB, P1a, P1aB)
                            yield
                            mm4(p2p, p2pB, Tm, TmB, P1b, P1bB)
                            yield
                            upd(T32, T32B, Tm, TmB, p2, p2B)
                            yield
                            upd(Y32, Y32B, Ym, YmB, p2p, p2pB)
                            yield
                    pu, puB = bank("d")
                    pw, pwB = bank("d")
                    for h in range(4):
                        P.pe(lambda e, h=h: e.matmul(pu[0:bn, h * 128:(h + 1) * 128], lhsT=Ym[0:bn, h, 0:bn], rhs=vb[0:bn, h, :], start=True, stop=True), [YmB, vbB], [puB])
                        yield
                        P.pe(lambda e, h=h: e.matmul(pw[:, h * 128:h * 128 + bn], lhsT=kbg[0:bn, h, :], rhs=Ym[0:bn, h, 0:bn], start=True, stop=True), [YmB, kbgB], [pwB])
                        yield
                    P.dve(lambda e: e.tensor_copy(out=u32[0:bn], in_=pu[0:bn, :].rearrange("p (h m) -> p h m", h=4)), [puB], [u32B])
                    yield
                    P.act(lambda e: e.copy(out=wT[:, :, 0:bn], in_=pw[:, :].rearrange("p (h m) -> p h m", h=4)[:, :, 0:bn]), [pwB], [wTB])
                    yield

                    if l == 0 and mode == "p" and first and bi == 0:
                        dump("bg", bg[:], [bgB], F32)
                        dump("gst", gst[:], [gstB], F32)
                        dump("dlo", dlo[:], [dloB], F32)
                        dump("dup", dup[:], [dupB], F32)
                        dump("egb", egb[:], [egbB], F32)
                        dump("u32", u32[:], [u32B], F32)
                        dump("wT", wT[:], [wTB], BF16)
                        dump("qkT", qkT[:], [qkTB], BF16)
                        dump("kgm", kgm[:], [kgmB], BF16)
                        dump("kbg", kbg[:], [kbgB], BF16)
                        dump("qn", qn[:, :, 0:128], qnB, BF16)
                        dump("kn", kn[:, :, 0:128], knB, BF16)
                    def state_step(Ssrc, SsrcB, Sb, SbB, c0, cn, po, poB, kg_, kg_B, glcol, Sdst, SdstB):
                        pws, pwsB = bank("d")
                        for h in range(4):
                            P.pe(lambda e, h=h: e.matmul(pws[0:bn, h * 128:(h + 1) * 128], lhsT=wT[:, h, 0:bn], rhs=Sb[:, h, :], start=True, stop=True), [wTB, SbB], [pwsB])
                            yield
                        P.dve(lambda e: e.tensor_tensor(out=dlt[0:bn], in0=u32[0:bn], in1=pws[0:bn, :].rearrange("p (h m) -> p h m", h=4), op=ALU.subtract),
                              [u32B, pwsB], [dltB])
                        yield
                        for h in range(4):
                            P.pe(lambda e, h=h: e.matmul(po[:, h * 128 + c0:h * 128 + c0 + cn], lhsT=Sb[:, h, :], rhs=qgT[:, h, c0:c0 + cn], start=True, stop=False), [SbB, qgTB], [poB])
                            yield
                            P.pe(lambda e, h=h: e.matmul(po[:, h * 128 + c0:h * 128 + c0 + cn], lhsT=dlt[0:bn, h, :], rhs=qkT[0:bn, h, c0:c0 + cn], start=False, stop=True), [dltB, qkTB], [poB])
                            yield
                        pS, pSB = bank("d")
                        for h in range(4):
                            P.pe(lambda e, h=h: e.matmul(pS[:, h * 128:(h + 1) * 128], lhsT=kg_[0:bn, h, :], rhs=dlt[0:bn, h, :], start=True, stop=True), [kg_B, dltB], [pSB])
                            yield
                        for h in range(4):
                            P.dve(lambda e, h=h: e.scalar_tensor_tensor(out=Sdst[:, h, :], in0=Ssrc[:, h, :], scalar=egb[:, h, glcol:glcol + 1], in1=pS[:, h * 128:(h + 1) * 128],
                                                                        op0=ALU.mult, op1=ALU.add), [SsrcB, egbB, pSB], [SdstB])
                            yield

                    if not smp:
                        po, poB = bank("d")
                        yield from state_step(S32[:, l], S32B[l], Sbf, SbfB, 0, bn, po, poB, kgm, kgmB, bn - 1, S32[:, l], S32B[l])
                        P.act(lambda e: e.copy(out=Sbf[:], in_=S32[:, l]), [S32B[l]], [SbfB])
                        yield
                        evac(oT[:, :, b0:b0 + bn], po[:, :].rearrange("p (h m) -> p h m", h=4)[:, :, 0:bn], [poB], oTB)
                        yield
                    else:
                        for s in range(NS):
                            k = s % 2
                            P.dma(Sld[k][:], st_delta[l, s].rearrange("h k v -> k h v"), writes=[SldB[k]])
                            yield
                            P.act(lambda e, k=k: e.copy(out=Sbf[:], in_=Sld[k][:]), [SldB[k]], [SbfB])
                            yield
                            P.dve(lambda e, s=s: e.tensor_scalar(out=kgs[:], in0=kgm[:], scalar1=pv[:, PV_RM + s:PV_RM + s + 1], scalar2=None, op0=ALU.mult),
                                  [kgmB, pvB], [kgsB])
                            yield
                            yield from state_step(Sld[k], SldB[k], Sbf, SbfB, s * TS, TS, LB, LBb, kgs, kgsB, s * TS + TS - 1, Sld[k], SldB[k])
                            P.dma(o_delta_s[l, s].rearrange("h k v -> k h v"), Sld[k][:], reads=[SldB[k]], writes=[nul])
                            yield
                        evac(oT[:, :, 0:128], LB[:, :].rearrange("p (h m) -> p h m", h=4), [LBb], oTB)
                        yield


            def swa_chain():
                for bi, (b0, bn) in enumerate(blocks):
                    smp = mode == "s"
                    pq_, pq_B = bank("s")
                    for kc in range(8):
                        P.pe(lambda e, kc=kc: e.matmul(pq_[0:bn, :], lhsT=xn[:, kc, b0:b0 + bn], rhs=wsq[:, kc, :], start=(kc == 0), stop=(kc == 7)), [wsqB, xnB[kc]], [pq_B])
                        yield
                    pk_, pk_B = bank("s")
                    for kc in range(8):
                        P.pe(lambda e, kc=kc: e.matmul(pk_[0:bn, 0:256], lhsT=xn[:, kc, b0:b0 + bn], rhs=wkv[:, kc, :], start=(kc == 0), stop=(kc == 7)), [wkvB, xnB[kc]], [pk_B])
                        yield
                    P.act(lambda e: e.copy(out=qtm[0:bn], in_=pq_[0:bn, :]), [pq_B], [qtmB])
                    yield
                    P.dve(lambda e: e.tensor_copy(out=kvtm[0:bn], in_=pk_[0:bn, 0:256]), [pk_B], [kvtmB])
                    yield
                    if smp:
                        P.dma(rp[:], rope_s, writes=[rpB])
                        yield
                    else:
                        P.dma(rp[0:bn], rope_p[pos0 + b0:pos0 + b0 + bn, :], writes=[rpB])
                        yield

                    def rope(src, srcB, nh, dst32, dst32B, dstb, dstbB):
                        sv = src.rearrange("p (h t d) -> p h t d", h=nh, t=2)
                        x1, x2 = sv[:, :, 0, :], sv[:, :, 1, :]
                        cosb = rp[0:bn, 0:32].unsqueeze(1).to_broadcast([bn, nh, 32])
                        sinb = rp[0:bn, 32:64].unsqueeze(1).to_broadcast([bn, nh, 32])
                        t0 = rt[0][0:bn, 0:nh * 32].rearrange("p (h d) -> p h d", h=nh)
                        t1 = rt[1][0:bn, 0:nh * 32].rearrange("p (h d) -> p h d", h=nh)
                        dv = dst32.rearrange("p (h t d) -> p h t d", h=nh, t=2)
                        P.dve(lambda e: e.tensor_tensor(out=t0, in0=x1, in1=cosb, op=ALU.mult), [srcB, rpB], [rtB[0]])
                        P.dve(lambda e: e.tensor_tensor(out=t1, in0=x2, in1=sinb, op=ALU.mult), [srcB, rpB], [rtB[1]])
                        P.dve(lambda e: e.tensor_tensor(out=dv[:, :, 0, :], in0=t0, in1=t1, op=ALU.subtract), [rtB[0], rtB[1]], [dst32B])
                        P.dve(lambda e: e.tensor_tensor(out=t0, in0=x2, in1=cosb, op=ALU.mult), [srcB, rpB], [rtB[0]])
                        P.dve(lambda e: e.tensor_tensor(out=t1, in0=x1, in1=sinb, op=ALU.mult), [srcB, rpB], [rtB[1]])
                        P.dve(lambda e: e.tensor_tensor(out=dv[:, :, 1, :], in0=t0, in1=t1, op=ALU.add), [rtB[0], rtB[1]], [dst32B])
                        if dstb is not None:
                            P.act(lambda e: e.copy(out=dstb, in_=dst32), [dst32B], [dstbB])

                    rope(qtm[0:bn, :], qtmB, 8, smx[0:bn, 0:2, :].rearrange("p a b -> p (a b)"), smxB, qr[0:bn, :], qrB)
                    yield
                    rope(kvtm[0:bn, 0:128], kvtmB, 2, kr32[0:bn, :], kr32B, krb[0:bn, :], krbB)
                    yield
                    P.act(lambda e: e.copy(out=vtm[0:bn, 1, :], in_=kvtm[0:bn, 128:256]), [kvtmB], [vtmB[1]])
                    yield
                    if mode == "p":
                        apos = pos0 + b0
                        lo = max(apos, TP - 128)
                        hi = apos + bn
                        if hi > lo:
                            P.dma(o_k_p[l, lo - (TP - 128):hi - (TP - 128), :], kr32[lo - apos:hi - apos, :], reads=[kr32B], writes=[nul])
                            yield
                            P.dma(o_v_p[l, lo - (TP - 128):hi - (TP - 128), :], kvtm[lo - apos:hi - apos, 128:256], reads=[kvtmB], writes=[nul])
                            yield
                    else:
                        for s_ in range(NS):
                            P.dma(o_k_s[l, s_, 120:128, :], kr32[s_ * TS:(s_ + 1) * TS, :], reads=[kr32B], writes=[nul])
                            yield
                            P.dma(o_v_s[l, s_, 120:128, :], kvtm[s_ * TS:(s_ + 1) * TS, 128:256], reads=[kvtmB], writes=[nul])
                            yield
                        P.dma(o_k_s[l, :, 0:120, :], st_k[l, :, 8:128, :], writes=[nul])
                        yield
                        P.dma(o_v_s[l, :, 0:120, :], st_v[l, :, 8:128, :], writes=[nul])
                        yield
                    ptq, ptqB = bank("s")
                    ptqb = ptq[:].bitcast(BF16)
                    for h in range(8):
                        P.pe(lambda e, h=h: e.transpose(ptqb[0:64, h * 128:h * 128 + bn], qr[0:bn, h * 64:(h + 1) * 64], idb[0:bn, 0:bn]), [qrB, idbB], [ptqB])
                        yield
                    P.dve(lambda e: e.tensor_copy(out=qTt[:, :, 0:bn], in_=ptqb[0:64, :].rearrange("p (h m) -> p h m", h=8)[:, :, 0:bn]), [ptqB], [qTtB])
                    yield
                    ptk2, ptk2B = bank("s")
                    ptk2b = ptk2[:].bitcast(BF16)
                    for g in range(2):
                        P.pe(lambda e, g=g: e.transpose(ptk2b[0:64, g * 128:g * 128 + bn], krb[0:bn, g * 64:(g + 1) * 64], idb[0:bn, 0:bn]), [krbB, idbB], [ptk2B])
                        yield
                    P.act(lambda e: e.copy(out=kTc[:, :, 128:128 + bn], in_=ptk2b[0:64, 0:256].rearrange("p (g m) -> p g m", g=2)[:, :, 0:bn]), [ptk2B], [kTcB])
                    yield

                    def attend(maskap, maskB, first_acc, last_acc, pov, povB):
                        nk = 128 + bn
                        for hp_ in range(4):
                            psc, pscB = bank("s")
                            for j in range(2):
                                h = hp_ * 2 + j
                                P.pe(lambda e, h=h, j=j: e.matmul(psc[0:bn, j * 256:j * 256 + nk], lhsT=qTt[:, h, 0:bn], rhs=kTc[:, h // 4, 0:nk], start=True, stop=True),
                                     [qTtB, kTcB], [pscB])
                                yield
                            P.dve(lambda e, hp_=hp_: e.scalar_tensor_tensor(out=smx[0:bn, hp_ * 2:hp_ * 2 + 2, 0:nk], in0=psc[0:bn, :].rearrange("p (j m) -> p j m", j=2)[:, :, 0:nk],
                                                                        scalar=0.125, in1=maskap[0:bn, 0:nk].unsqueeze(1).to_broadcast([bn, 2, nk]),
                                                                        op0=ALU.mult, op1=ALU.add), [pscB, maskB], [smxB])
                            yield
                        P.dve(lambda e: e.tensor_reduce(out=st8[0:bn, 0:8], in_=smx[0:bn, :, 0:nk], axis=AX.X, op=ALU.max), [smxB], [st8B])
                        yield
                        P.dve(lambda e: e.tensor_tensor(out=st8[0:bn, 0:8], in0=st8[0:bn, 0:8], in1=rc[0:bn, 16 + l * 8:24 + l * 8], op=ALU.max), [st8B, rcB], [st8B])
                        yield
                        P.dve(lambda e: e.tensor_scalar(out=st8[0:bn, 8:16], in0=st8[0:bn, 0:8], scalar1=-1.0, scalar2=None, op0=ALU.mult), [st8B], [st8B])
                        yield
                        P.dve(lambda e: e.tensor_tensor(out=st8[0:bn, 16:24], in0=rc[0:bn, 16 + l * 8:24 + l * 8], in1=st8[0:bn, 0:8], op=ALU.subtract), [st8B, rcB], [st8B])
                        yield
                        P.act(lambda e: e.activation(out=st8[0:bn, 16:24], in_=st8[0:bn, 16:24], func=AF.Exp), [st8B], [st8B])
                        yield
                        P.dve(lambda e: e.memset(st8[0:bn, 24:32], 0.0), [st8B], [st8B])
                        yield
                        for h in range(8):
                            P.act(lambda e, h=h: e.activation(out=smx[0:bn, h, 0:nk], in_=smx[0:bn, h, 0:nk], func=AF.Exp, bias=st8[0:bn, 8 + h:9 + h],
                                                              accum_out=st8[0:bn, 24 + h:25 + h]), [smxB, st8B], [smxB, st8B])
                            yield
                        P.dve(lambda e: e.tensor_tensor(out=st8[0:bn, 32:40], in0=st8[0:bn, 24:32], in1=st8[0:bn, 16:24], op=ALU.add), [st8B], [st8B])
                        yield
                        P.dve(lambda e: e.reciprocal(out=st8[0:bn, 40:48], in_=st8[0:bn, 32:40]), [st8B], [st8B])
                        yield
                        P.dve(lambda e: e.tensor_tensor(out=pexp[0:bn, :, 0:nk], in0=smx[0:bn, :, 0:nk], in1=st8[0:bn, 40:48].unsqueeze(2).to_broadcast([bn, 8, nk]), op=ALU.mult),
                              [smxB, st8B], [pexpB])
                        yield
                        for half in range(2):
                            ptp, ptpB = bank("s")
                            ptpb = ptp[:].bitcast(BF16)
                            for hh in range(4):
                                h = half * 4 + hh
                                P.pe(lambda e, h=h, hh=hh: e.transpose(ptpb[:, (hh * 2) * 128:(hh * 2) * 128 + bn], pexp[0:bn, h, 0:128], idb[0:bn, 0:bn]), [pexpB, idbB], [ptpB])
                                yield
                                P.pe(lambda e, h=h, hh=hh: e.transpose(ptpb[0:bn, (hh * 2 + 1) * 128:(hh * 2 + 1) * 128 + bn], pexp[0:bn, h, 128:128 + bn], idb[0:bn, 0:bn]), [pexpB, idbB], [ptpB])
                                yield
                            if bn == 128:
                                evac(pT[:, half * 8:half * 8 + 8, :], ptpb[:, :].rearrange("p (a m) -> p a m", a=8), [ptpB], [pTB])
                                yield
                            else:
                                for hh in range(4):
                                    evac(pT[:, half * 8 + hh * 2, 0:bn], ptpb[:, (hh * 2) * 128:(hh * 2) * 128 + bn], [ptpB], [pTB])
                                    yield
                                    evac(pT[0:bn, half * 8 + hh * 2 + 1, 0:bn], ptpb[0:bn, (hh * 2 + 1) * 128:(hh * 2 + 1) * 128 + bn], [ptpB], [pTB])
                                    yield
                        for h in range(8):
                            g = h // 4
                            P.pe(lambda e, h=h, g=g: e.matmul(pov[0:bn, h * 64:(h + 1) * 64], lhsT=pT[:, h * 2, 0:bn], rhs=vtm[:, 0, g * 64:(g + 1) * 64],
                                                              start=first_acc, stop=False), [pTB, vtmB[0]], [povB])
                            yield
                            P.pe(lambda e, h=h, g=g: e.matmul(pov[0:bn, h * 64:(h + 1) * 64], lhsT=pT[0:bn, h * 2 + 1, 0:bn], rhs=vtm[0:bn, 1, g * 64:(g + 1) * 64],
                                                              start=False, stop=last_acc), [pTB, vtmB[1]], [povB])
                            yield

                    if not smp:
                        mi_ = 1 if (first and bi == 0) else 0
                        pov, povB = bank("s")
                        yield from attend(bnd[:, mi_ * 256:(mi_ + 1) * 256], bndB, True, True, pov, povB)
                        P.act(lambda e: e.copy(out=otm[0:bn, :], in_=pov[0:bn, :]), [povB], [otmB])
                        yield
                        if l == 0 and first and bi == 0:
                            dump("qr", qr[:], [qrB], BF16)
                            dump("qtm", qtm[:], [qtmB], F32)
                            dump("rp", rp[:], [rpB], F32)
                            dump("kr32", kr32[:], [kr32B], F32)
                            dump("qTt", qTt[:], [qTtB], BF16)
                            dump("kTc", kTc[:], [kTcB], BF16)
                            dump("st8", st8[:], [st8B], F32)
                            dump("pexp", pexp[:], [pexpB], BF16)
                            dump("pT", pT[:], [pTB], BF16)
                            dump("otm", otm[:], [otmB], BF16)
                            dump("vtm", vtm[:], vtmB, BF16)
                        if not last:
                            P.pool(lambda e: e.tensor_copy(out=kTc[:, :, 0:128], in_=kTc[:, :, 128:256]), [kTcB], [kTcB])
                            yield
                            P.pool(lambda e: e.tensor_copy(out=vtm[:, 0, :], in_=vtm[:, 1, :]), [vtmB[1]], [vtmB[0]])
                            yield
                    else:
                        for s in range(NS):
                            k = s % 2
                            P.dma(msk[k], smask[s], writes=[mskB[k]])
                            yield
                            P.dma(ctm[k][:, 0:128], st_k[l, s], writes=[ctmB[k]])
                            yield
                            P.dma(ctm[k][:, 128:256], st_v[l, s], writes=[ctmB[k]])
                            yield
                            P.act(lambda e, k=k: e.copy(out=ctb[:], in_=ctm[k]), [ctmB[k]], [ctbB])
                            yield
                            P.pool(lambda e: e.tensor_copy(out=vtm[:, 0, :], in_=ctb[:, 128:256]), [ctbB], [vtmB[0]])
                            yield
                            pck, pckB = bank("s")
                            pckb = pck[:].bitcast(BF16)
                            for g in range(2):
                                P.pe(lambda e, g=g: e.transpose(pckb[0:64, g * 128:(g + 1) * 128], ctb[:, g * 64:(g + 1) * 64], idb[:]), [ctbB, idbB], [pckB])
                                yield
                            P.dve(lambda e: e.tensor_copy(out=kTc[:, :, 0:128], in_=pckb[0:64, 0:256].rearrange("p (g m) -> p g m", g=2)), [pckB], [kTcB])
                            yield
                            pov, povB = bank("s")
                            yield from attend(msk[k], mskB[k], True, True, pov, povB)
                            if s == 0:
                                P.dve(lambda e: e.tensor_copy(out=qtm[:, :], in_=pov[:, :]), [povB], [qtmB])
                                yield
                            elif s < NS - 1:
                                P.dve(lambda e: e.tensor_tensor(out=qtm[:, :], in0=qtm[:, :], in1=pov[:, :], op=ALU.add), [qtmB, povB], [qtmB])
                                yield
                            else:
                                P.dve(lambda e: e.tensor_tensor(out=otm[:, :], in0=qtm[:, :], in1=pov[:, :], op=ALU.add), [qtmB, povB], [otmB])
                                yield
                    pto, ptoB = bank("s")
                    ptob = pto[:].bitcast(BF16)
                    for c in range(4):
                        P.pe(lambda e, c=c: e.transpose(ptob[:, c * 128:c * 128 + bn], otm[0:bn, c * 128:(c + 1) * 128], idb[0:bn, 0:bn]), [otmB, idbB], [ptoB])
                        yield
                    evac(oc[:, :, b0:b0 + bn], ptob[:, 0:512].rearrange("p (c m) -> p c m", c=4)[:, :, 0:bn], [ptoB], ocB)
                    yield

            chains = [pool_chain(), delta_chain(), swa_chain()]
            while chains:
                for g_ in list(chains):
                    try:
                        next(g_)
                    except StopIteration:
                        chains.remove(g_)
            if mode == "p":
                if not last:
                    P.pool(lambda e: e.tensor_copy(out=kTh[:, l], in_=kTc[:, :, 0:128]), [kTcB], [kThB[l]])
                    P.pool(lambda e: e.tensor_copy(out=vh[:, l], in_=vtm[:, 0, :]), [vtmB[0]], [vhB[l]])
                else:
                    P.dma(o_delta_p[l].rearrange("h k v -> k h v"), S32[:, l], reads=[S32B[l]], writes=[nul])

            for h in range(4):
                rms_stats(lambda c, h=h: oT[:, h, 0:n], [oTB[h]], 1, n, 1.0 / 128.0, 92)
                P.dve(lambda e, h=h: e.scalar_tensor_tensor(out=sil[:, 0:n], in0=oT[:, h, 0:n], scalar=pv[:, PV_ON + l:PV_ON + l + 1], in1=rstd[:, 0:n],
                                                            op0=ALU.mult, op1=ALU.mult), [oTB[h], pvB, rstdB], [silB])
                P.dve(lambda e, h=h: e.tensor_tensor(out=ob[:, h, 0:n], in0=sil[:, 0:n], in1=zs[:, h, 0:n], op=ALU.mult), [silB, zsB[h]], [obB[h]])

            wpv, wpB = wget("poolw")
            for g in range(4):
                pb, pbB = bank()
                P.pe(lambda e, g=g: e.matmul(pb[:, 0:n], lhsT=wpv[:, g, :], rhs=dpool[:, g, 0:n], start=True, stop=True), [wpB, dpoolB[g]], [pbB])
                P.dve(lambda e, g=g: e.tensor_scalar(out=oa[:, g, 0:n], in0=pb[:, 0:n], scalar1=pv[:, PV_PS + l * 4 + g:PV_PS + l * 4 + g + 1], scalar2=None, op0=ALU.mult),
                      [pbB, pvB], [oaB[g]])

            if l == 0 and mode == "p" and first:
                dump("oa", oa[:], oaB, BF16)
                dump("ob", ob[:], obB, BF16)
                dump("oc", oc[:], ocB, BF16)
                dump("oT", oT[:], oTB, F32)
                dump("zs", zs[:], zsB, BF16)
                dump("S", S32[:, 0], [S32B[0]], F32)
            if l == 0 and mode == "s":
                dump("s_oa", oa[:, :, 0:128], oaB, BF16)
                dump("s_ob", ob[:, :, 0:128], obB, BF16)
                dump("s_oc", oc[:, :, 0:128], ocB, BF16)
                dump("s_oT", oT[:, :, 0:128], oTB, F32)
            for i, (osrc, osrcB) in enumerate(((oa, oaB), (ob, obB), (oc, ocB))):
                wpj, wpjB = wget("proj%d" % i)
                for half in range(2):
                    wg, wgB = wget("g%d_%d" % (i, half))
                    for m4 in range(4):
                        mc = half * 4 + m4
                        pg_, pg_B = bank()
                        for kc in range(8):
                            P.pe(lambda e, kc=kc, m4=m4: e.matmul(pg_[:, 0:n], lhsT=wg[:, kc, m4 * 128:(m4 + 1) * 128], rhs=xn[:, kc, 0:n], start=(kc == 0), stop=(kc == 7)),
                                 [wgB, xnB[kc]], [pg_B])
                        k = mc % 2
                        P.act(lambda e, k=k: e.activation(out=sg[k][:, 0:n], in_=pg_[:, 0:n], func=AF.Sigmoid), [pg_B], [sgB[k]])
                        pp_, pp_B = bank()
                        for kc in range(4):
                            P.pe(lambda e, kc=kc, mc=mc: e.matmul(pp_[:, 0:n], lhsT=wpj[:, kc, mc * 128:(mc + 1) * 128], rhs=osrc[:, kc, 0:n], start=(kc == 0), stop=(kc == 3)),
                                 [wpjB, osrcB[kc]], [pp_B])
                        if i == 0:
                            P.dve(lambda e, k=k, mc=mc: e.tensor_tensor(out=macc[:, mc, 0:n], in0=sg[k][:, 0:n], in1=pp_[:, 0:n], op=ALU.mult), [sgB[k], pp_B], [maccB[mc]])
                        else:
                            P.dve(lambda e, k=k: e.tensor_tensor(out=sg[k][:, 0:n], in0=sg[k][:, 0:n], in1=pp_[:, 0:n], op=ALU.mult), [sgB[k], pp_B], [sgB[k]])
                            if i == 1:
                                P.pool(lambda e, k=k, mc=mc: e.tensor_tensor(out=macc[:, mc, 0:n], in0=macc[:, mc, 0:n], in1=sg[k][:, 0:n], op=ALU.add), [sgB[k], maccB[mc]], [maccB[mc]])
                            else:
                                P.pool(lambda e, k=k, mc=mc: e.tensor_tensor(out=mbf[:, mc, 0:n], in0=macc[:, mc, 0:n], in1=sg[k][:, 0:n], op=ALU.add), [sgB[k], maccB[mc]], [mbfB[mc]])

            for half in range(2):
                wo, woB = wget("out%d" % half)

                def conso(mi, pb, pbB, half=half):
                    c = half * 4 + mi
                    P.dve(lambda e: e.tensor_tensor(out=xT[:, c, 0:n], in0=xT[:, c, 0:n], in1=pb[:, 0:n], op=ALU.add), [xTB[c], pbB], [xTB[c]])
                fm_group(wo, woB, [0, 128, 256, 384], n, list(range(8)), lambda kc: mbf[:, kc, 0:n], mbfB, conso)

            if l == 0 and mode == "p" and first:
                dump("mbf", mbf[:], mbfB, BF16)
                dump("h", xT[:], xTB, F32)
            rmsnorm_fm(n, PV_N2 + l * 8)
            for half in range(2):
                for j in range(4):
                    wu, wuB = wget("up%d" % (half * 4 + j))

                    def consu(mi, pb, pbB, j=j):
                        k = mi % 2
                        P.act(lambda e: e.activation(out=rl[k][:, 0:n], in_=pb[:, 0:n], func=AF.Relu), [pbB], [rlB[k]])
                        P.pool(lambda e: e.tensor_tensor(out=upb[:, j * 4 + mi, 0:n], in0=rl[k][:, 0:n], in1=rl[k][:, 0:n], op=ALU.mult), [rlB[k]], [upB[j * 4 + mi]])
                    fm_group(wu, wuB, [0, 128, 256, 384], n, list(range(8)), lambda kc: xn[:, kc, 0:n], xnB, consu)
                acc = {}
                for ch in range(2):
                    for kg in range(2):
                        wd, wdB = wget("dn%d_%d_%d" % (half, kg, ch))
                        for mi in range(4):
                            c = ch * 4 + mi
                            if kg == 0:
                                acc[c] = bank()
                            pb, pbB = acc[c]
                            for kc in range(8):
                                P.pe(lambda e, kc=kc, mi=mi, pb=pb, kg=kg: e.matmul(pb[:, 0:n], lhsT=wd[:, kc, mi * 128:(mi + 1) * 128], rhs=upb[:, kg * 8 + kc, 0:n],
                                                                                start=(kg == 0 and kc == 0), stop=(kg == 1 and kc == 7)), [wdB, upB[kg * 8 + kc]], [pbB])
                            if kg == 1:
                                P.dve(lambda e, c=c, pb=pb: e.tensor_tensor(out=xT[:, c, 0:n], in0=xT[:, c, 0:n], in1=pb[:, 0:n], op=ALU.add), [xTB[c], pbB], [xTB[c]])

        def after_layer(l, mode, first):
            if l == 0 and mode == "p" and first:
                dump("x1", xT[:], xTB, F32)

        def load_x(src, r0, nr, c0):
            k = (r0 // 128) % 2
            P.dma(xtm[k][0:nr, :], src[r0:r0 + nr, :], writes=[xtmB[k]])
            kp = max(32, nr)
            if DBG.get("xdmaonly"):
                return
            for hf in range(2):
                pb, pbB = bank()
                for j in range(4):
                    c = hf * 4 + j
                    P.pe(lambda e, c=c, j=j: e.transpose(pb[:, j * 128:j * 128 + kp], xtm[k][0:kp, c * 128:(c + 1) * 128], ID32[0:kp, 0:kp]), [xtmB[k], cm32B], [pbB])
                if DBG.get("xnoevac"):
                    continue
                for j in range(4):
                    c = hf * 4 + j
                    evac(xT[:, c, c0:c0 + nr], pb[:, j * 128:j * 128 + nr], [pbB], [xTB[c]], eng=None if DBG.get("alt") else ("act" if hf else "dve"))

        def store_y(dst, r0, nr, c0):
            for hf in range(2):
                pb, pbB = bank()
                for j in range(4):
                    c = hf * 4 + j
                    P.pe(lambda e, c=c, j=j: e.transpose(pb[0:nr, j * 128:(j + 1) * 128], macc[:, c, c0:c0 + nr], ID32[:, :]), [maccB[c], cm32B], [pbB])
                evac(ytm[0:nr, hf * 512:(hf + 1) * 512], pb[0:nr, :], [pbB], [ytmB])
            P.dma(dst[r0:r0 + nr, :], ytm[0:nr, :], reads=[ytmB], writes=[nul])

        def final_norm(n):
            rms_stats(lambda c: xT[:, c, 0:n], xTB, 8, n, 1.0 / D, 92)
            for c in range(8):
                P.dve(lambda e, c=c: e.scalar_tensor_tensor(out=macc[:, c, 0:n], in0=xT[:, c, 0:n], scalar=pv[:, PV_FN + c:PV_FN + c + 1], in1=rstd[:, 0:n],
                                                            op0=ALU.mult, op1=ALU.mult), [xTB[c], rstdB, pvB], [maccB[c]])

        P.pool(lambda e: e.memset(xtm0[:], 0.0), [], [_xb])
        for i in range(2):
            P.pool(lambda e, i=i: e.memset(hb[i][:], 0.0), [], [hbB[i]])
        for i in range(2):
            P.pool(lambda e, i=i: e.memset(pscr[i][:], 0.0), [], [pscrB[i]])
        for l in range(L):
            P.pool(lambda e, l=l: e.memset(poolh[:, l], 0.0), [], [poolhB[l]])
            P.pool(lambda e, l=l: e.memset(convh[:, l], 0.0), [], [convhB[l]])
            P.pool(lambda e, l=l: e.memset(S32[:, l], 0.0), [], [S32B[l]])
            P.pool(lambda e, l=l: e.memset(kTh[:, l], 0.0), [], [kThB[l]])
            P.pool(lambda e, l=l: e.memset(vh[:, l], 0.0), [], [vhB[l]])
        P.pool(lambda e: e.memset(vtm[:], 0.0), [], vtmB)
        P.pool(lambda e: e.memset(pT[:], 0.0), [], [pTB])

        tiles = [(t * 512, 512) for t in range(NT)] + [(NT * 512, 16)]
        if DBG.get("notiles"):
            tiles = []
        if DBG.get("notail"):
            tiles = tiles[:-1]
        if DBG.get("tailonly"):
            tiles = tiles[-1:]
        for ti, (p0, n) in enumerate(tiles):
            for r in range(0, n, 128):
                load_x(xin, p0 + r, min(128, n - r), r)
            if not DBG.get("nolayers"):
                for l in range(L):
                    tile_layer(l, "p", n, p0, ti == 0, ti == len(tiles) - 1)
                    after_layer(l, "p", ti == 0)
            if not DBG.get("nofinal"):
                final_norm(n)
            if not DBG.get("nostore"):
                for r in range(0, n, 128):
                    store_y(y_p, p0 + r, min(128, n - r), r)
        if not DBG.get("nosample"):
            load_x(xs_in, 0, 128, 0)
            if not DBG.get("nolayers"):
                for l in range(L):
                    tile_layer(l, "s", 128, PAST, False, False)
            final_norm(128)
            store_y(y_s, 0, 128, 0)
        if not DBG:
            assert wstate["next"] == len(specs), (wstate["next"], len(specs))
        P.emit()
    return nc


def _consts(TP):
    f = np.float32
    cm = np.zeros((6, 128, 128), f)
    cm[0] = np.eye(128)
    cm[1] = 1.0
    i = np.arange(128)
    cm[2] = (i[:, None] <= i[None, :])
    same = (i[:, None] // TS) == (i[None, :] // TS)
    cm[3] = cm[2] * same
    cm[4] = 1.0
    cm[5] = same
    cm = np.ascontiguousarray(cm.transpose(1, 0, 2).reshape(128, 6 * 128))
    a_lo = np.where(i[None, :] < i[:, None], 0.0, BIG).astype(f)
    a_up = np.where(i[None, :] >= i[:, None], 0.0, -BIG).astype(f)
    a_lo_s = np.where((i[None, :] < i[:, None]) & same, 0.0, BIG).astype(f)
    a_up_s = np.where((i[None, :] >= i[:, None]) & same, 0.0, -BIG).astype(f)
    cmask = np.concatenate([a_lo, a_up, a_lo_s, a_up_s], axis=1)
    r = np.arange(128)[:, None]
    j = np.arange(256)[None, :]
    ok = (j > r) & (j <= r + 128)
    band0 = np.where(ok, 0.0, -BIG).astype(f)
    band1 = np.where(ok & (j >= 128), 0.0, -BIG).astype(f)
    band = np.concatenate([band0, band1, band0], axis=1)
    sm = np.full((NS, 128, 256), -BIG, f)
    rs, rt_ = np.arange(128) // TS, np.arange(128) % TS
    for s in range(NS):
        rows = rs == s
        cache_ok = (np.arange(128)[None, :] > rt_[:, None]) & rows[:, None]
        new_ok = (rs[None, :] == s) & (rt_[None, :] <= rt_[:, None]) & rows[:, None]
        sm[s][:, 0:128][cache_ok] = 0.0
        sm[s][:, 128:256][new_ok] = 0.0
    selp = np.zeros((3, 128, NS * 23), f)
    for tok in range(128):
        selp[0, tok, (tok // TS) * 23 + 15 + tok % TS] = 1.0
    for rr in range(240):
        selp[1 + rr // 120, rr % 120, (rr // 15) * 23 + rr % 15] = 1.0
    selc = np.zeros((2, 128, NS * 11), f)
    for tok in range(128):
        selc[0, tok, (tok // TS) * 11 + 3 + tok % TS] = 1.0
    for rr in range(48):
        selc[1, rr, (rr // 3) * 11 + rr % 3] = 1.0
    invc = np.zeros((128, 4 * 16), f)
    for g, w in enumerate((2, 4, 8, 16)):
        invc[:, g * 16:(g + 1) * 16] = 1.0 / np.minimum(w, np.arange(16) + 1)
    half = 32
    inv = (10000.0 ** (-np.arange(half, dtype=np.float32) / half)).astype(f)

    def rope_tab(pos):
        ang = pos.astype(f)[:, None] * inv[None, :]
        return np.concatenate([np.cos(ang), np.sin(ang)], axis=1).astype(f)
    rope_p = rope_tab(np.arange(TP))
    rope_s = rope_tab(PAST + (np.arange(128) % TS))
    import ml_dtypes
    Dm = lambda s_: ((i[:, None] // s_) == (i[None, :] // s_)).astype(f)
    dmask = np.concatenate([Dm(16), Dm(32) - Dm(16), Dm(64) - Dm(32), 1.0 - Dm(64)], axis=1).astype(ml_dtypes.bfloat16)
    return dict(dmask=dmask, cm=cm, cmask=cmask, band=band, smask=sm, selp=selp, selc=selc, invc=invc, rope_p=rope_p, rope_s=rope_s)


_CACHE = {}


def kernel(x_prompt, x_sample, state_pool, state_conv, state_delta, cache_swa_k, cache_swa_v,
           meta_tokens, norm1_w, w_in, pool_w, pool_scale, dn_conv_w, dn_a_log, dn_dt_bias,
           dn_onorm_w, swa_sinks, proj_a, proj_b, proj_c, w_out, norm2_w, w_up, w_down, final_norm_w):
    f = np.float32
    A = lambda a: np.ascontiguousarray(np.asarray(a, dtype=f))
    x_prompt, x_sample = A(x_prompt), A(x_sample)
    B, SEQ, _ = x_prompt.shape
    NT = SEQ // 512
    TP = NT * 512 + 16
    assert SEQ == NT * 512
    nb = x_sample.shape[0]
    assert nb == 8 * NS and x_sample.shape[1] == TS
    if NT not in _CACHE:
        _CACHE[NT] = (build(NT), _consts(TP))
    nc, cst = _CACHE[NT]
    pvec = np.zeros((128, 96), f)
    n1, n2, fn = A(norm1_w), A(norm2_w), A(final_norm_w)
    for l in range(L):
        pvec[:, 0 + l * 8:8 + l * 8] = n1[l].reshape(8, 128).T
        pvec[:, 16 + l * 8:24 + l * 8] = n2[l].reshape(8, 128).T
        pvec[:, 40 + l * 4:44 + l * 4] = A(pool_scale)[l].reshape(4, 128).T
        pvec[:, 48 + l] = A(dn_onorm_w)[l]
    pvec[:, 32:40] = fn.reshape(8, 128).T
    pvec[:, 50:66] = (np.arange(128)[:, None] // TS == np.arange(NS)[None, :])
    pvec[:, 94] = 1.0
    pvec[:, 93] = -0.5 * np.log(128.0)
    pvec[:, 92] = 0.0
    pvec[:, 95] = 1e-6
    convw = A(dn_conv_w).reshape(L, 4, 12, 128).transpose(3, 0, 2, 1).reshape(128, L * 48)
    rowc = np.zeros((1, 64), f)
    rowc[0, 0:8] = A(dn_a_log).reshape(-1)
    rowc[0, 8:16] = A(dn_dt_bias).reshape(-1)
    rowc[0, 16:32] = A(swa_sinks).reshape(-1)
    shared = dict(w_in=A(w_in), pool_w=A(pool_w), proj_a=A(proj_a), proj_b=A(proj_b), proj_c=A(proj_c),
                  w_out=A(w_out), w_up=A(w_up), w_down=A(w_down), pvec=pvec, convw=np.ascontiguousarray(convw), rowc=rowc, **cst)
    meta = A(meta_tokens)
    sp, sc, sd = A(state_pool), A(state_conv), A(state_delta)
    ck, cv = A(cache_swa_k), A(cache_swa_v)
    in_maps = []
    for c in range(8):
        b = c % B
        s0 = c * NS
        m = dict(shared)
        m["xin"] = np.ascontiguousarray(np.concatenate([meta, x_prompt[b]], axis=0))
        m["xs"] = np.ascontiguousarray(x_sample[s0:s0 + NS].reshape(NS * TS, D))
        m["st_pool"] = np.ascontiguousarray(sp[:, s0:s0 + NS].reshape(L, NS * 15, 512))
        m["st_conv"] = np.ascontiguousarray(sc[:, s0:s0 + NS].reshape(L, NS * 3, 1536))
        m["st_delta"] = np.ascontiguousarray(sd[:, s0:s0 + NS])
        m["st_k"] = np.ascontiguousarray(ck[:, s0:s0 + NS].reshape(L, NS, 128, 128))
        m["st_v"] = np.ascontiguousarray(cv[:, s0:s0 + NS].reshape(L, NS, 128, 128))
        in_maps.append(m)
    res = run_bass_kernel_spmd(nc, in_maps, core_ids=list(range(8)))
    R = res.results
    y_prompt = np.stack([R[b]["y_p"][NMETA:] for b in range(B)])
    y_sample = np.concatenate([R[c]["y_s"].reshape(NS, TS, D) for c in range(8)])
    st = lambda k, shp: np.stack([R[b][k].reshape(shp) for b in range(B)], axis=1)
    pool_p = st("pool_p", (L, 15, 512))
    conv_p = st("conv_p", (L, 3, 1536))
    delta_p = st("delta_p", (L, 4, 128, 128))
    k_p = st("k_p", (L, 128, 2, 64))
    v_p = st("v_p", (L, 128, 2, 64))
    cat = lambda k, shp: np.concatenate([R[c][k].reshape(shp) for c in range(8)], axis=1)
    pool_s = cat("pool_s", (L, NS, 15, 512))
    conv_s = cat("conv_s", (L, NS, 3, 1536))
    delta_s = cat("delta_s", (L, NS, 4, 128, 128))
    k_s = cat("k_s", (L, NS, 128, 2, 64))
    v_s = cat("v_s", (L, NS, 128, 2, 64))
    return (y_prompt.astype(f), y_sample.astype(f), pool_p, conv_p, delta_p, k_p, v_p, pool_s, conv_s, delta_s, k_s, v_s)
```

```python
import contextlib
import numpy as np
import concourse.bass as bass
import concourse.mybir as mybir
from concourse.bass_utils import run_bass_kernel_spmd

F32 = mybir.dt.float32
BF16 = mybir.dt.bfloat16
AF = mybir.ActivationFunctionType
ALU = mybir.AluOpType
AX = mybir.AxisListType

SAME_ENGINE_SYNC = ("gpsimd", "vector", "scalar")
ENGS = ("tensor", "vector", "scalar", "gpsimd", "sync")
BIG = 30000.0


class Buf:
    __slots__ = ("name", "last_w", "readers", "aliases", "excl")

    def __init__(self, name="", excl=False):
        self.excl = excl
        self.name = name
        self.last_w = None
        self.readers = []
        self.aliases = []


class Op:
    __slots__ = ("eng", "fn", "deps", "is_dma", "signal", "count", "slot", "use", "dbg")

    def __init__(self, eng, fn, is_dma):
        self.eng = eng
        self.fn = fn
        self.deps = []
        self.is_dma = is_dma
        self.signal = False
        self.count = 0
        self.slot = None
        self.use = 0


class _Rec:
    def __init__(self):
        self.call = None

    def __getattr__(self, name):
        def f(*a, **k):
            self.call = (name, a, k)
            return None
        return f


class Prog:
    def __init__(self, nc, n_dma_slots=8):
        self.nc = nc
        self.ops = {e: [] for e in ENGS}
        self.n_dma_slots = n_dma_slots
        self.dma_rr = {e: 0 for e in ENGS}
        self.dma_uses = {}

    def add(self, eng, fn, reads=(), writes=(), dma=False):
        rec = _Rec()
        fn(rec)
        name_, a_, k_ = rec.call
        op = Op(eng, (lambda e: getattr(e, name_)(*a_, **k_)), dma)
        op.dbg = name_
        deps = {}
        wr = []
        for b in writes:
            wr.append(b)
            wr.extend(b.aliases)
        for b in reads:
            if b.last_w is not None:
                deps[id(b.last_w)] = b.last_w
            if b.excl:
                for r in b.readers:
                    if r.eng != eng:
                        deps[id(r)] = r
        for b in wr:
            if b.last_w is not None:
                deps[id(b.last_w)] = b.last_w
            for r in b.readers:
                deps[id(r)] = r
        op.deps = list(deps.values())
        for b in reads:
            b.readers.append(op)
        for b in wr:
            b.last_w = op
            b.readers = []
        if dma:
            s = self.dma_rr[eng]
            self.dma_rr[eng] = (s + 1) % self.n_dma_slots
            key = (eng, s)
            self.dma_uses[key] = self.dma_uses.get(key, 0) + 1
            op.slot = key
            op.use = self.dma_uses[key]
        self.ops[eng].append(op)
        return op

    def pe(self, fn, reads=(), writes=()):
        return self.add("tensor", fn, reads, writes)

    def dve(self, fn, reads=(), writes=()):
        return self.add("vector", fn, reads, writes)

    def act(self, fn, reads=(), writes=()):
        return self.add("scalar", fn, reads, writes)

    def pool(self, fn, reads=(), writes=()):
        return self.add("gpsimd", fn, reads, writes)

    def dma(self, out, in_, reads=(), writes=(), eng="gpsimd", **kw):
        return self.add(eng, lambda e: e.dma_start(out=out, in_=in_, **kw), reads, writes, dma=True)

    def emit(self):
        nc = self.nc
        for e in ENGS:
            for op in self.ops[e]:
                for d in op.deps:
                    if d.is_dma:
                        continue
                    if d.eng == op.eng and d.eng not in SAME_ENGINE_SYNC:
                        continue
                    d.signal = True
        for e in ENGS:
            c = 0
            for op in self.ops[e]:
                if op.signal and not op.is_dma:
                    c += 1
                op.count = c
        with contextlib.ExitStack() as st:
            esem = {e: st.enter_context(nc.semaphore("s_" + e)) for e in ENGS}
            dsem = {}
            for key in self.dma_uses:
                dsem[key] = st.enter_context(nc.semaphore("d_%s_%d" % key))
            block = st.enter_context(nc.Block())

            def make(ename):
                ops = self.ops[ename]

                def body(eng):
                    waited = {}

                    def wait(sem, val):
                        k = id(sem)
                        if waited.get(k, 0) >= val:
                            return
                        waited[k] = val
                        eng.wait_ge(sem, val)

                    for op in ops:
                        if DBG.get("trace"):
                            print("OP", ename, op.dbg, "sig" if op.signal else "", op.count, "slot", op.slot, op.use,
                                  "deps", [(d.eng, d.dbg, (d.slot, d.use) if d.is_dma else d.count) for d in op.deps])
                        for d in op.deps:
                            if d.is_dma:
                                wait(dsem[d.slot], 16 * d.use)
                            else:
                                if d.eng == ename and ename not in SAME_ENGINE_SYNC:
                                    continue
                                wait(esem[d.eng], d.count)
                        if op.is_dma:
                            if op.use > 1:
                                wait(dsem[op.slot], 16 * (op.use - 1))
                            op.fn(eng).then_inc(dsem[op.slot], 16)
                        else:
                            ins = op.fn(eng)
                            if op.signal:
                                ins.then_inc(esem[ename], 1)
                    for key, uses in self.dma_uses.items():
                        if key[0] == ename:
                            wait(dsem[key], 16 * uses)
                return body

            block.tensor(make("tensor"))
            block.vector(make("vector"))
            block.scalar(make("scalar"))
            block.gpsimd(make("gpsimd"))
            block.sync(make("sync"))


D = 1024
KC = 8
DIN = 6408
DFF = 4096
L = 2
NMETA = 16
PAST = 16384
C_UA, C_QB, C_KB, C_VB, C_Z, C_B, C_SQ, C_SK, C_SV, C_GA, C_GB, C_GC = (
    0, 512, 1024, 1536, 2048, 2560, 2568, 3080, 3208, 3336, 4360, 5384)
NS = 16
TS = 8


DBG = {}


def build(NT):
    TP = NT * 512 + 16
    nc = bass.Bass("TRN2", target_bir_lowering=False)
    P = Prog(nc)

    def din(name, shape, dt=F32):
        return nc.dram_tensor(name, list(shape), dt, kind="ExternalInput").ap()

    def dout(name, shape, dt=F32):
        return nc.dram_tensor(name, list(shape), dt, kind="ExternalOutput").ap()

    def dscr(name, shape, dt=BF16):
        return nc.dram_tensor(name, list(shape), dt, kind="Internal").ap()

    xin = din("xin", [TP, D])
    xs_in = din("xs", [NS * TS, D])
    st_pool = din("st_pool", [L, NS * 15, 512])
    st_conv = din("st_conv", [L, NS * 3, 1536])
    st_delta = din("st_delta", [L, NS, 4, 128, 128])
    st_k = din("st_k", [L, NS, 128, 128])
    st_v = din("st_v", [L, NS, 128, 128])
    w_in = din("w_in", [L, D, DIN])
    pool_w = din("pool_w", [L, 4, 128, 128])
    proj = [din("proj_a", [L, 512, D]), din("proj_b", [L, 512, D]), din("proj_c", [L, 512, D])]
    w_out = din("w_out", [L, D, D])
    w_up = din("w_up", [L, D, DFF])
    w_down = din("w_down", [L, DFF, D])
    pvec = din("pvec", [128, 96])
    convw = din("convw", [128, L * 12 * 4])
    rowc = din("rowc", [1, 64])
    rope_p = din("rope_p", [TP, 64])
    rope_s = din("rope_s", [128, 64])
    cm = din("cm", [128, 6 * 128])
    cmask = din("cmask", [128, 4 * 128])
    band = din("band", [128, 3 * 256])
    smask = din("smask", [NS, 128, 256])
    selp = din("selp", [3, 128, NS * 23])
    selc = din("selc", [2, 128, NS * 11])
    invc = din("invc", [128, 4 * 16])
    dmask = din("dmask", [128, 4 * 128], BF16)

    y_p = dout("y_p", [TP, D])
    y_s = dout("y_s", [NS * TS, D])
    o_pool_p = dout("pool_p", [L, 15, 512])
    o_conv_p = dout("conv_p", [L, 3, 1536])
    o_delta_p = dout("delta_p", [L, 4, 128, 128])
    o_k_p = dout("k_p", [L, 128, 128])
    o_v_p = dout("v_p", [L, 128, 128])
    o_pool_s = dout("pool_s", [L, NS, 15, 512])
    o_conv_s = dout("conv_s", [L, NS, 3, 1536])
    o_delta_s = dout("delta_s", [L, NS, 4, 128, 128])
    o_k_s = dout("k_s", [L, NS, 128, 128])
    o_v_s = dout("v_s", [L, NS, 128, 128])

    win_b = [dscr("win_b%d" % l, [D, DIN]) for l in range(L)]
    poolw_b = [dscr("poolw_b%d" % l, [512, 128]) for l in range(L)]
    proj_b = [[dscr("proj_b%d_%d" % (l, i), [512, D]) for i in range(3)] for l in range(L)]
    wout_b = [dscr("wout_b%d" % l, [D, D]) for l in range(L)]
    wup_b = [dscr("wup_b%d" % l, [D, DFF]) for l in range(L)]
    wdown_b = [dscr("wdown_b%d" % l, [DFF, D]) for l in range(L)]
    castB = []

    with contextlib.ExitStack() as st:
        def sb(name, shape, dt):
            return st.enter_context(nc.sbuf_tensor(name, list(shape), dt))

        banks = [st.enter_context(nc.psum_tensor("pb%d" % i, [128, 512], F32)) for i in range(8)]
        bbufs = [Buf("pb%d" % i, excl=True) for i in range(8)]
        rot = {"all": 0, "d": 0, "s": 0}
        BANKSETS = {"all": [0, 1, 2, 3, 4, 5, 6], "d": [0, 1, 2, 3], "s": [4, 5, 6]}

        def bank(pool="all"):
            rot[pool] = (rot[pool] + 1) % len(BANKSETS[pool])
            i_ = BANKSETS[pool][rot[pool]]
            return banks[i_], bbufs[i_]
        LB, LBb = banks[7], bbufs[7]

        evi = [0]

        def evac(out, in_, reads, writes, func=None, scale=1.0, eng=None):
            if func is not None:
                eng = "act"
            if eng is None:
                evi[0] ^= 1
                eng = "act" if evi[0] else "dve"
                if DBG.get("evdve"):
                    eng = "dve"
                if DBG.get("evact"):
                    eng = "act"
            if eng == "act":
                f = func if func is not None else AF.Copy
                P.act(lambda e: e.activation(out=out, in_=in_, func=f, scale=scale), reads, writes)
            else:
                P.dve(lambda e: e.tensor_copy(out=out, in_=in_), reads, writes)

        pv = sb("pv", [128, 96], F32); pvB = Buf()
        cw = sb("cw", [128, L * 48], F32); cwB = Buf()
        rc = sb("rc", [128, 64], F32); rcB = Buf()
        cm32 = sb("cm32", [128, 6 * 128], F32); cm32B = Buf()
        cmk = sb("cmk", [128, 4 * 128], F32); cmkB = Buf()
        bnd = sb("bnd", [128, 3 * 256], F32); bndB = Buf()
        ivc = sb("ivc", [128, 64], F32); ivcB = Buf()
        idb = sb("idb", [128, 128], BF16); idbB = Buf()
        oneb = sb("oneb", [128, 128], BF16); onebB = Buf()
        P.dma(pv[:], pvec, writes=[pvB])
        P.dma(cw[:], convw, writes=[cwB])
        P.dma(rc[:], rowc.partition_broadcast(128), writes=[rcB])
        P.dma(cm32[:], cm, writes=[cm32B])
        P.dma(cmk[:], cmask, writes=[cmkB])
        P.dma(bnd[:], band, writes=[bndB])
        P.dma(ivc[:], invc, writes=[ivcB])

        def CM(i):
            return cm32[:, i * 128:(i + 1) * 128]
        ID32, ONE32, U_P, U_S, BM_P, BM_S = CM(0), CM(1), CM(2), CM(3), CM(4), CM(5)
        P.dve(lambda e: e.tensor_copy(out=idb[:], in_=ID32), [cm32B], [idbB])
        P.dve(lambda e: e.tensor_copy(out=oneb[:], in_=ONE32), [cm32B], [onebB])
        cmb = sb("cmb", [128, 12 * 128], BF16); cmbB = Buf()
        P.dve(lambda e: e.tensor_copy(out=cmb[:, 0:512], in_=cm32[:, 256:768]), [cm32B], [cmbB])
        P.dve(lambda e: e.tensor_copy(out=cmb[:, 512:1024], in_=cmk[:, :]), [cmkB, cmbB], [cmbB])
        P.dma(cmb[:, 1024:1536], dmask, reads=[cmbB], writes=[cmbB])
        PV_N1, PV_N2, PV_FN, PV_PS, PV_ON, PV_RM = 0, 16, 32, 40, 48, 50
        negA = sb("negA", [128, 8], F32); negAB = Buf()
        P.act(lambda e: e.activation(out=negA[:], in_=rc[:, 0:8], func=AF.Exp), [rcB], [negAB])
        P.dve(lambda e: e.tensor_scalar(out=negA[:], in0=negA[:], scalar1=-1.0, scalar2=None, op0=ALU.mult), [negAB], [negAB])

        def _cb():
            castB.append(Buf())
            return castB[-1]

        for l in range(L):
            for r in range(0, D, 128):
                P.dma(win_b[l][r:r + 128, :], w_in[l, r:r + 128, :], writes=[_cb()])
                P.dma(wout_b[l][r:r + 128, :], w_out[l, r:r + 128, :], writes=[_cb()])
                P.dma(wup_b[l][r:r + 128, :], w_up[l, r:r + 128, :], writes=[_cb()])
            for r in range(0, DFF, 128):
                P.dma(wdown_b[l][r:r + 128, :], w_down[l, r:r + 128, :], writes=[_cb()])
            for i in range(3):
                for r in range(0, 512, 128):
                    P.dma(proj_b[l][i][r:r + 128, :], proj[i][l, r:r + 128, :], writes=[_cb()])
            P.dma(poolw_b[l], pool_w[l].rearrange("g c d -> (g c) d"), writes=[_cb()])

        NWB = 3
        wbufs = [sb("wb%d" % i, [128, 8, 512], BF16) for i in range(NWB)]
        wbB = [Buf("wb%d" % i) for i in range(NWB)]
        wsm = [sb("wsm%d" % i, [128, 8, 8], BF16) for i in range(2)]
        wsmB = [Buf() for i in range(2)]

        def layer_specs(l):
            s = []
            W = win_b[l].rearrange("(kc p) m -> p kc m", p=128)
            for c0 in (C_UA, C_QB, C_KB, C_VB, C_Z):
                s.append(("in%d" % c0, W[:, :, c0:c0 + 512], (8, 512)))
            s.append(("ba", W[:, :, C_B:C_B + 8], (8, 8)))
            s.append(("sq", W[:, :, C_SQ:C_SQ + 512], (8, 512)))
            s.append(("skv", W[:, :, C_SK:C_SK + 256], (8, 256)))
            s.append(("poolw", poolw_b[l].rearrange("(g c) d -> c g d", c=128), (4, 128)))
            for i, c0 in enumerate((C_GA, C_GB, C_GC)):
                s.append(("proj%d" % i, proj_b[l][i].rearrange("(kc p) m -> p kc m", p=128), (4, 1024)))
                s.append(("g%d_0" % i, W[:, :, c0:c0 + 512], (8, 512)))
                s.append(("g%d_1" % i, W[:, :, c0 + 512:c0 + 1024], (8, 512)))
            Wo = wout_b[l].rearrange("(kc p) m -> p kc m", p=128)
            s.append(("out0", Wo[:, :, 0:512], (8, 512)))
            s.append(("out1", Wo[:, :, 512:1024], (8, 512)))
            Wu = wup_b[l].rearrange("(kc p) m -> p kc m", p=128)
            Wd = wdown_b[l].rearrange("(kc p) m -> p kc m", p=128)
            for half in range(2):
                for j in range(4):
                    c0 = half * 2048 + j * 512
                    s.append(("up%d" % (half * 4 + j), Wu[:, :, c0:c0 + 512], (8, 512)))
                for ch in range(2):
                    for kg in range(2):
                        k0 = half * 16 + kg * 8
                        s.append(("dn%d_%d_%d" % (half, kg, ch), Wd[:, k0:k0 + 8, ch * 512:(ch + 1) * 512], (8, 512)))
            return s

        n_tl = (NT + 2)
        specs = []
        for t in range(n_tl):
            for l in range(L):
                specs.extend(layer_specs(l))
        HOLD = {"sq": 1, "proj0": 2, "proj1": 2, "proj2": 2}
        wstate = {"issued": 0, "next": 0, "big": 0, "sm": 0, "loc": {}, "occ": [None] * NWB}

        def w_can_issue(j, i):
            if specs[j][0] == "ba":
                return True
            p = wstate["occ"][wstate["big"] % NWB]
            return p is None or p + HOLD.get(specs[p][0], 0) < i

        def w_issue(i):
            name, src, (a, b) = specs[i]
            if name == "ba":
                k = wstate["sm"] % 2
                wstate["sm"] += 1
                dst, bf = wsm[k][:, 0:a, 0:b], wsmB[k]
                view = wsm[k]
            else:
                k = wstate["big"] % NWB
                wstate["big"] += 1
                wstate["occ"][k] = i
                flat = wbufs[k][:].rearrange("p a b -> p (a b)")
                view = flat[:, 0:a * b].rearrange("p (a b) -> p a b", a=a)
                dst, bf = view, wbB[k]
            P.dma(dst, src, reads=castB[-8:], writes=[bf], eng="sync")
            wstate["loc"][i] = (view, bf)

        def wget(name):
            i = wstate["next"]
            assert specs[i][0] == name, (specs[i][0], name)
            while wstate["issued"] < min(len(specs), i + NWB) and w_can_issue(wstate["issued"], i):
                w_issue(wstate["issued"])
                wstate["issued"] += 1
            assert wstate["issued"] > i, ("weight buffer deadlock", name, i)
            wstate["next"] += 1
            return wstate["loc"].pop(i)

        xT = sb("xT", [128, 8, 512], F32); xTB = [Buf("xT%d" % c) for c in range(8)]
        xn = sb("xn", [128, 8, 512], BF16); xnB = [Buf("xn%d" % c) for c in range(8)]
        xtm0 = sb("xtm0", [128, 1024], F32); xtm = [xtm0, xtm0]; _xb = Buf(); xtmB = [_xb, _xb]
        sq = [sb("sq%d" % i, [128, 512], BF16) for i in range(2)]; sqB = [Buf(), Buf()]
        rstd = sb("rstd", [128, 512], F32); rstdB = Buf()
        R1 = sb("R1", [128, 8320], F32)
        exta = R1[:, 0:4 * 528].rearrange("p (g m) -> p g m", g=4)
        qkvp = R1[:, 2112:2112 + 12 * 516].rearrange("p (g m) -> p g m", g=12)
        macc = R1[:, 0:4096].rearrange("p (g m) -> p g m", g=8)
        upb = R1[:, 4096:8192].bitcast(BF16).rearrange("p (g m) -> p g m", g=16)
        extaB = [Buf("exta%d" % g) for g in range(4)]
        qkvpB = [Buf("qkvp%d" % g) for g in range(12)]
        maccB = [Buf("macc%d" % g) for g in range(8)]
        upB = [Buf("up%d" % g) for g in range(16)]
        for a in extaB + qkvpB:
            for b in maccB + upB:
                a.aliases.append(b)
                b.aliases.append(a)
        oT = R1[:, 0:2048].rearrange("p (g m) -> p g m", g=4); oTB = [Buf() for g in range(4)]
        for a in oTB:
            for b in extaB + maccB[0:4]:
                a.aliases.append(b)
                b.aliases.append(a)
        pscr = [sb("pscr%d" % i, [128, 528], F32) for i in range(2)]; pscrB = [Buf(), Buf()]
        dpool = sb("dpool", [128, 4, 512], BF16); dpoolB = [Buf() for g in range(4)]
        oa = sb("oa", [128, 4, 512], BF16); oaB = [Buf() for g in range(4)]
        ob = sb("ob", [128, 4, 512], BF16); obB = [Buf() for g in range(4)]
        oc = sb("oc", [128, 4, 512], BF16); ocB = [Buf() for g in range(4)]
        zs = sb("zs", [128, 4, 512], BF16); zsB = [Buf() for g in range(4)]
        cvo = sb("cvo", [128, 512], F32); cvoB = Buf()
        R3 = sb("R3", [128, 8, 512], BF16)
        qn = R3[:, 0:4, :]; qnB = [Buf() for g in range(4)]
        kn = R3[:, 4:8, :]; knB = [Buf() for g in range(4)]
        vf = sb("vf", [128, 4, 512], BF16); vfB = [Buf() for g in range(4)]
        sil = sb("sil", [128, 512], F32); silB = Buf()
        mbf = R3; mbfB = [Buf() for g in range(8)]
        for g in range(4):
            for a, b in ((qnB[g], mbfB[g]), (knB[g], mbfB[4 + g])):
                a.aliases.append(b)
                b.aliases.append(a)
        qtm = sb("qtm", [128, 512], F32); qtmB = Buf()
        kvtm = sb("kvtm", [128, 256], F32); kvtmB = Buf()
        rp = sb("rp", [128, 64], F32); rpB = Buf()
        rt = [sb("rt%d" % i, [128, 256], F32) for i in range(2)]; rtB = [Buf(), Buf()]
        qr = sb("qr", [128, 512], BF16); qrB = Buf()
        kr32 = sb("kr32", [128, 128], F32); kr32B = Buf()
        krb = sb("krb", [128, 128], BF16); krbB = Buf()
        qTt = sb("qTt", [64, 8, 128], BF16); qTtB = Buf()
        kTc = sb("kTc", [64, 2, 256], BF16); kTcB = Buf()
        vtm = sb("vtm", [128, 2, 128], BF16); vtmB = [Buf(), Buf()]
        R2 = sb("R2", [128, 2048], F32)
        smx = R2[:, :].rearrange("p (h m) -> p h m", h=8); smxB = Buf()
        sg = [R2[:, 0:512], R2[:, 512:1024]]; sgB = [Buf(), Buf()]
        rl = [R2[:, 1024:1536], R2[:, 1536:2048]]; rlB = [Buf(), Buf()]
        for b in sgB + rlB:
            b.aliases.append(smxB)
            smxB.aliases.append(b)
        R4 = sb("R4", [128, 4096], BF16)
        pexp = R4[:, 0:2048].rearrange("p (h m) -> p h m", h=8); pexpB = Buf()
        pT = R4[:, 2048:4096].rearrange("p (a m) -> p a m", a=16); pTB = Buf()
        T32 = R1[:, 2304:2816].rearrange("p (h m) -> p h m", h=4); T32B = Buf()
        Tm = R1[:, 2816:3072].bitcast(BF16).rearrange("p (h m) -> p h m", h=4); TmB = Buf()
        P1a = R1[:, 3072:3328].bitcast(BF16).rearrange("p (h m) -> p h m", h=4); P1aB = Buf()
        P1b = R1[:, 3328:3584].bitcast(BF16).rearrange("p (h m) -> p h m", h=4); P1bB = Buf()
        Nf = R1[:, 3584:3840].bitcast(BF16).rearrange("p (h m) -> p h m", h=4); NfB = Buf()
        NTf = R1[:, 3840:4096].bitcast(BF16).rearrange("p (h m) -> p h m", h=4); NTfB = Buf()
        for a in (T32B, TmB, P1aB, P1bB, NfB, NTfB):
            for b in qkvpB + maccB[4:8]:
                a.aliases.append(b)
                b.aliases.append(a)
        st8 = sb("st8", [128, 64], F32); st8B = Buf()
        otm = sb("otm", [128, 512], BF16); otmB = Buf()
        poolh = sb("poolh", [128, L, 4, 16], F32); poolhB = [Buf() for l in range(L)]
        convh = sb("convh", [128, L, 12, 4], F32); convhB = [Buf() for l in range(L)]
        S32 = sb("S32", [128, L, 4, 128], F32); S32B = [Buf() for l in range(L)]
        Sbf = sb("Sbf", [128, 4, 128], BF16); SbfB = Buf()
        kTh = sb("kTh", [64, L, 2, 128], BF16); kThB = [Buf() for l in range(L)]
        vh = sb("vh", [128, L, 128], BF16); vhB = [Buf() for l in range(L)]
        bg = sb("bg", [128, 16], F32); bgB = Buf()
        gst = sb("gst", [128, 32], F32); gstB = Buf()
        Ug = sb("Ug", [128, 12, 128], BF16); UgB = Buf()
        g3 = sb("g3", [128, 12], BF16); g3B = Buf()
        dlo = sb("dlo", [128, 4, 128], F32); dloB = Buf()
        dup = sb("dup", [128, 4, 128], F32); dupB = Buf()
        egb = sb("egb", [128, 4, 128], F32); egbB = Buf()
        Qm = [sb("Qm%d" % i, [128, 4, 128], BF16) for i in range(2)]; QmB = [Buf(), Buf()]
        Rm = [sb("Rm%d" % i, [128, 4, 128], BF16) for i in range(2)]; RmB = [Buf(), Buf()]
        Ym = sb("Ym", [128, 4, 128], BF16); YmB = Buf()
        Y32 = sb("Y32", [128, 4, 128], F32); Y32B = Buf()
        qkT = sb("qkT", [128, 4, 128], BF16); qkTB = Buf()
        qgT = sb("qgT", [128, 4, 128], BF16); qgTB = Buf()
        kbg = sb("kbg", [128, 4, 128], BF16); kbgB = Buf()
        kgm = sb("kgm", [128, 4, 128], BF16); kgmB = Buf()
        kgs = sb("kgs", [128, 4, 128], BF16); kgsB = Buf()
        vb = sb("vb", [128, 4, 128], BF16); vbB = Buf()
        u32 = sb("u32", [128, 4, 128], F32); u32B = Buf()
        wT = sb("wT", [128, 4, 128], BF16); wTB = Buf()
        dlt = sb("dlt", [128, 4, 128], BF16); dltB = Buf()
        Sld = [S32[:, 0], S32[:, 1]]; SldB = S32B
        utm = xtm0[:, 0:512]; utmB = _xb
        hb = [sb("hb%d" % i, [128, 512], F32) for i in range(2)]; hbB = [Buf(), Buf()]
        msk = [hb[0][:, 0:256]] * 2; mskB = [hbB[0]] * 2
        ctm = [hb[1][:, 0:256]] * 2; ctmB = [hbB[1]] * 2
        ctb = sb("ctb", [128, 256], BF16); ctbB = Buf()
        ytm = xtm0; ytmB = _xb
        pfix = sb("pfix", [128, 16], F32); pfixB = Buf()
        cvo2 = xtm0[:, 0:512]; cvo2B = Buf()
        sil2 = xtm0[:, 512:1024]; sil2B = Buf()
        rstd2 = hb[0][:, 0:512]; rstd2B = Buf()
        for a, b in ((cvo2B, _xb), (sil2B, _xb), (rstd2B, hbB[0])):
            a.aliases.append(b)
            b.aliases.append(a)
        nul = Buf("outs")
        if DBG.get("mem"):
            print("SBUF remaining", nc.sbuf_bytes_remaining)

        dumps = {}

        def dump(name, ap, bufs, dt):
            if not DBG.get("dump") or name in dumps:
                return
            shp = list(ap.shape)
            d = nc.dram_tensor("dbg_" + name, shp, dt, kind="ExternalOutput").ap()
            dumps[name] = d
            P.dma(d, ap, reads=bufs, writes=[nul])

        def rms_stats(src_fn, src_bufs, nch, n, scale, eps_bias, pool="all", sqk=None, rs=None, rsB=None):
            pb, pbB = bank(pool)
            rs_, rsB_ = (rstd, rstdB) if rs is None else (rs, rsB)
            for c in range(nch):
                k = c % 2 if sqk is None else sqk
                P.act(lambda e, c=c, k=k: e.activation(out=sq[k][:, 0:n], in_=src_fn(c), func=AF.Square),
                      [src_bufs[c]], [sqB[k]])
                P.pe(lambda e, c=c, k=k: e.matmul(pb[:, 0:n], lhsT=oneb[:], rhs=sq[k][:, 0:n],
                                                 start=(c == 0), stop=(c == nch - 1)), [sqB[k], onebB], [pbB])
            P.act(lambda e: e.activation(out=rs_[:, 0:n], in_=pb[:, 0:n], func=AF.Ln, bias=pv[:, 95:96], scale=scale),
                  [pbB, pvB], [rsB_])
            P.act(lambda e: e.activation(out=rs_[:, 0:n], in_=rs_[:, 0:n], func=AF.Exp, scale=-0.5, bias=pv[:, eps_bias:eps_bias + 1]),
                  [rsB_, pvB], [rsB_])

        def rmsnorm_fm(n, wcol0):
            rms_stats(lambda c: xT[:, c, 0:n], xTB, 8, n, 1.0 / D, 92)
            for c in range(8):
                P.dve(lambda e, c=c: e.scalar_tensor_tensor(out=xn[:, c, 0:n], in0=xT[:, c, 0:n],
                                                            scalar=pv[:, wcol0 + c:wcol0 + c + 1], in1=rstd[:, 0:n],
                                                            op0=ALU.mult, op1=ALU.mult),
                      [xTB[c], rstdB, pvB], [xnB[c]])

        def fm_group(wv, wB, mc_list, n, kcs, rhs_fn, rhs_bufs, consume):
            for mi, m0 in enumerate(mc_list):
                pb, pbB = bank()
                for j, kc in enumerate(kcs):
                    P.pe(lambda e, kc=kc, j=j, m0=m0: e.matmul(pb[:, 0:n], lhsT=wv[:, j, m0:m0 + 128], rhs=rhs_fn(kc),
                                                              start=(j == 0), stop=(j == len(kcs) - 1)),
                         [wB, rhs_bufs[kc]], [pbB])
                consume(mi, pb, pbB)

        def tm_group(wv, wB, ncols, r0, nr, consume):
            pb, pbB = bank()
            for kc in range(8):
                P.pe(lambda e, kc=kc: e.matmul(pb[0:nr, 0:ncols], lhsT=xn[:, kc, r0:r0 + nr], rhs=wv[:, kc, 0:ncols],
                                               start=(kc == 0), stop=(kc == 7)), [wB, xnB[kc]], [pbB])
            consume(pb, pbB)

        def tile_layer(l, mode, n, pos0, first, last):
            S, T = (1, n) if mode == "p" else (NS, TS)
            HP, HC = (16, 4) if mode == "p" else (15, 3)
            blocks = [(o, min(128, n - o)) for o in range(0, n, 128)]
            rmsnorm_fm(n, PV_N1 + l * 8)

            def ext_view(base, g, H):
                return base[:, g, 0:S * (H + T)].rearrange("p (s m) -> p s m", s=S)

            need_tm = (mode == "s") or last
            for gi, c0 in enumerate((C_UA, C_QB, C_KB, C_VB)):
                wv, wB = wget("in%d" % c0)
                if mode == "p":
                    def cons(mi, pb, pbB, gi=gi):
                        if gi == 0:
                            evac(exta[:, mi, HP:HP + n], pb[:, 0:n], [pbB], [extaB[mi]])
                        else:
                            g = (gi - 1) * 4 + mi
                            evac(qkvp[:, g, HC:HC + n], pb[:, 0:n], [pbB], [qkvpB[g]])
                    fm_group(wv, wB, [0, 128, 256, 384], n, list(range(8)), lambda kc: xn[:, kc, 0:n], xnB, cons)
                if need_tm:
                    def cons2(pb, pbB, gi=gi):
                        evac(utm[0:n, :], pb[0:n, :], [pbB], [utmB])
                    tm_group(wv, wB, 512, 0, n, cons2)
                    if mode == "s":
                        for s_ in range(NS):
                            if gi == 0:
                                P.dma(o_pool_s[l, s_, 7:15, :], utm[s_ * TS:(s_ + 1) * TS, :], reads=[utmB], writes=[nul])
                            else:
                                P.dma(o_conv_s[l, s_, :, (gi - 1) * 512:gi * 512], utm[s_ * TS + 5:(s_ + 1) * TS, :], reads=[utmB], writes=[nul])
                        if gi == 0:
                            P.dma(o_pool_s[l, :, 0:7, :], st_pool[l].rearrange("(s t) c -> s t c", t=15)[:, 8:15, :], writes=[nul])
                            for i in range(2):
                                P.dma(hb[i][0:120, :], st_pool[l, i * 120:(i + 1) * 120, :], writes=[hbB[i]])
                            for g in range(4):
                                pb, pbB = bank()
                                P.pe(lambda e, g=g, pb=pb: e.transpose(pb[:, 0:128], utm[:, g * 128:(g + 1) * 128], ID32), [utmB, cm32B], [pbB])
                                for i in range(2):
                                    P.pe(lambda e, g=g, pb=pb, i=i: e.transpose(pb[:, 128 + i * 128:256 + i * 128], hb[i][:, g * 128:(g + 1) * 128], ID32), [hbB[i], cm32B], [pbB])
                                ev = exta[:, g, 0:368].rearrange("p (s m) -> p s m", s=NS)
                                evac(ev[:, :, 15:23], pb[:, 0:128].rearrange("p (s t) -> p s t", t=TS), [pbB], [extaB[g]])
                                for i in range(2):
                                    evac(ev[:, i * 8:(i + 1) * 8, 0:15], pb[:, 128 + i * 128:128 + i * 128 + 120].rearrange("p (s t) -> p s t", t=15), [pbB], [extaB[g]])
                        else:
                            hk = gi % 2
                            P.dma(hb[hk][0:48, :], st_conv[l, :, (gi - 1) * 512:gi * 512], writes=[hbB[hk]])
                            for g4 in range(4):
                                g = (gi - 1) * 4 + g4
                                pb, pbB = bank()
                                P.pe(lambda e, g4=g4, pb=pb: e.transpose(pb[:, 0:128], utm[:, g4 * 128:(g4 + 1) * 128], ID32), [utmB, cm32B], [pbB])
                                P.pe(lambda e, g4=g4, pb=pb, hk=hk: e.transpose(pb[:, 128:192], hb[hk][0:64, g4 * 128:(g4 + 1) * 128], ID32[0:64, 0:64]), [hbB[hk], cm32B], [pbB])
                                ev = qkvp[:, g, 0:176].rearrange("p (s m) -> p s m", s=NS)
                                evac(ev[:, :, 3:11], pb[:, 0:128].rearrange("p (s t) -> p s t", t=TS), [pbB], [qkvpB[g]])
                                evac(ev[:, :, 0:3], pb[:, 128:176].rearrange("p (s t) -> p s t", t=3), [pbB], [qkvpB[g]])
                    elif last:
                        if gi == 0:
                            P.dma(o_pool_p[l], utm[1:16, :], reads=[utmB], writes=[nul])
                        else:
                            P.dma(o_conv_p[l, :, (gi - 1) * 512:gi * 512], utm[13:16, :], reads=[utmB], writes=[nul])
            if mode == "p":
                for g in range(4):
                    P.pool(lambda e, g=g: e.tensor_copy(out=exta[:, g, 0:16], in_=poolh[:, l, g, :]), [poolhB[l]], [extaB[g]])
                for g in range(12):
                    P.pool(lambda e, g=g: e.tensor_copy(out=qkvp[:, g, 0:4], in_=convh[:, l, g, :]), [convhB[l]], [qkvpB[g]])

            wv, wB = wget("in%d" % C_Z)

            def consz(mi, pb, pbB):
                evac(zs[:, mi, 0:n], pb[:, 0:n], [pbB], [zsB[mi]], func=AF.Silu)
            fm_group(wv, wB, [0, 128, 256, 384], n, list(range(8)), lambda kc: xn[:, kc, 0:n], xnB, consz)
            wba, wbaB = wget("ba")
            wsq, wsqB = wget("sq")
            wkv, wkvB = wget("skv")


            def pool_chain():
                wpv, wpB = None, None
                for g in range(4):
                    e_ = ext_view(exta, g, HP)
                    W = HP + T
                    cur, curB = e_, extaB[g]
                    for k in range(g + 1):
                        sh = 1 << k
                        o_ = pscr[k % 2][:, 0:S * W].rearrange("p (s m) -> p s m", s=S)
                        P.pool(lambda e, cur=cur, o_=o_, sh=sh, W=W: e.tensor_tensor(out=o_[:, :, sh:W], in0=cur[:, :, sh:W],
                                                                                  in1=cur[:, :, 0:W - sh], op=ALU.add),
                               [curB], [pscrB[k % 2]])
                        yield
                        cur, curB = o_, pscrB[k % 2]
                    w_ = 2 << g
                    dv_ = dpool[:, g, 0:n].rearrange("p (s m) -> p s m", s=S)
                    P.dve(lambda e, cur=cur, e_=e_, dv_=dv_, w_=w_: e.scalar_tensor_tensor(
                        out=dv_, in0=cur[:, :, HP:HP + T], scalar=1.0 / w_, in1=e_[:, :, HP:HP + T],
                        op0=ALU.mult, op1=ALU.subtract), [curB, extaB[g]], [dpoolB[g]])
                    yield
                    if mode == "p" and first:
                        P.dve(lambda e, cur=cur, g=g: e.tensor_tensor(out=pfix[:, 0:16], in0=cur[:, 0, HP:HP + 16],
                                                                      in1=ivc[:, g * 16:(g + 1) * 16], op=ALU.mult),
                              [curB, ivcB], [pfixB])
                        yield
                        P.dve(lambda e, g=g: e.tensor_tensor(out=dpool[:, g, 0:16], in0=pfix[:, 0:16], in1=exta[:, g, HP:HP + 16],
                                                             op=ALU.subtract), [pfixB, extaB[g]], [dpoolB[g]])
                        yield
                    if mode == "p" and not last:
                        P.pool(lambda e, g=g: e.tensor_copy(out=poolh[:, l, g, :], in_=exta[:, g, n:n + 16]), [extaB[g]], [poolhB[l]])
                        yield

            def conv_prologue(gs, cvo, cvoB, sil, silB, rstd, rstdB, sqk):
                for g in gs:
                    e_ = ext_view(qkvp, g, HC)
                    cv = cvo[:, 0:n].rearrange("p (s m) -> p s m", s=S)
                    wc0 = l * 48 + g * 4
                    off = 4 - HC if mode == "p" else 0
                    j0 = 1 if mode == "p" else 0
                    P.act(lambda e, e_=e_, cv=cv, wc0=wc0, j0=j0: e.activation(out=cv, in_=e_[:, :, j0:j0 + T], func=AF.Copy, scale=cw[:, wc0:wc0 + 1]),
                          [qkvpB[g], cwB], [cvoB])
                    yield
                    for j in range(1, 4):
                        P.dve(lambda e, e_=e_, cv=cv, wc0=wc0, j=j, j0=j0: e.scalar_tensor_tensor(
                            out=cv, in0=e_[:, :, j0 + j:j0 + j + T], scalar=cw[:, wc0 + j:wc0 + j + 1], in1=cv,
                            op0=ALU.mult, op1=ALU.add), [qkvpB[g], cwB, cvoB], [cvoB])
                        yield
                    if mode == "p" and not last:
                        P.pool(lambda e, g=g: e.tensor_copy(out=convh[:, l, g, :], in_=qkvp[:, g, n:n + 4]), [qkvpB[g]], [convhB[l]])
                        yield
                    h = g % 4
                    if g < 8:
                        dst, dstB = (qn, qnB) if g < 4 else (kn, knB)
                        P.act(lambda e: e.activation(out=sil[:, 0:n], in_=cvo[:, 0:n], func=AF.Silu), [cvoB], [silB])
                        yield
                        rms_stats(lambda c: sil[:, 0:n], [silB], 1, n, 1.0, 93 if g < 4 else 92, pool="d", sqk=sqk, rs=rstd, rsB=rstdB)
                        yield
                        P.dve(lambda e, dst=dst, h=h: e.tensor_tensor(out=dst[:, h, 0:n], in0=sil[:, 0:n], in1=rstd[:, 0:n], op=ALU.mult),
                              [silB, rstdB], [dstB[h]])
                        yield
                    else:
                        P.act(lambda e, h=h: e.activation(out=vf[:, h, 0:n], in_=cvo[:, 0:n], func=AF.Silu), [cvoB], [vfB[h]])
                        yield

            if mode == "p":
                P.act(lambda e: e.copy(out=Sbf[:], in_=S32[:, l]), [S32B[l]], [SbfB])
                P.pool(lambda e: e.tensor_copy(out=kTc[:, :, 0:128], in_=kTh[:, l]), [kThB[l]], [kTcB])
                P.pool(lambda e: e.tensor_copy(out=vtm[:, 0, :], in_=vh[:, l]), [vhB[l]], [vtmB[0]])
            def delta_chain():
                if mode == "p" and DBG.get("conv2"):
                    subs = [conv_prologue(range(0, 12, 2), cvo, cvoB, sil, silB, rstd, rstdB, 0),
                            conv_prologue(range(1, 12, 2), cvo2, cvo2B, sil2, sil2B, rstd2, rstd2B, 1)]
                else:
                    subs = [conv_prologue(range(12), cvo, cvoB, sil, silB, rstd, rstdB, 0)]
                while subs:
                    for g_ in list(subs):
                        try:
                            next(g_)
                            yield
                        except StopIteration:
                            subs.remove(g_)
                for bi, (b0, bn) in enumerate(blocks):
                    smp = mode == "s"
                    U_, BM_ = (U_S, BM_S) if smp else (U_P, BM_P)
                    A_lo = cmk[:, (2 if smp else 0) * 128:(3 if smp else 1) * 128]
                    A_up = cmk[:, (3 if smp else 1) * 128:(4 if smp else 2) * 128]
                    nlev = 3 if smp else (4 if bn == 16 else 6)
                    def consba(pb, pbB):
                        P.act(lambda e: e.activation(out=bg[0:bn, 0:4], in_=pb[0:bn, 0:4], func=AF.Exp, scale=-1.0), [pbB], [bgB])
                        P.dve(lambda e: e.tensor_scalar(out=bg[0:bn, 0:4], in0=bg[0:bn, 0:4], scalar1=1.0, scalar2=None, op0=ALU.add), [bgB], [bgB])
                        P.dve(lambda e: e.reciprocal(out=bg[0:bn, 0:4], in_=bg[0:bn, 0:4]), [bgB], [bgB])
                        P.dve(lambda e: e.tensor_tensor(out=bg[0:bn, 12:16], in0=pb[0:bn, 4:8], in1=rc[0:bn, 8 + l * 4:12 + l * 4], op=ALU.add),
                              [pbB, rcB], [bgB])
                    pbx, pbxB = bank("d")
                    for kc in range(8):
                        P.pe(lambda e, kc=kc: e.matmul(pbx[0:bn, 0:8], lhsT=xn[:, kc, b0:b0 + bn], rhs=wba[:, kc, 0:8],
                                                       start=(kc == 0), stop=(kc == 7)), [wbaB, xnB[kc]], [pbxB])
                        yield
                    consba(pbx, pbxB)
                    yield
                    P.act(lambda e: e.activation(out=bg[0:bn, 12:16], in_=bg[0:bn, 12:16], func=AF.Exp), [bgB], [bgB])
                    yield
                    P.act(lambda e: e.activation(out=bg[0:bn, 12:16], in_=bg[0:bn, 12:16], func=AF.Ln, bias=pv[0:bn, 94:95]), [bgB, pvB], [bgB])
                    yield
                    P.dve(lambda e: e.tensor_tensor(out=bg[0:bn, 4:8], in0=bg[0:bn, 12:16], in1=negA[0:bn, l * 4:l * 4 + 4], op=ALU.mult),
                          [bgB, negAB], [bgB])
                    yield
                    P.dve(lambda e: e.tensor_scalar(out=bg[0:bn, 8:12], in0=bg[0:bn, 0:4], scalar1=-1.0, scalar2=None, op0=ALU.mult), [bgB], [bgB])
                    yield
                    P.dve(lambda e: e.tensor_copy(out=g3[0:bn, 0:4], in_=bg[0:bn, 4:8]), [bgB], [g3B])
                    yield
                    P.dve(lambda e: e.tensor_tensor(out=bg[0:bn, 12:16], in0=bg[0:bn, 4:8], in1=g3[0:bn, 0:4], op=ALU.subtract), [bgB, g3B], [bgB])
                    yield
                    P.dve(lambda e: e.tensor_copy(out=g3[0:bn, 4:8], in_=bg[0:bn, 12:16]), [bgB], [g3B])
                    yield
                    P.dve(lambda e: e.tensor_tensor(out=bg[0:bn, 12:16], in0=bg[0:bn, 12:16], in1=g3[0:bn, 4:8], op=ALU.subtract), [bgB, g3B], [bgB])
                    yield
                    P.dve(lambda e: e.tensor_copy(out=g3[0:bn, 8:12], in_=bg[0:bn, 12:16]), [bgB], [g3B])
                    yield
                    Ub, BMb = cmb[:, (1 if smp else 0) * 128:(2 if smp else 1) * 128], cmb[:, (3 if smp else 2) * 128:(4 if smp else 3) * 128]
                    Alo_b, Aup_b = cmb[:, (6 if smp else 4) * 128:(7 if smp else 5) * 128], cmb[:, (7 if smp else 5) * 128:(8 if smp else 6) * 128]
                    P.dve(lambda e: e.tensor_tensor(out=Ug[0:bn, :, 0:bn], in0=Ub[0:bn, 0:bn].unsqueeze(1).to_broadcast([bn, 12, bn]),
                                                    in1=g3[0:bn, 0:12].unsqueeze(2).to_broadcast([bn, 12, bn]), op=ALU.mult),
                          [cmbB, g3B], [UgB])
                    yield
                    plo, ploB = bank("d")
                    pup, pupB = bank("d")
                    ppl, pplB = bank("d")
                    pg, pgB = bank("d")
                    for (pb_, pbB_, A_) in ((plo, ploB, Alo_b), (pup, pupB, Aup_b), (ppl, pplB, None)):
                        for h in range(4):
                            mo = 128 if A_ is None else bn
                            for q3 in range(3):
                                P.pe(lambda e, pb_=pb_, h=h, A_=A_, mo=mo, q3=q3: e.matmul(pb_[0:mo, h * 128:h * 128 + bn], lhsT=oneb[0:bn, 0:mo], rhs=Ug[0:bn, q3 * 4 + h, 0:bn],
                                                                                       start=(q3 == 0), stop=(A_ is None and q3 == 2)), [onebB, UgB], [pbB_])
                                yield
                            if A_ is not None:
                                P.pe(lambda e, pb_=pb_, h=h, A_=A_: e.matmul(pb_[0:bn, h * 128:h * 128 + bn], lhsT=idb[0:bn, 0:bn], rhs=A_[0:bn, 0:bn],
                                                                           start=False, stop=True), [idbB, cmbB], [pbB_])
                                yield
                    for q3 in range(3):
                        P.pe(lambda e, q3=q3: e.matmul(pg[0:bn, 0:4], lhsT=Ub[0:bn, 0:bn], rhs=g3[0:bn, q3 * 4:q3 * 4 + 4], start=(q3 == 0), stop=(q3 == 2)), [cmbB, g3B], [pgB])
                        yield
                    for q3 in range(3):
                        P.pe(lambda e, q3=q3: e.matmul(pg[0:bn, 4:8], lhsT=BMb[0:bn, 0:bn], rhs=g3[0:bn, q3 * 4:q3 * 4 + 4], start=(q3 == 0), stop=(q3 == 2)), [cmbB, g3B], [pgB])
                        yield
                    P.dve(lambda e: e.tensor_copy(out=gst[0:bn, 0:8], in_=pg[0:bn, 0:8]), [pgB], [gstB])
                    yield
                    P.dve(lambda e: e.tensor_scalar(out=gst[0:bn, 16:20], in0=gst[0:bn, 0:4], scalar1=-1.0, scalar2=None, op0=ALU.mult), [gstB], [gstB])
                    yield
                    P.dve(lambda e: e.tensor_tensor(out=gst[0:bn, 12:16], in0=gst[0:bn, 4:8], in1=gst[0:bn, 0:4], op=ALU.subtract), [gstB], [gstB])
                    yield
                    P.act(lambda e: e.activation(out=gst[0:bn, 8:12], in_=gst[0:bn, 0:4], func=AF.Exp), [gstB], [gstB])
                    yield
                    P.act(lambda e: e.activation(out=gst[0:bn, 12:16], in_=gst[0:bn, 12:16], func=AF.Exp), [gstB], [gstB])
                    yield
                    P.dve(lambda e: e.tensor_tensor(out=gst[0:bn, 8:12], in0=gst[0:bn, 8:12], in1=bg[0:bn, 0:4], op=ALU.mult), [gstB, bgB], [gstB])
                    yield
                    for h in range(4):
                        P.act(lambda e, h=h: e.activation(out=dlo[0:bn, h, 0:bn], in_=plo[0:bn, h * 128:h * 128 + bn], func=AF.Exp,
                                                          bias=gst[0:bn, h:h + 1], scale=-1.0), [ploB, gstB], [dloB])
                        yield
                        P.act(lambda e, h=h: e.activation(out=dup[0:bn, h, 0:bn], in_=pup[0:bn, h * 128:h * 128 + bn], func=AF.Exp,
                                                          bias=gst[0:bn, 16 + h:17 + h], scale=1.0), [pupB, gstB], [dupB])
                        yield
                        P.act(lambda e, h=h: e.activation(out=egb[:, h, 0:bn], in_=ppl[:, h * 128:h * 128 + bn], func=AF.Exp),
                              [pplB], [egbB])
                        yield
                    ptk, ptkB = bank("d")
                    ptkb = ptk[:].bitcast(BF16)
                    for h in range(4):
                        P.pe(lambda e, h=h: e.transpose(ptkb[0:bn, h * 128:(h + 1) * 128], kn[:, h, b0:b0 + bn], idb[:]), [knB[h], idbB], [ptkB])
                        yield
                        P.pe(lambda e, h=h: e.transpose(ptkb[0:bn, 512 + h * 128:512 + (h + 1) * 128], vf[:, h, b0:b0 + bn], idb[:]), [vfB[h], idbB], [ptkB])
                        yield
                    kview = ptkb[0:bn, 0:512].rearrange("p (h m) -> p h m", h=4)
                    vview = ptkb[0:bn, 512:1024].rearrange("p (h m) -> p h m", h=4)
                    P.dve(lambda e: e.tensor_tensor(out=kbg[0:bn], in0=kview, in1=gst[0:bn, 8:12].unsqueeze(2).to_broadcast([bn, 4, 128]), op=ALU.mult),
                          [ptkB, gstB], [kbgB])
                    yield
                    P.dve(lambda e: e.tensor_tensor(out=kgm[0:bn], in0=kview, in1=gst[0:bn, 12:16].unsqueeze(2).to_broadcast([bn, 4, 128]), op=ALU.mult),
                          [ptkB, gstB], [kgmB])
                    yield
                    P.dve(lambda e: e.tensor_tensor(out=vb[0:bn], in0=vview, in1=bg[0:bn, 0:4].unsqueeze(2).to_broadcast([bn, 4, 128]), op=ALU.mult),
                          [ptkB, bgB], [vbB])
                    yield
                    pkk, pkkB = bank("d")
                    pkq, pkqB = bank("d")
                    for h in range(4):
                        P.pe(lambda e, h=h: e.matmul(pkk[0:bn, h * 128:h * 128 + bn], lhsT=kn[:, h, b0:b0 + bn], rhs=kn[:, h, b0:b0 + bn], start=True, stop=True),
                             [knB[h]], [pkkB])
                        yield
                        P.pe(lambda e, h=h: e.matmul(pkq[0:bn, h * 128:h * 128 + bn], lhsT=kn[:, h, b0:b0 + bn], rhs=qn[:, h, b0:b0 + bn], start=True, stop=True),
                             [knB[h], qnB[h]], [pkqB])
                        yield
                    for h in range(4):
                        P.dve(lambda e, h=h: e.scalar_tensor_tensor(out=Nf[0:bn, h, 0:bn], in0=pkk[0:bn, h * 128:h * 128 + bn], scalar=bg[0:bn, 8 + h:9 + h],
                                                                    in1=dlo[0:bn, h, 0:bn], op0=ALU.mult, op1=ALU.mult), [pkkB, bgB, dloB], [NfB])
                        yield
                        P.dve(lambda e, h=h: e.tensor_tensor(out=qkT[0:bn, h, 0:bn], in0=pkq[0:bn, h * 128:h * 128 + bn], in1=dup[0:bn, h, 0:bn], op=ALU.mult),
                              [pkqB, dupB], [qkTB])
                        yield
                        P.dve(lambda e, h=h: e.tensor_tensor(out=qgT[:, h, 0:bn], in0=qn[:, h, b0:b0 + bn], in1=egb[:, h, 0:bn], op=ALU.mult),
                              [qnB[h], egbB], [qgTB])
                        yield
                    ptr, ptrB = bank("d")
                    ptrb = ptr[:].bitcast(BF16)
                    for h in range(4):
                        P.pe(lambda e, h=h: e.transpose(ptrb[0:bn, h * 128:h * 128 + bn], Nf[0:bn, h, 0:bn], idb[0:bn, 0:bn]), [NfB, idbB], [ptrB])
                        yield
                    P.act(lambda e: e.copy(out=NTf[0:bn, :, 0:bn], in_=ptrb[0:bn, 0:512].rearrange("p (h m) -> p h m", h=4)[:, :, 0:bn]), [ptrB], [NTfB])
                    yield

                    def mk(i_):
                        return cmb[0:bn, (8 + i_) * 128:(8 + i_) * 128 + bn].unsqueeze(1).to_broadcast([bn, 4, bn])
                    P.dve(lambda e: e.tensor_tensor(out=Qm[0][0:bn, :, 0:bn], in0=Nf[0:bn, :, 0:bn], in1=mk(0), op=ALU.mult), [NfB, cmbB], [QmB[0]])
                    yield
                    P.pool(lambda e: e.tensor_tensor(out=Rm[0][0:bn, :, 0:bn], in0=NTf[0:bn, :, 0:bn], in1=mk(0), op=ALU.mult), [NTfB, cmbB], [RmB[0]])
                    yield
                    idbc = ID32[0:bn, 0:bn].unsqueeze(1).to_broadcast([bn, 4, bn])
                    P.dve(lambda e: e.tensor_tensor(out=Ym[0:bn, :, 0:bn], in0=Rm[0][0:bn, :, 0:bn], in1=idbc, op=ALU.add), [RmB[0], cm32B], [YmB])
                    yield
                    P.dve(lambda e: e.tensor_tensor(out=Tm[0:bn, :, 0:bn], in0=Qm[0][0:bn, :, 0:bn], in1=idbc, op=ALU.add), [QmB[0], cm32B], [TmB])
                    yield

                    def v4(pb_):
                        return pb_[0:bn, :].rearrange("p (h m) -> p h m", h=4)[:, :, 0:bn]

                    def mm4(pb_, pbB_, lhs, lhsB, rhs, rhsB):
                        for h in range(4):
                            P.pe(lambda e, h=h: e.matmul(pb_[0:bn, h * 128:h * 128 + bn], lhsT=lhs[0:bn, h, 0:bn], rhs=rhs[0:bn, h, 0:bn], start=True, stop=True),
                                 [lhsB, rhsB], [pbB_])

                    def upd(M32, M32B, Mb, MbB, pb_, pbB_):
                        P.dve(lambda e: e.tensor_tensor(out=Mb[0:bn, :, 0:bn], in0=Mb[0:bn, :, 0:bn], in1=v4(pb_), op=ALU.add), [MbB, pbB_], [MbB])

                    cq = 0
                    for lev in range(3):
                        nq_ = 1 - cq
                        pq, pqB = bank("d")
                        pr, prB = bank("d")
                        mm4(pq, pqB, Rm[cq], RmB[cq], Qm[cq], QmB[cq])
                        yield
                        mm4(pr, prB, Qm[cq], QmB[cq], Rm[cq], RmB[cq])
                        yield
                        P.dve(lambda e, nq_=nq_: e.tensor_copy(out=Qm[nq_][0:bn, :, 0:bn], in_=v4(pq)), [pqB], [QmB[nq_]])
                        yield
                        P.act(lambda e, nq_=nq_: e.copy(out=Rm[nq_][0:bn, :, 0:bn], in_=v4(pr)), [prB], [RmB[nq_]])
                        yield
                        py, pyB = bank("d")
                        pt_, pt_B = bank("d")
                        mm4(py, pyB, Qm[nq_], QmB[nq_], Ym, YmB)
                        yield
                        mm4(pt_, pt_B, Rm[nq_], RmB[nq_], Tm, TmB)
                        yield
                        upd(Y32, Y32B, Ym, YmB, py, pyB)
                        yield
                        upd(T32, T32B, Tm, TmB, pt_, pt_B)
                        yield
                        cq = nq_
                    if bn == 128 and not smp:
                        for mi_ in (1, 2, 3):
                            P.dve(lambda e, mi_=mi_: e.tensor_tensor(out=Qm[0][:, :, :], in0=Nf[:, :, :], in1=mk(mi_), op=ALU.mult), [NfB, cmbB], [QmB[0]])
                            yield
                            P.pool(lambda e, mi_=mi_: e.tensor_tensor(out=Rm[0][:, :, :], in0=NTf[:, :, :], in1=mk(mi_), op=ALU.mult), [NTfB, cmbB], [RmB[0]])
                            yield
                            p1, p1B = bank("d")
                            p1p, p1pB = bank("d")
                            mm4(p1, p1B, Rm[0], RmB[0], Tm, TmB)
                            yield
                            mm4(p1p, p1pB, Qm[0], QmB[0], Ym, YmB)
                            yield
                            P.dve(lambda e: e.tensor_copy(out=P1a[:, :, :], in_=v4(p1)), [p1B], [P1aB])
                            yield
                            P.act(lambda e: e.copy(out=P1b[:, :, :], in_=v4(p1p)), [p1pB], [P1bB])
                            yield
                            p2, p2B = bank("d")
                            p2p, p2pB = bank("d")
                            mm4(p2, p2B, Ym, YmB, P1a, P1aB)
                            yield
                            mm4(p2p, p2pB, Tm, TmB, P1b, P1bB)
                            yield
                            upd(T32, T32B, Tm, TmB, p2, p2B)
                            yield
                            upd(Y32, Y32B, Ym, YmB, p2p, p2pB)
                            yield
                    pu, puB = bank("d")
                    pw, pwB = bank("d")
                    for h in range(4):
                        P.pe(lambda e, h=h: e.matmul(pu[0:bn, h * 128:(h + 1) * 128], lhsT=Ym[0:bn, h, 0:bn], rhs=vb[0:bn, h, :], start=True, stop=True), [YmB, vbB], [puB])
                        yield
                        P.pe(lambda e, h=h: e.matmul(pw[:, h * 128:h * 128 + bn], lhsT=kbg[0:bn, h, :], rhs=Ym[0:bn, h, 0:bn], start=True, stop=True), [YmB, kbgB], [pwB])
                        yield
                    P.dve(lambda e: e.tensor_copy(out=u32[0:bn], in_=pu[0:bn, :].rearrange("p (h m) -> p h m", h=4)), [puB], [u32B])
                    yield
                    P.act(lambda e: e.copy(out=wT[:, :, 0:bn], in_=pw[:, :].rearrange("p (h m) -> p h m", h=4)[:, :, 0:bn]), [pwB], [wTB])
                    yield

                    if l == 0 and mode == "p" and first and bi == 0:
                        dump("bg", bg[:], [bgB], F32)
                        dump("gst", gst[:], [gstB], F32)
                        dump("dlo", dlo[:], [dloB], F32)
                        dump("dup", dup[:], [dupB], F32)
                        dump("egb", egb[:], [egbB], F32)
                        dump("u32", u32[:], [u32B], F32)
                        dump("wT", wT[:], [wTB], BF16)
                        dump("qkT", qkT[:], [qkTB], BF16)
                        dump("kgm", kgm[:], [kgmB], BF16)
                        dump("kbg", kbg[:], [kbgB], BF16)
                        dump("qn", qn[:, :, 0:128], qnB, BF16)
                        dump("kn", kn[:, :, 0:128], knB, BF16)
                    def state_step(Ssrc, SsrcB, Sb, SbB, c0, cn, po, poB, kg_, kg_B, glcol, Sdst, SdstB):
                        pws, pwsB = bank("d")
                        for h in range(4):
                            P.pe(lambda e, h=h: e.matmul(pws[0:bn, h * 128:(h + 1) * 128], lhsT=wT[:, h, 0:bn], rhs=Sb[:, h, :], start=True, stop=True), [wTB, SbB], [pwsB])
                            yield
                        P.dve(lambda e: e.tensor_tensor(out=dlt[0:bn], in0=u32[0:bn], in1=pws[0:bn, :].rearrange("p (h m) -> p h m", h=4), op=ALU.subtract),
                              [u32B, pwsB], [dltB])
                        yield
                        for h in range(4):
                            P.pe(lambda e, h=h: e.matmul(po[:, h * 128 + c0:h * 128 + c0 + cn], lhsT=Sb[:, h, :], rhs=qgT[:, h, c0:c0 + cn], start=True, stop=False), [SbB, qgTB], [poB])
                            yield
                            P.pe(lambda e, h=h: e.matmul(po[:, h * 128 + c0:h * 128 + c0 + cn], lhsT=dlt[0:bn, h, :], rhs=qkT[0:bn, h, c0:c0 + cn], start=False, stop=True), [dltB, qkTB], [poB])
                            yield
                        pS, pSB = bank("d")
                        for h in range(4):
                            P.pe(lambda e, h=h: e.matmul(pS[:, h * 128:(h + 1) * 128], lhsT=kg_[0:bn, h, :], rhs=dlt[0:bn, h, :], start=True, stop=True), [kg_B, dltB], [pSB])
                            yield
                        for h in range(4):
                            P.dve(lambda e, h=h: e.scalar_tensor_tensor(out=Sdst[:, h, :], in0=Ssrc[:, h, :], scalar=egb[:, h, glcol:glcol + 1], in1=pS[:, h * 128:(h + 1) * 128],
                                                                        op0=ALU.mult, op1=ALU.add), [SsrcB, egbB, pSB], [SdstB])
                            yield

                    if not smp:
                        po, poB = bank("d")
                        yield from state_step(S32[:, l], S32B[l], Sbf, SbfB, 0, bn, po, poB, kgm, kgmB, bn - 1, S32[:, l], S32B[l])
                        P.act(lambda e: e.copy(out=Sbf[:], in_=S32[:, l]), [S32B[l]], [SbfB])
                        yield
                        evac(oT[:, :, b0:b0 + bn], po[:, :].rearrange("p (h m) -> p h m", h=4)[:, :, 0:bn], [poB], oTB)
                        yield
                    else:
                        for s in range(NS):
                            k = s % 2
                            P.dma(Sld[k][:], st_delta[l, s].rearrange("h k v -> k h v"), writes=[SldB[k]])
                            yield
                            P.act(lambda e, k=k: e.copy(out=Sbf[:], in_=Sld[k][:]), [SldB[k]], [SbfB])
                            yield
                            P.dve(lambda e, s=s: e.tensor_scalar(out=kgs[:], in0=kgm[:], scalar1=pv[:, PV_RM + s:PV_RM + s + 1], scalar2=None, op0=ALU.mult),
                                  [kgmB, pvB], [kgsB])
                            yield
                            yield from state_step(Sld[k], SldB[k], Sbf, SbfB, s * TS, TS, LB, LBb, kgs, kgsB, s * TS + TS - 1, Sld[k], SldB[k])
                            P.dma(o_delta_s[l, s].rearrange("h k v -> k h v"), Sld[k][:], reads=[SldB[k]], writes=[nul])
                            yield
                        evac(oT[:, :, 0:128], LB[:, :].rearrange("p (h m) -> p h m", h=4), [LBb], oTB)
                        yield


            def swa_chain():
                for bi, (b0, bn) in enumerate(blocks):
                    smp = mode == "s"
                    pq_, pq_B = bank("s")
                    for kc in range(8):
                        P.pe(lambda e, kc=kc: e.matmul(pq_[0:bn, :], lhsT=xn[:, kc, b0:b0 + bn], rhs=wsq[:, kc, :], start=(kc == 0), stop=(kc == 7)), [wsqB, xnB[kc]], [pq_B])
                        yield
                    pk_, pk_B = bank("s")
                    for kc in range(8):
                        P.pe(lambda e, kc=kc: e.matmul(pk_[0:bn, 0:256], lhsT=xn[:, kc, b0:b0 + bn], rhs=wkv[:, kc, :], start=(kc == 0), stop=(kc == 7)), [wkvB, xnB[kc]], [pk_B])
                        yield
                    P.act(lambda e: e.copy(out=qtm[0:bn], in_=pq_[0:bn, :]), [pq_B], [qtmB])
                    yield
                    P.dve(lambda e: e.tensor_copy(out=kvtm[0:bn], in_=pk_[0:bn, 0:256]), [pk_B], [kvtmB])
                    yield
                    if smp:
                        P.dma(rp[:], rope_s, writes=[rpB])
                        yield
                    else:
                        P.dma(rp[0:bn], rope_p[pos0 + b0:pos0 + b0 + bn, :], writes=[rpB])
                        yield

                    def rope(src, srcB, nh, dst32, dst32B, dstb, dstbB):
                        sv = src.rearrange("p (h t d) -> p h t d", h=nh, t=2)
                        x1, x2 = sv[:, :, 0, :], sv[:, :, 1, :]
                        cosb = rp[0:bn, 0:32].unsqueeze(1).to_broadcast([bn, nh, 32])
                        sinb = rp[0:bn, 32:64].unsqueeze(1).to_broadcast([bn, nh, 32])
                        t0 = rt[0][0:bn, 0:nh * 32].rearrange("p (h d) -> p h d", h=nh)
                        t1 = rt[1][0:bn, 0:nh * 32].rearrange("p (h d) -> p h d", h=nh)
                        dv = dst32.rearrange("p (h t d) -> p h t d", h=nh, t=2)
                        P.dve(lambda e: e.tensor_tensor(out=t0, in0=x1, in1=cosb, op=ALU.mult), [srcB, rpB], [rtB[0]])
                        P.dve(lambda e: e.tensor_tensor(out=t1, in0=x2, in1=sinb, op=ALU.mult), [srcB, rpB], [rtB[1]])
                        P.dve(lambda e: e.tensor_tensor(out=dv[:, :, 0, :], in0=t0, in1=t1, op=ALU.subtract), [rtB[0], rtB[1]], [dst32B])
                        P.dve(lambda e: e.tensor_tensor(out=t0, in0=x2, in1=cosb, op=ALU.mult), [srcB, rpB], [rtB[0]])
                        P.dve(lambda e: e.tensor_tensor(out=t1, in0=x1, in1=sinb, op=ALU.mult), [srcB, rpB], [rtB[1]])
                        P.dve(lambda e: e.tensor_tensor(out=dv[:, :, 1, :], in0=t0, in1=t1, op=ALU.add), [rtB[0], rtB[1]], [dst32B])
                        if dstb is not None:
                            P.act(lambda e: e.copy(out=dstb, in_=dst32), [dst32B], [dstbB])

                    rope(qtm[0:bn, :], qtmB, 8, smx[0:bn, 0:2, :].rearrange("p a b -> p (a b)"), smxB, qr[0:bn, :], qrB)
                    yield
                    rope(kvtm[0:bn, 0:128], kvtmB, 2, kr32[0:bn, :], kr32B, krb[0:bn, :], krbB)
                    yield
                    P.act(lambda e: e.copy(out=vtm[0:bn, 1, :], in_=kvtm[0:bn, 128:256]), [kvtmB], [vtmB[1]])
                    yield
                    if mode == "p":
                        apos = pos0 + b0
                        lo = max(apos, TP - 128)
                        hi = apos + bn
                        if hi > lo:
                            P.dma(o_k_p[l, lo - (TP - 128):hi - (TP - 128), :], kr32[lo - apos:hi - apos, :], reads=[kr32B], writes=[nul])
                            yield
                            P.dma(o_v_p[l, lo - (TP - 128):hi - (TP - 128), :], kvtm[lo - apos:hi - apos, 128:256], reads=[kvtmB], writes=[nul])
                            yield
                    else:
                        for s_ in range(NS):
                            P.dma(o_k_s[l, s_, 120:128, :], kr32[s_ * TS:(s_ + 1) * TS, :], reads=[kr32B], writes=[nul])
                            yield
                            P.dma(o_v_s[l, s_, 120:128, :], kvtm[s_ * TS:(s_ + 1) * TS, 128:256], reads=[kvtmB], writes=[nul])
                            yield
                        P.dma(o_k_s[l, :, 0:120, :], st_k[l, :, 8:128, :], writes=[nul])
                        yield
                        P.dma(o_v_s[l, :, 0:120, :], st_v[l, :, 8:128, :], writes=[nul])
                        yield
                    ptq, ptqB = bank("s")
                    ptqb = ptq[:].bitcast(BF16)
                    for h in range(8):
                        P.pe(lambda e, h=h: e.transpose(ptqb[0:64, h * 128:h * 128 + bn], qr[0:bn, h * 64:(h + 1) * 64], idb[0:bn, 0:bn]), [qrB, idbB], [ptqB])
                        yield
                    P.dve(lambda e: e.tensor_copy(out=qTt[:, :, 0:bn], in_=ptqb[0:64, :].rearrange("p (h m) -> p h m", h=8)[:, :, 0:bn]), [ptqB], [qTtB])
                    yield
                    ptk2, ptk2B = bank("s")
                    ptk2b = ptk2[:].bitcast(BF16)
                    for g in range(2):
                        P.pe(lambda e, g=g: e.transpose(ptk2b[0:64, g * 128:g * 128 + bn], krb[0:bn, g * 64:(g + 1) * 64], idb[0:bn, 0:bn]), [krbB, idbB], [ptk2B])
                        yield
                    P.act(lambda e: e.copy(out=kTc[:, :, 128:128 + bn], in_=ptk2b[0:64, 0:256].rearrange("p (g m) -> p g m", g=2)[:, :, 0:bn]), [ptk2B], [kTcB])
                    yield

                    def attend(maskap, maskB, first_acc, last_acc, pov, povB, norm_p=True):
                        nk = 128 + bn
                        for hp_ in range(4):
                            psc, pscB = bank("s")
                            for j in range(2):
                                h = hp_ * 2 + j
                                P.pe(lambda e, h=h, j=j: e.matmul(psc[0:bn, j * 256:j * 256 + nk], lhsT=qTt[:, h, 0:bn], rhs=kTc[:, h // 4, 0:nk], start=True, stop=True),
                                     [qTtB, kTcB], [pscB])
                                yield
                            P.dve(lambda e, hp_=hp_: e.scalar_tensor_tensor(out=smx[0:bn, hp_ * 2:hp_ * 2 + 2, 0:nk], in0=psc[0:bn, :].rearrange("p (j m) -> p j m", j=2)[:, :, 0:nk],
                                                                        scalar=0.125, in1=maskap[0:bn, 0:nk].unsqueeze(1).to_broadcast([bn, 2, nk]),
                                                                        op0=ALU.mult, op1=ALU.add), [pscB, maskB], [smxB])
                            yield
                        P.dve(lambda e: e.tensor_reduce(out=st8[0:bn, 0:8], in_=smx[0:bn, :, 0:nk], axis=AX.X, op=ALU.max), [smxB], [st8B])
                        yield
                        P.dve(lambda e: e.tensor_tensor(out=st8[0:bn, 0:8], in0=st8[0:bn, 0:8], in1=rc[0:bn, 16 + l * 8:24 + l * 8], op=ALU.max), [st8B, rcB], [st8B])
                        yield
                        P.dve(lambda e: e.tensor_scalar(out=st8[0:bn, 8:16], in0=st8[0:bn, 0:8], scalar1=-1.0, scalar2=None, op0=ALU.mult), [st8B], [st8B])
                        yield
                        P.dve(lambda e: e.tensor_tensor(out=st8[0:bn, 16:24], in0=rc[0:bn, 16 + l * 8:24 + l * 8], in1=st8[0:bn, 0:8], op=ALU.subtract), [st8B, rcB], [st8B])
                        yield
                        P.act(lambda e: e.activation(out=st8[0:bn, 16:24], in_=st8[0:bn, 16:24], func=AF.Exp), [st8B], [st8B])
                        yield
                        P.dve(lambda e: e.memset(st8[0:bn, 24:32], 0.0), [st8B], [st8B])
                        yield
                        for h in range(8):
                            if norm_p:
                                P.act(lambda e, h=h: e.activation(out=smx[0:bn, h, 0:nk], in_=smx[0:bn, h, 0:nk], func=AF.Exp, bias=st8[0:bn, 8 + h:9 + h],
                                                                  accum_out=st8[0:bn, 24 + h:25 + h]), [smxB, st8B], [smxB, st8B])
                            else:
                                P.act(lambda e, h=h: e.activation(out=pexp[0:bn, h, 0:nk], in_=smx[0:bn, h, 0:nk], func=AF.Exp, bias=st8[0:bn, 8 + h:9 + h],
                                                                  accum_out=st8[0:bn, 24 + h:25 + h]), [smxB, st8B], [pexpB, st8B])
                            yield
                        P.dve(lambda e: e.tensor_tensor(out=st8[0:bn, 32:40], in0=st8[0:bn, 24:32], in1=st8[0:bn, 16:24], op=ALU.add), [st8B], [st8B])
                        yield
                        P.dve(lambda e: e.reciprocal(out=st8[0:bn, 40:48], in_=st8[0:bn, 32:40]), [st8B], [st8B])
                        yield
                        if norm_p:
                            P.dve(lambda e: e.tensor_tensor(out=pexp[0:bn, :, 0:nk], in0=smx[0:bn, :, 0:nk], in1=st8[0:bn, 40:48].unsqueeze(2).to_broadcast([bn, 8, nk]), op=ALU.mult),
                                  [smxB, st8B], [pexpB])
                        yield
                        for half in range(2):
                            ptp, ptpB = bank("s")
                            ptpb = ptp[:].bitcast(BF16)
                            for hh in range(4):
                                h = half * 4 + hh
                                P.pe(lambda e, h=h, hh=hh: e.transpose(ptpb[:, (hh * 2) * 128:(hh * 2) * 128 + bn], pexp[0:bn, h, 0:128], idb[0:bn, 0:bn]), [pexpB, idbB], [ptpB])
                                yield
                                P.pe(lambda e, h=h, hh=hh: e.transpose(ptpb[0:bn, (hh * 2 + 1) * 128:(hh * 2 + 1) * 128 + bn], pexp[0:bn, h, 128:128 + bn], idb[0:bn, 0:bn]), [pexpB, idbB], [ptpB])
                                yield
                            if bn == 128:
                                evac(pT[:, half * 8:half * 8 + 8, :], ptpb[:, :].rearrange("p (a m) -> p a m", a=8), [ptpB], [pTB])
                                yield
                            else:
                                for hh in range(4):
                                    evac(pT[:, half * 8 + hh * 2, 0:bn], ptpb[:, (hh * 2) * 128:(hh * 2) * 128 + bn], [ptpB], [pTB])
                                    yield
                                    evac(pT[0:bn, half * 8 + hh * 2 + 1, 0:bn], ptpb[0:bn, (hh * 2 + 1) * 128:(hh * 2 + 1) * 128 + bn], [ptpB], [pTB])
                                    yield
                        for h in range(8):
                            g = h // 4
                            P.pe(lambda e, h=h, g=g: e.matmul(pov[0:bn, h * 64:(h + 1) * 64], lhsT=pT[:, h * 2, 0:bn], rhs=vtm[:, 0, g * 64:(g + 1) * 64],
                                                              start=first_acc, stop=False), [pTB, vtmB[0]], [povB])
                            yield
                            P.pe(lambda e, h=h, g=g: e.matmul(pov[0:bn, h * 64:(h + 1) * 64], lhsT=pT[0:bn, h * 2 + 1, 0:bn], rhs=vtm[0:bn, 1, g * 64:(g + 1) * 64],
                                                              start=False, stop=last_acc), [pTB, vtmB[1]], [povB])
                            yield

                    if not smp:
                        mi_ = 1 if (first and bi == 0) else 0
                        pov, povB = bank("s")
                        yield from attend(bnd[:, mi_ * 256:(mi_ + 1) * 256], bndB, True, True, pov, povB, norm_p=False)
                        P.dve(lambda e: e.tensor_tensor(out=otm[0:bn, :].rearrange("p (h d) -> p h d", h=8), in0=pov[0:bn, :].rearrange("p (h d) -> p h d", h=8),
                                                        in1=st8[0:bn, 40:48].unsqueeze(2).to_broadcast([bn, 8, 64]), op=ALU.mult), [povB, st8B], [otmB])
                        yield
                        if l == 0 and first and bi == 0:
                            dump("qr", qr[:], [qrB], BF16)
                            dump("qtm", qtm[:], [qtmB], F32)
                            dump("rp", rp[:], [rpB], F32)
                            dump("kr32", kr32[:], [kr32B], F32)
                            dump("qTt", qTt[:], [qTtB], BF16)
                            dump("kTc", kTc[:], [kTcB], BF16)
                            dump("st8", st8[:], [st8B], F32)
                            dump("pexp", pexp[:], [pexpB], BF16)
                            dump("pT", pT[:], [pTB], BF16)
                            dump("otm", otm[:], [otmB], BF16)
                            dump("vtm", vtm[:], vtmB, BF16)
                        if not last:
                            P.pool(lambda e: e.tensor_copy(out=kTc[:, :, 0:128], in_=kTc[:, :, 128:256]), [kTcB], [kTcB])
                            yield
                            P.pool(lambda e: e.tensor_copy(out=vtm[:, 0, :], in_=vtm[:, 1, :]), [vtmB[1]], [vtmB[0]])
                            yield
                    else:
                        for s in range(NS):
                            k = s % 2
                            P.dma(msk[k], smask[s], writes=[mskB[k]])
                            yield
                            P.dma(ctm[k][:, 0:128], st_k[l, s], writes=[ctmB[k]])
                            yield
                            P.dma(ctm[k][:, 128:256], st_v[l, s], writes=[ctmB[k]])
                            yield
                            P.act(lambda e, k=k: e.copy(out=ctb[:], in_=ctm[k]), [ctmB[k]], [ctbB])
                            yield
                            P.pool(lambda e: e.tensor_copy(out=vtm[:, 0, :], in_=ctb[:, 128:256]), [ctbB], [vtmB[0]])
                            yield
                            pck, pckB = bank("s")
                            pckb = pck[:].bitcast(BF16)
                            for g in range(2):
                                P.pe(lambda e, g=g: e.transpose(pckb[0:64, g * 128:(g + 1) * 128], ctb[:, g * 64:(g + 1) * 64], idb[:]), [ctbB, idbB], [pckB])
                                yield
                            P.dve(lambda e: e.tensor_copy(out=kTc[:, :, 0:128], in_=pckb[0:64, 0:256].rearrange("p (g m) -> p g m", g=2)), [pckB], [kTcB])
                            yield
                            pov, povB = bank("s")
                            yield from attend(msk[k], mskB[k], True, True, pov, povB)
                            if s == 0:
                                P.dve(lambda e: e.tensor_copy(out=qtm[:, :], in_=pov[:, :]), [povB], [qtmB])
                                yield
                            elif s < NS - 1:
                                P.dve(lambda e: e.tensor_tensor(out=qtm[:, :], in0=qtm[:, :], in1=pov[:, :], op=ALU.add), [qtmB, povB], [qtmB])
                                yield
                            else:
                                P.dve(lambda e: e.tensor_tensor(out=otm[:, :], in0=qtm[:, :], in1=pov[:, :], op=ALU.add), [qtmB, povB], [otmB])
                                yield
                    pto, ptoB = bank("s")
                    ptob = pto[:].bitcast(BF16)
                    for c in range(4):
                        P.pe(lambda e, c=c: e.transpose(ptob[:, c * 128:c * 128 + bn], otm[0:bn, c * 128:(c + 1) * 128], idb[0:bn, 0:bn]), [otmB, idbB], [ptoB])
                        yield
                    evac(oc[:, :, b0:b0 + bn], ptob[:, 0:512].rearrange("p (c m) -> p c m", c=4)[:, :, 0:bn], [ptoB], ocB)
                    yield

            chains = [pool_chain(), delta_chain(), swa_chain()]
            while chains:
                for g_ in list(chains):
                    try:
                        next(g_)
                    except StopIteration:
                        chains.remove(g_)
            if mode == "p":
                if not last:
                    P.pool(lambda e: e.tensor_copy(out=kTh[:, l], in_=kTc[:, :, 0:128]), [kTcB], [kThB[l]])
                    P.pool(lambda e: e.tensor_copy(out=vh[:, l], in_=vtm[:, 0, :]), [vtmB[0]], [vhB[l]])
                else:
                    P.dma(o_delta_p[l].rearrange("h k v -> k h v"), S32[:, l], reads=[S32B[l]], writes=[nul])

            for h in range(4):
                rms_stats(lambda c, h=h: oT[:, h, 0:n], [oTB[h]], 1, n, 1.0 / 128.0, 92)
                P.dve(lambda e, h=h: e.scalar_tensor_tensor(out=sil[:, 0:n], in0=oT[:, h, 0:n], scalar=pv[:, PV_ON + l:PV_ON + l + 1], in1=rstd[:, 0:n],
                                                            op0=ALU.mult, op1=ALU.mult), [oTB[h], pvB, rstdB], [silB])
                P.dve(lambda e, h=h: e.tensor_tensor(out=ob[:, h, 0:n], in0=sil[:, 0:n], in1=zs[:, h, 0:n], op=ALU.mult), [silB, zsB[h]], [obB[h]])

            wpv, wpB = wget("poolw")
            for g in range(4):
                pb, pbB = bank()
                P.pe(lambda e, g=g: e.matmul(pb[:, 0:n], lhsT=wpv[:, g, :], rhs=dpool[:, g, 0:n], start=True, stop=True), [wpB, dpoolB[g]], [pbB])
                P.dve(lambda e, g=g: e.tensor_scalar(out=oa[:, g, 0:n], in0=pb[:, 0:n], scalar1=pv[:, PV_PS + l * 4 + g:PV_PS + l * 4 + g + 1], scalar2=None, op0=ALU.mult),
                      [pbB, pvB], [oaB[g]])

            if l == 0 and mode == "p" and first:
                dump("oa", oa[:], oaB, BF16)
                dump("ob", ob[:], obB, BF16)
                dump("oc", oc[:], ocB, BF16)
                dump("oT", oT[:], oTB, F32)
                dump("zs", zs[:], zsB, BF16)
                dump("S", S32[:, 0], [S32B[0]], F32)
            if l == 0 and mode == "s":
                dump("s_oa", oa[:, :, 0:128], oaB, BF16)
                dump("s_ob", ob[:, :, 0:128], obB, BF16)
                dump("s_oc", oc[:, :, 0:128], ocB, BF16)
                dump("s_oT", oT[:, :, 0:128], oTB, F32)
            for i, (osrc, osrcB) in enumerate(((oa, oaB), (ob, obB), (oc, ocB))):
                wpj, wpjB = wget("proj%d" % i)
                for half in range(2):
                    wg, wgB = wget("g%d_%d" % (i, half))
                    for m4 in range(4):
                        mc = half * 4 + m4
                        pg_, pg_B = bank()
                        for kc in range(8):
                            P.pe(lambda e, kc=kc, m4=m4: e.matmul(pg_[:, 0:n], lhsT=wg[:, kc, m4 * 128:(m4 + 1) * 128], rhs=xn[:, kc, 0:n], start=(kc == 0), stop=(kc == 7)),
                                 [wgB, xnB[kc]], [pg_B])
                        k = mc % 2
                        P.act(lambda e, k=k: e.activation(out=sg[k][:, 0:n], in_=pg_[:, 0:n], func=AF.Sigmoid), [pg_B], [sgB[k]])
                        pp_, pp_B = bank()
                        for kc in range(4):
                            P.pe(lambda e, kc=kc, mc=mc: e.matmul(pp_[:, 0:n], lhsT=wpj[:, kc, mc * 128:(mc + 1) * 128], rhs=osrc[:, kc, 0:n], start=(kc == 0), stop=(kc == 3)),
                                 [wpjB, osrcB[kc]], [pp_B])
                        if i == 0:
                            P.dve(lambda e, k=k, mc=mc: e.tensor_tensor(out=macc[:, mc, 0:n], in0=sg[k][:, 0:n], in1=pp_[:, 0:n], op=ALU.mult), [sgB[k], pp_B], [maccB[mc]])
                        else:
                            P.dve(lambda e, k=k: e.tensor_tensor(out=sg[k][:, 0:n], in0=sg[k][:, 0:n], in1=pp_[:, 0:n], op=ALU.mult), [sgB[k], pp_B], [sgB[k]])
                            if i == 1:
                                P.pool(lambda e, k=k, mc=mc: e.tensor_tensor(out=macc[:, mc, 0:n], in0=macc[:, mc, 0:n], in1=sg[k][:, 0:n], op=ALU.add), [sgB[k], maccB[mc]], [maccB[mc]])
                            else:
                                P.pool(lambda e, k=k, mc=mc: e.tensor_tensor(out=mbf[:, mc, 0:n], in0=macc[:, mc, 0:n], in1=sg[k][:, 0:n], op=ALU.add), [sgB[k], maccB[mc]], [mbfB[mc]])

            for half in range(2):
                wo, woB = wget("out%d" % half)

                def conso(mi, pb, pbB, half=half):
                    c = half * 4 + mi
                    P.dve(lambda e: e.tensor_tensor(out=xT[:, c, 0:n], in0=xT[:, c, 0:n], in1=pb[:, 0:n], op=ALU.add), [xTB[c], pbB], [xTB[c]])
                fm_group(wo, woB, [0, 128, 256, 384], n, list(range(8)), lambda kc: mbf[:, kc, 0:n], mbfB, conso)

            if l == 0 and mode == "p" and first:
                dump("mbf", mbf[:], mbfB, BF16)
                dump("h", xT[:], xTB, F32)
            rmsnorm_fm(n, PV_N2 + l * 8)
            for half in range(2):
                for j in range(4):
                    wu, wuB = wget("up%d" % (half * 4 + j))

                    def consu(mi, pb, pbB, j=j):
                        k = mi % 2
                        P.act(lambda e: e.activation(out=rl[k][:, 0:n], in_=pb[:, 0:n], func=AF.Relu), [pbB], [rlB[k]])
                        P.pool(lambda e: e.tensor_tensor(out=upb[:, j * 4 + mi, 0:n], in0=rl[k][:, 0:n], in1=rl[k][:, 0:n], op=ALU.mult), [rlB[k]], [upB[j * 4 + mi]])
                    fm_group(wu, wuB, [0, 128, 256, 384], n, list(range(8)), lambda kc: xn[:, kc, 0:n], xnB, consu)
                acc = {}
                for ch in range(2):
                    for kg in range(2):
                        wd, wdB = wget("dn%d_%d_%d" % (half, kg, ch))
                        for mi in range(4):
                            c = ch * 4 + mi
                            if kg == 0:
                                acc[c] = bank()
                            pb, pbB = acc[c]
                            for kc in range(8):
                                P.pe(lambda e, kc=kc, mi=mi, pb=pb, kg=kg: e.matmul(pb[:, 0:n], lhsT=wd[:, kc, mi * 128:(mi + 1) * 128], rhs=upb[:, kg * 8 + kc, 0:n],
                                                                                start=(kg == 0 and kc == 0), stop=(kg == 1 and kc == 7)), [wdB, upB[kg * 8 + kc]], [pbB])
                            if kg == 1:
                                P.dve(lambda e, c=c, pb=pb: e.tensor_tensor(out=xT[:, c, 0:n], in0=xT[:, c, 0:n], in1=pb[:, 0:n], op=ALU.add), [xTB[c], pbB], [xTB[c]])

        def after_layer(l, mode, first):
            if l == 0 and mode == "p" and first:
                dump("x1", xT[:], xTB, F32)

        def load_x(src, r0, nr, c0):
            k = (r0 // 128) % 2
            P.dma(xtm[k][0:nr, :], src[r0:r0 + nr, :], writes=[xtmB[k]])
            kp = max(32, nr)
            if DBG.get("xdmaonly"):
                return
            for hf in range(2):
                pb, pbB = bank()
                for j in range(4):
                    c = hf * 4 + j
                    P.pe(lambda e, c=c, j=j: e.transpose(pb[:, j * 128:j * 128 + kp], xtm[k][0:kp, c * 128:(c + 1) * 128], ID32[0:kp, 0:kp]), [xtmB[k], cm32B], [pbB])
                if DBG.get("xnoevac"):
                    continue
                for j in range(4):
                    c = hf * 4 + j
                    evac(xT[:, c, c0:c0 + nr], pb[:, j * 128:j * 128 + nr], [pbB], [xTB[c]], eng=None if DBG.get("alt") else ("act" if hf else "dve"))

        def store_y(dst, r0, nr, c0):
            for hf in range(2):
                pb, pbB = bank()
                for j in range(4):
                    c = hf * 4 + j
                    P.pe(lambda e, c=c, j=j: e.transpose(pb[0:nr, j * 128:(j + 1) * 128], macc[:, c, c0:c0 + nr], ID32[:, :]), [maccB[c], cm32B], [pbB])
                evac(ytm[0:nr, hf * 512:(hf + 1) * 512], pb[0:nr, :], [pbB], [ytmB])
            P.dma(dst[r0:r0 + nr, :], ytm[0:nr, :], reads=[ytmB], writes=[nul])

        def final_norm(n):
            rms_stats(lambda c: xT[:, c, 0:n], xTB, 8, n, 1.0 / D, 92)
            for c in range(8):
                P.dve(lambda e, c=c: e.scalar_tensor_tensor(out=macc[:, c, 0:n], in0=xT[:, c, 0:n], scalar=pv[:, PV_FN + c:PV_FN + c + 1], in1=rstd[:, 0:n],
                                                            op0=ALU.mult, op1=ALU.mult), [xTB[c], rstdB, pvB], [maccB[c]])

        P.pool(lambda e: e.memset(xtm0[:], 0.0), [], [_xb])
        for i in range(2):
            P.pool(lambda e, i=i: e.memset(hb[i][:], 0.0), [], [hbB[i]])
        for i in range(2):
            P.pool(lambda e, i=i: e.memset(pscr[i][:], 0.0), [], [pscrB[i]])
        for l in range(L):
            P.pool(lambda e, l=l: e.memset(poolh[:, l], 0.0), [], [poolhB[l]])
            P.pool(lambda e, l=l: e.memset(convh[:, l], 0.0), [], [convhB[l]])
            P.pool(lambda e, l=l: e.memset(S32[:, l], 0.0), [], [S32B[l]])
            P.pool(lambda e, l=l: e.memset(kTh[:, l], 0.0), [], [kThB[l]])
            P.pool(lambda e, l=l: e.memset(vh[:, l], 0.0), [], [vhB[l]])
        P.pool(lambda e: e.memset(vtm[:], 0.0), [], vtmB)
        P.pool(lambda e: e.memset(pT[:], 0.0), [], [pTB])

        tiles = [(t * 512, 512) for t in range(NT)] + [(NT * 512, 16)]
        if DBG.get("notiles"):
            tiles = []
        if DBG.get("notail"):
            tiles = tiles[:-1]
        if DBG.get("tailonly"):
            tiles = tiles[-1:]
        for ti, (p0, n) in enumerate(tiles):
            for r in range(0, n, 128):
                load_x(xin, p0 + r, min(128, n - r), r)
            if not DBG.get("nolayers"):
                for l in range(L):
                    tile_layer(l, "p", n, p0, ti == 0, ti == len(tiles) - 1)
                    after_layer(l, "p", ti == 0)
            if not DBG.get("nofinal"):
                final_norm(n)
            if not DBG.get("nostore"):
                for r in range(0, n, 128):
                    store_y(y_p, p0 + r, min(128, n - r), r)
        if not DBG.get("nosample"):
            load_x(xs_in, 0, 128, 0)
            if not DBG.get("nolayers"):
                for l in range(L):
                    tile_layer(l, "s", 128, PAST, False, False)
            final_norm(128)
            store_y(y_s, 0, 128, 0)
        if not DBG:
            assert wstate["next"] == len(specs), (wstate["next"], len(specs))
        P.emit()
    return nc


def _consts(TP):
    f = np.float32
    cm = np.zeros((6, 128, 128), f)
    cm[0] = np.eye(128)
    cm[1] = 1.0
    i = np.arange(128)
    cm[2] = (i[:, None] <= i[None, :])
    same = (i[:, None] // TS) == (i[None, :] // TS)
    cm[3] = cm[2] * same
    cm[4] = 1.0
    cm[5] = same
    cm = np.ascontiguousarray(cm.transpose(1, 0, 2).reshape(128, 6 * 128))
    a_lo = np.where(i[None, :] < i[:, None], 0.0, BIG).astype(f)
    a_up = np.where(i[None, :] >= i[:, None], 0.0, -BIG).astype(f)
    a_lo_s = np.where((i[None, :] < i[:, None]) & same, 0.0, BIG).astype(f)
    a_up_s = np.where((i[None, :] >= i[:, None]) & same, 0.0, -BIG).astype(f)
    cmask = np.concatenate([a_lo, a_up, a_lo_s, a_up_s], axis=1)
    r = np.arange(128)[:, None]
    j = np.arange(256)[None, :]
    ok = (j > r) & (j <= r + 128)
    band0 = np.where(ok, 0.0, -BIG).astype(f)
    band1 = np.where(ok & (j >= 128), 0.0, -BIG).astype(f)
    band = np.concatenate([band0, band1, band0], axis=1)
    sm = np.full((NS, 128, 256), -BIG, f)
    rs, rt_ = np.arange(128) // TS, np.arange(128) % TS
    for s in range(NS):
        rows = rs == s
        cache_ok = (np.arange(128)[None, :] > rt_[:, None]) & rows[:, None]
        new_ok = (rs[None, :] == s) & (rt_[None, :] <= rt_[:, None]) & rows[:, None]
        sm[s][:, 0:128][cache_ok] = 0.0
        sm[s][:, 128:256][new_ok] = 0.0
    selp = np.zeros((3, 128, NS * 23), f)
    for tok in range(128):
        selp[0, tok, (tok // TS) * 23 + 15 + tok % TS] = 1.0
    for rr in range(240):
        selp[1 + rr // 120, rr % 120, (rr // 15) * 23 + rr % 15] = 1.0
    selc = np.zeros((2, 128, NS * 11), f)
    for tok in range(128):
        selc[0, tok, (tok // TS) * 11 + 3 + tok % TS] = 1.0
    for rr in range(48):
        selc[1, rr, (rr // 3) * 11 + rr % 3] = 1.0
    invc = np.zeros((128, 4 * 16), f)
    for g, w in enumerate((2, 4, 8, 16)):
        invc[:, g * 16:(g + 1) * 16] = 1.0 / np.minimum(w, np.arange(16) + 1)
    half = 32
    inv = (10000.0 ** (-np.arange(half, dtype=np.float32) / half)).astype(f)

    def rope_tab(pos):
        ang = pos.astype(f)[:, None] * inv[None, :]
        return np.concatenate([np.cos(ang), np.sin(ang)], axis=1).astype(f)
    rope_p = rope_tab(np.arange(TP))
    rope_s = rope_tab(PAST + (np.arange(128) % TS))
    import ml_dtypes
    Dm = lambda s_: ((i[:, None] // s_) == (i[None, :] // s_)).astype(f)
    dmask = np.concatenate([Dm(16), Dm(32) - Dm(16), Dm(64) - Dm(32), 1.0 - Dm(64)], axis=1).astype(ml_dtypes.bfloat16)
    return dict(dmask=dmask, cm=cm, cmask=cmask, band=band, smask=sm, selp=selp, selc=selc, invc=invc, rope_p=rope_p, rope_s=rope_s)


_CACHE = {}


def kernel(x_prompt, x_sample, state_pool, state_conv, state_delta, cache_swa_k, cache_swa_v,
           meta_tokens, norm1_w, w_in, pool_w, pool_scale, dn_conv_w, dn_a_log, dn_dt_bias,
           dn_onorm_w, swa_sinks, proj_a, proj_b, proj_c, w_out, norm2_w, w_up, w_down, final_norm_w):
    f = np.float32
    A = lambda a: np.ascontiguousarray(np.asarray(a, dtype=f))
    x_prompt, x_sample = A(x_prompt), A(x_sample)
    B, SEQ, _ = x_prompt.shape
    NT = SEQ // 512
    TP = NT * 512 + 16
    assert SEQ == NT * 512
    nb = x_sample.shape[0]
    assert nb == 8 * NS and x_sample.shape[1] == TS
    if NT not in _CACHE:
        _CACHE[NT] = (build(NT), _consts(TP))
    nc, cst = _CACHE[NT]
    pvec = np.zeros((128, 96), f)
    n1, n2, fn = A(norm1_w), A(norm2_w), A(final_norm_w)
    for l in range(L):
        pvec[:, 0 + l * 8:8 + l * 8] = n1[l].reshape(8, 128).T
        pvec[:, 16 + l * 8:24 + l * 8] = n2[l].reshape(8, 128).T
        pvec[:, 40 + l * 4:44 + l * 4] = A(pool_scale)[l].reshape(4, 128).T
        pvec[:, 48 + l] = A(dn_onorm_w)[l]
    pvec[:, 32:40] = fn.reshape(8, 128).T
    pvec[:, 50:66] = (np.arange(128)[:, None] // TS == np.arange(NS)[None, :])
    pvec[:, 94] = 1.0
    pvec[:, 93] = -0.5 * np.log(128.0)
    pvec[:, 92] = 0.0
    pvec[:, 95] = 1e-6
    convw = A(dn_conv_w).reshape(L, 4, 12, 128).transpose(3, 0, 2, 1).reshape(128, L * 48)
    rowc = np.zeros((1, 64), f)
    rowc[0, 0:8] = A(dn_a_log).reshape(-1)
    rowc[0, 8:16] = A(dn_dt_bias).reshape(-1)
    rowc[0, 16:32] = A(swa_sinks).reshape(-1)
    shared = dict(w_in=A(w_in), pool_w=A(pool_w), proj_a=A(proj_a), proj_b=A(proj_b), proj_c=A(proj_c),
                  w_out=A(w_out), w_up=A(w_up), w_down=A(w_down), pvec=pvec, convw=np.ascontiguousarray(convw), rowc=rowc, **cst)
    meta = A(meta_tokens)
    sp, sc, sd = A(state_pool), A(state_conv), A(state_delta)
    ck, cv = A(cache_swa_k), A(cache_swa_v)
    in_maps = []
    for c in range(8):
        b = c % B
        s0 = c * NS
        m = dict(shared)
        m["xin"] = np.ascontiguousarray(np.concatenate([meta, x_prompt[b]], axis=0))
        m["xs"] = np.ascontiguousarray(x_sample[s0:s0 + NS].reshape(NS * TS, D))
        m["st_pool"] = np.ascontiguousarray(sp[:, s0:s0 + NS].reshape(L, NS * 15, 512))
        m["st_conv"] = np.ascontiguousarray(sc[:, s0:s0 + NS].reshape(L, NS * 3, 1536))
        m["st_delta"] = np.ascontiguousarray(sd[:, s0:s0 + NS])
        m["st_k"] = np.ascontiguousarray(ck[:, s0:s0 + NS].reshape(L, NS, 128, 128))
        m["st_v"] = np.ascontiguousarray(cv[:, s0:s0 + NS].reshape(L, NS, 128, 128))
        in_maps.append(m)
    res = run_bass_kernel_spmd(nc, in_maps, core_ids=list(range(8)))
    R = res.results
    y_prompt = np.stack([R[b]["y_p"][NMETA:] for b in range(B)])
    y_sample = np.concatenate([R[c]["y_s"].reshape(NS, TS, D) for c in range(8)])
    st = lambda k, shp: np.stack([R[b][k].reshape(shp) for b in range(B)], axis=1)
    pool_p = st("pool_p", (L, 15, 512))
    conv_p = st("conv_p", (L, 3, 1536))
    delta_p = st("delta_p", (L, 4, 128, 128))
    k_p = st("k_p", (L, 128, 2, 64))
    v_p = st("v_p", (L, 128, 2, 64))
    cat = lambda k, shp: np.concatenate([R[c][k].reshape(shp) for c in range(8)], axis=1)
    pool_s = cat("pool_s", (L, NS, 15, 512))
    conv_s = cat("conv_s", (L, NS, 3, 1536))
    delta_s = cat("delta_s", (L, NS, 4, 128, 128))
    k_s = cat("k_s", (L, NS, 128, 2, 64))
    v_s = cat("v_s", (L, NS, 128, 2, 64))
    return (y_prompt.astype(f), y_sample.astype(f), pool_p, conv_p, delta_p, k_p, v_p, pool_s, conv_s, delta_s, k_s, v_s)
```
